# Optimizing a Trainium2 kernel written in Bass

```python
import math
import jax, jax.numpy as jnp
from jax import lax
import numpy as np

D_MODEL = 1024
BATCH = 32
SEQ = 2048
DEPTH = 1
DEC_BATCH = 128
DEC_SEQ = 4
PAST_LEN = 8192
PAGE_SIZE = 128

POOL_WINDOWS = (2, 4, 8, 16)
POOL_GROUPS = 4
POOL_GROUP_WIDTH = 128
POOL_WIDTH = POOL_GROUPS * POOL_GROUP_WIDTH
POOL_STATE = max(POOL_WINDOWS) - 1
ATT_CONFIGS = ((128, 1), (512, 4), (2048, 16))
N_ATT_GROUPS = len(ATT_CONFIGS)
HEADS_PER_GROUP = 8
HEAD_DIM = 64
N_ATT_HEADS = N_ATT_GROUPS * HEADS_PER_GROUP
ATT_WIDTH = N_ATT_HEADS * HEAD_DIM
ATT_OUT_WIDTH = HEADS_PER_GROUP * HEAD_DIM
D_FF = 2816
EPS = 1e-6
IN_WIDTH = POOL_WIDTH + 3 * ATT_WIDTH + 2 * D_MODEL

kernel_name = 'hybrid_pool_dilated_attn_macaron_step'


def rmsnorm(x, g):
    xf = x.astype(jnp.float32)
    y = xf * lax.rsqrt(jnp.mean(xf * xf, axis=-1, keepdims=True) + EPS)
    return (y * g.astype(jnp.float32)).astype(x.dtype)


def swiglu(x, w_gu, w_down):
    a, b = jnp.split(x @ w_gu, 2, axis=-1)
    return (jax.nn.silu(a) * b) @ w_down


def pool_branch(u_ctx, n_ctx, pool_w, pool_scale):
    B, L, C = u_ctx.shape
    rows = np.arange(n_ctx, L)
    uf = u_ctx.astype(jnp.float32)
    cs = jnp.concatenate([jnp.zeros((B, 1, C), jnp.float32), jnp.cumsum(uf, axis=1)], axis=1)
    hi = cs[:, n_ctx + 1:]
    u_new = uf[:, n_ctx:]
    parts = []
    for gi, w in enumerate(POOL_WINDOWS):
        c0, c1 = gi * POOL_GROUP_WIDTH, (gi + 1) * POOL_GROUP_WIDTH
        lo = cs[:, np.maximum(rows + 1 - w, 0), c0:c1]
        cnt = jnp.asarray(np.minimum(w, rows + 1), jnp.float32)[None, :, None]
        parts.append((hi[..., c0:c1] - lo) / cnt - u_new[..., c0:c1])
    pooled = jnp.stack(parts, axis=2).astype(u_ctx.dtype)
    mixed = jnp.einsum('bngc,gcd->bngd', pooled, pool_w)
    return mixed.reshape(B, L - n_ctx, POOL_WIDTH) * pool_scale


def dilated_attn_prompt(q, k, v, window, dil):
    B, S, H, E = q.shape
    blk = window // dil
    L = -(-S // dil)
    Lp = -(-L // blk) * blk
    Sp = Lp * dil
    nb = Lp // blk
    pad = ((0, 0), (0, Sp - S), (0, 0), (0, 0))
    def to_stream(a):
        return jnp.pad(a, pad).reshape(B, Lp, dil, H, E)
    qs, ks, vs = to_stream(q), to_stream(k), to_stream(v)
    qb = qs.reshape(B, nb, blk, dil, H, E)
    def band(a):
        ap = jnp.pad(a, ((0, 0), (blk, 0), (0, 0), (0, 0), (0, 0)))
        prev = ap[:, :Lp].reshape(B, nb, blk, dil, H, E)
        cur = a.reshape(B, nb, blk, dil, H, E)
        return jnp.concatenate([prev, cur], axis=2)
    kb, vb = band(ks), band(vs)
    s = jnp.einsum('bnidhe,bnjdhe->bndhij', qb, kb, preferred_element_type=jnp.float32) * (E ** -0.5)
    i = np.arange(blk)[:, None]
    j = np.arange(2 * blk)[None, :]
    n = np.arange(nb)[:, None, None]
    mask = (j >= i) & (j <= i + blk) & ((n > 0) | (j >= blk))
    s = jnp.where(mask[:, None, None], s, -jnp.inf)
    mx = jnp.max(s, axis=-1, keepdims=True)
    e = jnp.exp(s - mx)
    den = jnp.sum(e, axis=-1, keepdims=True)
    o = jnp.einsum('bndhij,bnjdhe->bnidhe', (e / den).astype(v.dtype), vb, preferred_element_type=jnp.float32)
    lse = (mx + jnp.log(den))[..., 0]
    o = o.reshape(B, Sp, H, E)[:, :S]
    lse = lse.transpose(0, 1, 4, 2, 3).reshape(B, Sp, H)[:, :S]
    return o, lse


def dilated_attn_sample(q, k_all, v_all, window, dil):
    B, T, H, E = q.shape
    n_ctx = k_all.shape[1] - T
    blk = window // dil
    idx = n_ctx + np.arange(T)[:, None] - dil * np.arange(blk + 1)[None, :]
    valid = idx >= 0
    idx_c = np.maximum(idx, 0)
    kg = k_all[:, idx_c]
    vg = v_all[:, idx_c]
    s = jnp.einsum('bthe,btkhe->bthk', q, kg, preferred_element_type=jnp.float32) * (E ** -0.5)
    s = jnp.where(valid[None, :, None, :], s, -jnp.inf)
    mx = jnp.max(s, axis=-1, keepdims=True)
    e = jnp.exp(s - mx)
    den = jnp.sum(e, axis=-1, keepdims=True)
    o = jnp.einsum('bthk,btkhe->bthe', (e / den).astype(v_all.dtype), vg, preferred_element_type=jnp.float32)
    lse = (mx + jnp.log(den))[..., 0]
    return o, lse


def decoder_layer(x, pool_ctx, kv_ctx, ffn1_norm, ffn1_w_gu, ffn1_w_down, mix_norm, w_in,
                  q_norm, k_norm, pool_w, pool_scale, w_branch_pool, w_branch_att, w_out,
                  ffn2_norm, ffn2_w_gu, ffn2_w_down):
    x = x + 0.5 * swiglu(rmsnorm(x, ffn1_norm), ffn1_w_gu, ffn1_w_down)
    h = rmsnorm(x, mix_norm)
    B, S, _ = h.shape
    splits = [POOL_WIDTH, POOL_WIDTH + ATT_WIDTH, POOL_WIDTH + 2 * ATT_WIDTH, POOL_WIDTH + 3 * ATT_WIDTH]
    u, q, k, v, gate = jnp.split(h @ w_in, splits, axis=-1)
    q = rmsnorm(q.reshape(B, S, N_ATT_HEADS, HEAD_DIM), q_norm)
    k = rmsnorm(k.reshape(B, S, N_ATT_HEADS, HEAD_DIM), k_norm)
    v = v.reshape(B, S, N_ATT_HEADS, HEAD_DIM)
    u_ctx = u if pool_ctx is None else jnp.concatenate([pool_ctx.astype(u.dtype), u], axis=1)
    pool_y = pool_branch(u_ctx, u_ctx.shape[1] - S, pool_w, pool_scale)
    new_pool = u_ctx[:, -POOL_STATE:]
    outs, lses, new_kv = [], [], []
    for gi, (window, dil) in enumerate(ATT_CONFIGS):
        hs = slice(gi * HEADS_PER_GROUP, (gi + 1) * HEADS_PER_GROUP)
        qg, kg, vg = q[:, :, hs], k[:, :, hs], v[:, :, hs]
        if kv_ctx is None:
            k_all, v_all = kg, vg
            o, lse = dilated_attn_prompt(qg, kg, vg, window, dil)
        else:
            cache = kv_ctx[gi].astype(kg.dtype)
            k_all = jnp.concatenate([cache[:, :, 0], kg], axis=1)
            v_all = jnp.concatenate([cache[:, :, 1], vg], axis=1)
            o, lse = dilated_attn_sample(qg, k_all, v_all, window, dil)
        keep = min(window, k_all.shape[1])
        new_kv.append(jnp.stack([k_all[:, -keep:], v_all[:, -keep:]], axis=2))
        outs.append(o)
        lses.append(lse)
    wts = jax.nn.softmax(jnp.stack(lses, axis=0), axis=0)
    att = jnp.sum(wts[..., None] * jnp.stack(outs, axis=0), axis=0)
    att = att.reshape(B, S, ATT_OUT_WIDTH).astype(x.dtype)
    g_pool, g_att = jnp.split(jax.nn.sigmoid(gate.astype(jnp.float32)), 2, axis=-1)
    merged = g_pool * (pool_y @ w_branch_pool).astype(jnp.float32) + g_att * (att @ w_branch_att).astype(jnp.float32)
    x = x + merged.astype(x.dtype) @ w_out
    x = x + 0.5 * swiglu(rmsnorm(x, ffn2_norm), ffn2_w_gu, ffn2_w_down)
    return x, new_kv, new_pool


def setup_inputs(seed: int = 0) -> dict:
    key = jax.random.key(seed)
    ks = jax.random.split(key, 24)
    def nrm(k, shape, scale):
        return jax.random.normal(k, shape, jnp.float32) * scale
    def cache_shape(window):
        return (DEPTH, DEC_BATCH, min(window, PAST_LEN), 2, HEADS_PER_GROUP, HEAD_DIM)
    return {
        'x_prompt': nrm(ks[0], (BATCH, SEQ, D_MODEL), 1.0),
        'x_sample': nrm(ks[1], (DEC_BATCH, DEC_SEQ, D_MODEL), 1.0),
        'cache_kv_w128': nrm(ks[2], cache_shape(ATT_CONFIGS[0][0]), 1.0),
        'cache_kv_w512': nrm(ks[3], cache_shape(ATT_CONFIGS[1][0]), 1.0),
        'cache_kv_w2048': nrm(ks[4], cache_shape(ATT_CONFIGS[2][0]), 1.0),
        'state_pool': nrm(ks[5], (DEPTH, DEC_BATCH, POOL_STATE, POOL_WIDTH), 1.0),
        'ffn1_norm': 1.0 + nrm(ks[6], (DEPTH, D_MODEL), 0.02),
        'ffn1_w_gu': nrm(ks[7], (DEPTH, D_MODEL, 2 * D_FF), D_MODEL ** -0.5),
        'ffn1_w_down': nrm(ks[8], (DEPTH, D_FF, D_MODEL), D_FF ** -0.5),
        'mix_norm': 1.0 + nrm(ks[9], (DEPTH, D_MODEL), 0.02),
        'w_in': nrm(ks[10], (DEPTH, D_MODEL, IN_WIDTH), D_MODEL ** -0.5),
        'q_norm': 1.0 + nrm(ks[11], (DEPTH, N_ATT_HEADS, HEAD_DIM), 0.02),
        'k_norm': 1.0 + nrm(ks[12], (DEPTH, N_ATT_HEADS, HEAD_DIM), 0.02),
        'pool_w': nrm(ks[13], (DEPTH, POOL_GROUPS, POOL_GROUP_WIDTH, POOL_GROUP_WIDTH), POOL_GROUP_WIDTH ** -0.5),
        'pool_scale': 1.0 + nrm(ks[14], (DEPTH, POOL_WIDTH), 0.02),
        'w_branch_pool': nrm(ks[15], (DEPTH, POOL_WIDTH, D_MODEL), POOL_WIDTH ** -0.5),
        'w_branch_att': nrm(ks[16], (DEPTH, ATT_OUT_WIDTH, D_MODEL), ATT_OUT_WIDTH ** -0.5),
        'w_out': nrm(ks[17], (DEPTH, D_MODEL, D_MODEL), D_MODEL ** -0.5),
        'ffn2_norm': 1.0 + nrm(ks[18], (DEPTH, D_MODEL), 0.02),
        'ffn2_w_gu': nrm(ks[19], (DEPTH, D_MODEL, 2 * D_FF), D_MODEL ** -0.5),
        'ffn2_w_down': nrm(ks[20], (DEPTH, D_FF, D_MODEL), D_FF ** -0.5),
    }


def reference(x_prompt, x_sample, cache_kv_w128, cache_kv_w512, cache_kv_w2048, state_pool,
              ffn1_norm, ffn1_w_gu, ffn1_w_down, mix_norm, w_in, q_norm, k_norm, pool_w,
              pool_scale, w_branch_pool, w_branch_att, w_out, ffn2_norm, ffn2_w_gu, ffn2_w_down):
    y_prompt, y_sample = x_prompt, x_sample
    kv_p = [[], [], []]
    kv_s = [[], [], []]
    pool_p, pool_s = [], []
    for l in range(DEPTH):
        wl = (ffn1_norm[l], ffn1_w_gu[l], ffn1_w_down[l], mix_norm[l], w_in[l], q_norm[l], k_norm[l],
              pool_w[l], pool_scale[l], w_branch_pool[l], w_branch_att[l], w_out[l],
              ffn2_norm[l], ffn2_w_gu[l], ffn2_w_down[l])
        y_prompt, nkv_p, npool_p = decoder_layer(y_prompt, None, None, *wl)
        y_sample, nkv_s, npool_s = decoder_layer(
            y_sample, state_pool[l], [cache_kv_w128[l], cache_kv_w512[l], cache_kv_w2048[l]], *wl)
        for gi in range(N_ATT_GROUPS):
            kv_p[gi].append(nkv_p[gi])
            kv_s[gi].append(nkv_s[gi])
        pool_p.append(npool_p)
        pool_s.append(npool_s)
    new_kv128_prompt = jnp.stack(kv_p[0])
    new_kv512_prompt = jnp.stack(kv_p[1])
    new_kv2048_prompt = jnp.stack(kv_p[2])
    new_pool_prompt = jnp.stack(pool_p)
    new_kv128_sample = jnp.stack(kv_s[0])
    new_kv512_sample = jnp.stack(kv_s[1])
    new_kv2048_sample = jnp.stack(kv_s[2])
    new_pool_sample = jnp.stack(pool_s)
    return (y_prompt, y_sample, new_kv128_prompt, new_kv512_prompt, new_kv2048_prompt, new_pool_prompt,
            new_kv128_sample, new_kv512_sample, new_kv2048_sample, new_pool_sample)
```

```python
import contextlib
import numpy as np
import concourse.bass as bass
import concourse.mybir as mybir
from concourse.bass_utils import run_bass_kernel_spmd

F32 = mybir.dt.float32
BF = mybir.dt.bfloat16
AF = mybir.ActivationFunctionType
OP = mybir.AluOpType
AX = mybir.AxisListType

NCORES = 8
EPS = 1e-6
SLOT = 5632
NSLOT = 3
POOL_W = (2, 4, 8, 16)


class _Op:
    __slots__ = ("eng", "fn", "deps", "sig", "dma_sem", "val", "idx")

    def __init__(self, eng, fn, dma_sem):
        self.eng, self.fn, self.dma_sem = eng, fn, dma_sem
        self.deps, self.sig, self.val, self.idx = set(), False, None, None


class Sched:
    ENGS = ("pe", "act", "dve", "pool", "sp")

    def __init__(self):
        self.ops, self.last_w, self.readers = [], {}, {}
        self.bar = set()
        self.last_eng = {}
        self.last_dma = {}

    def op(self, eng, fn, reads=(), writes=(), dma_sem=None):
        o = _Op(eng, fn, dma_sem)
        o.idx = len(self.ops)
        deps = set(self.bar)
        for k in reads:
            w = self.last_w.get(k)
            if w is not None:
                deps.add(w)
            if k.startswith("ps"):
                deps.update(r for r in self.readers.get(k, ()) if self.ops[r].eng != eng)
        for k in writes:
            w = self.last_w.get(k)
            if w is not None:
                deps.add(w)
            deps.update(self.readers.get(k, ()))
        for k in reads:
            self.readers.setdefault(k, []).append(o.idx)
        for k in writes:
            self.last_w[k] = o.idx
            self.readers[k] = []
        if eng == "pe" and dma_sem is None:
            deps = {d for d in deps if not (self.ops[d].eng == "pe" and self.ops[d].dma_sem is None)}
        o.deps = deps
        self.ops.append(o)
        if dma_sem is None:
            self.last_eng[eng] = o.idx
        else:
            self.last_dma[dma_sem] = o.idx
        return o

    def barrier(self):
        self.bar = set(self.last_eng.values()) | set(self.last_dma.values())

    def emit(self, nc):
        ops = self.ops
        for o in ops:
            for d in o.deps:
                ops[d].sig = True
        cnt = {e: 0 for e in self.ENGS}
        dma_names, dma_cnt, dma_eng = [], {}, {}
        for o in ops:
            if o.dma_sem is not None:
                if o.dma_sem not in dma_cnt:
                    dma_cnt[o.dma_sem] = 0
                    dma_names.append(o.dma_sem)
                    dma_eng[o.dma_sem] = o.eng
                assert dma_eng[o.dma_sem] == o.eng
                dma_cnt[o.dma_sem] += 16
                o.val = dma_cnt[o.dma_sem]
            elif o.sig:
                cnt[o.eng] += 1
                o.val = cnt[o.eng]
        with contextlib.ExitStack() as st:
            sems = {}
            for e in self.ENGS:
                sems[("eng", e)] = st.enter_context(nc.semaphore("s_" + e))
            for n in dma_names:
                sems[("dma", n)] = st.enter_context(nc.semaphore("d_" + str(n)))
            block = st.enter_context(nc.Block())

            def semkey(o):
                return ("dma", o.dma_sem) if o.dma_sem is not None else ("eng", o.eng)

            def run(engname, engobj):
                known = {}
                for o in ops:
                    if o.eng != engname:
                        continue
                    need = {}
                    for d in o.deps:
                        p = ops[d]
                        k = semkey(p)
                        if p.val > need.get(k, 0):
                            need[k] = p.val
                    for k, v in need.items():
                        if known.get(k, 0) < v:
                            engobj.wait_ge(sems[k], v)
                            known[k] = v
                    ins = o.fn(engobj)
                    if o.dma_sem is not None:
                        ins.then_inc(sems[("dma", o.dma_sem)], 16)
                    elif o.sig:
                        ins.then_inc(sems[("eng", engname)], 1)
                for n in dma_names:
                    if dma_eng[n] == engname and known.get(("dma", n), 0) < dma_cnt[n]:
                        engobj.wait_ge(sems[("dma", n)], dma_cnt[n])

            @block.tensor
            def _(e):
                run("pe", e)

            @block.scalar
            def _(e):
                run("act", e)

            @block.vector
            def _(e):
                run("dve", e)

            @block.gpsimd
            def _(e):
                run("pool", e)

            @block.sync
            def _(e):
                run("sp", e)


def _const_tables():
    k = np.arange(128)[:, None]
    q = np.arange(128)[None, :]
    m = np.zeros((128, 384), np.float32)
    m[:, 0:128] = (k <= q)
    m[:, 128:256] = (k >= q)
    for i in range(4):
        c = np.arange(32)[None, :]
        blk = np.where(k < 32 * i, 1.0, np.where(k < 32 * (i + 1), ((k - 32 * i) <= c) * 1.0, 0.0))
        m[:, 256 + 32 * i:256 + 32 * (i + 1)] = blk
    invc = np.zeros((128, 4, 16), np.float32)
    for g, w in enumerate(POOL_W):
        invc[:, g, :] = 1.0 / np.minimum(w, np.arange(16) + 1)[None, :]
    sh = np.zeros((64, 3, 64), np.float32)
    vd = np.zeros((64, 3), np.float32)
    for d in range(1, 4):
        for dst in range(64):
            if dst % 4 >= d:
                sh[dst - d, d - 1, dst] = 1.0
                vd[dst, d - 1] = 1.0
    m0 = np.zeros((128, 4), np.float32)
    for t in range(4):
        m0[:, t] = (np.arange(128) >= t)
    z = np.zeros((128, 127), np.float32)
    z[:, 63] = 1.0
    return {"c_ident": np.eye(128, dtype=np.float32), "c_mask": m, "c_invc": invc.reshape(128, 64),
            "c_shift": sh.reshape(64, 192), "c_valid": vd, "c_m0": m0, "c_z": z}


def build_program(NSEQ=4, NSB=16, SEQ=2048, do_sample=True, STAGE=99):
    T = 512
    NT = SEQ // T
    NTOKS = NSB * 4
    nc = bass.Bass("TRN2", target_bir_lowering=False)
    S = Sched()

    def din(name, shape):
        return nc.dram_tensor(name, list(shape), F32, kind="ExternalInput").ap()

    def dout(name, shape):
        return nc.dram_tensor(name, list(shape), F32, kind="ExternalOutput").ap()

    xp = din("xp", [NSEQ, SEQ, 1024])
    xs = din("xs", [NTOKS, 1024])
    cache = [din("c128", [NSB, 128, 1024]), din("c512", [NSB, 512, 1024]), din("c2048", [NSB, 2048, 1024])]
    spool = din("spool", [NSB, 15, 512])
    Wd = {}
    for nm, shp in (("ffn1_norm", [1024]), ("ffn1_w_gu", [1024, 5632]), ("ffn1_w_down", [2816, 1024]),
                    ("mix_norm", [1024]), ("w_in", [1024, 7168]), ("q_norm", [1536]), ("k_norm", [1536]),
                    ("pool_w", [4, 128, 128]), ("pool_scale", [512]), ("w_branch_pool", [512, 1024]),
                    ("w_branch_att", [512, 1024]), ("w_out", [1024, 1024]), ("ffn2_norm", [1024]),
                    ("ffn2_w_gu", [1024, 5632]), ("ffn2_w_down", [2816, 1024])):
        Wd[nm] = din(nm, shp)
    c_ident = din("c_ident", [128, 128])
    c_mask = din("c_mask", [128, 384])
    c_invc = din("c_invc", [128, 64])
    c_shift = din("c_shift", [64, 192])
    c_valid = din("c_valid", [64, 3])
    c_m0 = din("c_m0", [128, 4])
    c_z = din("c_z", [128, 127])

    yp = dout("yp", [NSEQ, SEQ, 1024])
    ys = dout("ys", [NTOKS, 1024])
    kvp = [dout("kv128p", [NSEQ, 128, 1024]), dout("kv512p", [NSEQ, 512, 1024]), dout("kv2048p", [NSEQ, SEQ, 1024])]
    poolp = dout("poolp", [NSEQ, 15, 512])
    kvs = [dout("kv128s", [NSB, 128, 1024]), dout("kv512s", [NSB, 512, 1024]), dout("kv2048s", [NSB, 2048, 1024])]
    pools = dout("pools", [NSB, 15, 512])

    slabs = []

    def piece(w, col0, ncols, nk, off):
        return (w[0:nk * 128, col0:col0 + ncols].rearrange("(kc p) c -> p kc c", p=128), off, nk, ncols)

    def add_slab(n, pieces):
        slabs.append((n, pieces))
        return len(slabs) - 1

    def ffn_slabs(wgu, wdn):
        gu = []
        for s in range(11):
            gu.append(add_slab(4096, [("ab", wgu, s)]))
        dn = [add_slab(5632, [piece(wdn, d * 256, 256, 22, 0)]) for d in range(4)]
        return gu, dn

    f1gu, f1dn = ffn_slabs(Wd["ffn1_w_gu"], Wd["ffn1_w_down"])
    sl_u = add_slab(4096, [piece(Wd["w_in"], 0, 512, 8, 0)])
    sl_q = [add_slab(4096, [piece(Wd["w_in"], 512 + g * 512, 512, 8, 0)]) for g in range(3)]
    sl_k = [add_slab(4096, [piece(Wd["w_in"], 2048 + g * 512, 512, 8, 0)]) for g in range(3)]
    sl_v = [add_slab(4096, [piece(Wd["w_in"], 3584 + g * 512, 512, 8, 0)]) for g in range(3)]
    sl_m = [add_slab(3072, [piece(Wd["w_in"], 5120 + m * 128, 128, 8, 0),
                            piece(Wd["w_in"], 6144 + m * 128, 128, 8, 1024),
                            piece(Wd["w_branch_pool"], m * 128, 128, 4, 2048),
                            piece(Wd["w_branch_att"], m * 128, 128, 4, 2560)]) for m in range(8)]
    sl_o = [add_slab(4096, [piece(Wd["w_out"], o * 512, 512, 8, 0)]) for o in range(2)]
    f2gu, f2dn = ffn_slabs(Wd["ffn2_w_gu"], Wd["ffn2_w_down"])
    NSLAB = len(slabs)
    scr = nc.dram_tensor("wscr", [NSLAB, 128, SLOT], BF).ap()

    tile_order = (f1gu + f1dn + [sl_u] + [x for g in range(3) for x in (sl_q[g], sl_k[g], sl_v[g])]
                  + sl_m + sl_o + f2gu + f2dn)
    conv_order = tile_order

    with contextlib.ExitStack() as st:
        E = st.enter_context

        def sb(name, shape, dt=F32):
            return E(nc.sbuf_tensor(name, list(shape), dt))

        x = sb("x", [128, 8, T])
        h = sb("h", [128, 8, T], BF)
        R = sb("R", [128, 22, T], BF)
        wring = [sb("wr%d" % i, [128, SLOT], BF) for i in range(NSLOT)]
        HIST = sb("HIST", [128, 16384], BF)
        VH = sb("VH", [128, 32, 512], BF)
        ub = sb("ub", [128, 4, 528])
        la = sb("la", [128, 528])
        lb = sb("lb", [128, 528])
        pl = sb("pl", [128, 4, T], BF)
        kvst = [sb("kvst%d" % i, [128, 512]) for i in range(4)]
        sq = [sb("sq%d" % i, [128, 512]) for i in range(2)]
        sqb = [sb("sqb%d" % i, [128, 512], BF) for i in range(2)]
        rstd = sb("rstd", [128, 512])
        Pt = [sb("Pt%d" % i, [128, 512], BF) for i in range(3)]
        sa = [sb("sa%d" % i, [128, 512], BF) for i in range(2)]
        gt = [sb("gt%d" % i, [128, 512]) for i in range(3)]
        small = sb("small", [128, 64])
        ident = sb("ident", [128, 128])
        onesb = sb("onesb", [128, 128], BF)
        maskf = sb("maskf", [128, 384])
        maskb = sb("maskb", [128, 384], BF)
        gk = sb("gk", [128, 1536])
        gqT = sb("gqT", [128, 12])
        gn = sb("gn", [128, 3, 8])
        psc = sb("psc", [128, 4])
        pwf = sb("pwf", [128, 4, 128])
        pwb = sb("pwb", [128, 4, 128], BF)
        invc = sb("invc", [128, 64])
        pst = sb("pst", [16, 512])
        ps = [E(nc.psum_tensor("ps%d" % i, [128, 512], F32)) for i in range(8)]

        def Rblk(j0, n):
            return R[:, j0:j0 + n, :]
        QT = lambda c0, n: Rblk(c0, n)
        attT = lambda c: R[:, 12 + c, :]
        py = lambda c: R[:, 16 + c, :]
        mg = lambda c: R[:, c, :]

        Rf = R[:].rearrange("p a b -> p (a b)").bitcast(F32)

        def tokst(i):
            return Rf[:, i * 1024:(i + 1) * 1024]

        def tokst_keys(i):
            return ["R%d" % j for j in range(4 * i, 4 * i + 4)]

        KT = [HIST[:, 0:4096].rearrange("p (c t) -> p c t", c=4),
              HIST[:, 4096:8192].rearrange("p (c t) -> p c t", c=4),
              HIST[:, 8192:16384].rearrange("p (c t) -> p c t", c=4)]

        bank_state = {"i": 0, "set": list(range(8))}

        def nb():
            s_ = bank_state["set"]
            b = s_[bank_state["i"] % len(s_)]
            bank_state["i"] += 1
            return b

        def MM(out, lhsT, rhs, start, stop, reads, writes, **kw):
            S.op("pe", lambda e: e.matmul(out, lhsT=lhsT, rhs=rhs, start=start, stop=stop, **kw), reads, writes)

        def TR(out, in_, idn, reads, writes):
            S.op("pe", lambda e: e.transpose(out, in_, idn), reads, writes)

        def ACT(out, in_, func, reads, writes, **kw):
            S.op("act", lambda e: e.activation(out=out, in_=in_, func=func, **kw), reads, writes)

        def TT(out, in0, in1, op, reads, writes, eng="dve"):
            S.op(eng, lambda e: e.tensor_tensor(out=out, in0=in0, in1=in1, op=op), reads, writes)

        def TS(out, in0, s1, s2, op0, op1, reads, writes, eng="dve"):
            if op1 is None:
                S.op(eng, lambda e: e.tensor_scalar(out=out, in0=in0, scalar1=s1, scalar2=None, op0=op0), reads, writes)
            else:
                S.op(eng, lambda e: e.tensor_scalar(out=out, in0=in0, scalar1=s1, scalar2=s2, op0=op0, op1=op1), reads, writes)

        def STT(out, in0, scalar, in1, op0, op1, reads, writes):
            S.op("dve", lambda e: e.scalar_tensor_tensor(out=out, in0=in0, scalar=scalar, in1=in1, op0=op0, op1=op1),
                 reads, writes)

        def CP(out, in_, reads, writes, eng="dve"):
            if eng == "act":
                S.op("act", lambda e: e.activation(out=out, in_=in_, func=AF.Copy), reads, writes)
            else:
                S.op(eng, lambda e: e.tensor_copy(out=out, in_=in_), reads, writes)

        def RCP(out, in_, reads, writes):
            S.op("dve", lambda e: e.reciprocal(out=out, in_=in_), reads, writes)

        def RED(out, in_, reads, writes):
            S.op("dve", lambda e: e.tensor_reduce(out=out, in_=in_, axis=AX.X, op=OP.add), reads, writes)

        def MSET(ap, v, writes, eng="dve"):
            S.op(eng, lambda e: e.memset(ap, v), (), writes)

        def DMA(out, in_, reads, writes, sem, eng="sp", **kw):
            S.op(eng, lambda e: e.dma_start(out=out, in_=in_, **kw), reads, writes, dma_sem=sem)

        DMA(ident[:], c_ident, [], ["ident"], "c0")
        DMA(maskf[:], c_mask, [], ["maskf"], "c1")
        DMA(invc[:], c_invc, [], ["invc"], "c2")
        DMA(gk[:], bass.AP(Wd["k_norm"].tensor, 0, [[0, 128], [1, 1536]]), [], ["gk"], "c3")
        DMA(pwf[:], Wd["pool_w"].rearrange("g c d -> c g d"), [], ["pwf"], "c4")
        for j, nm in enumerate(("ffn1_norm", "mix_norm", "ffn2_norm")):
            DMA(gn[:, j, :], Wd[nm].rearrange("(c p) -> p c", p=128), [], ["gn"], "c5", allow_slow_non_contiguous=True)
        DMA(gqT[:], Wd["q_norm"].rearrange("(c p) -> p c", p=128), [], ["gqT"], "c6", allow_slow_non_contiguous=True)
        DMA(psc[:], Wd["pool_scale"].rearrange("(g p) -> p g", p=128), [], ["psc"], "c7", allow_slow_non_contiguous=True)
        CP(maskb[:], maskf[:], ["maskf"], ["maskb"])
        CP(pwb[:], pwf[:], ["pwf"], ["pwb"])
        MSET(onesb[:], 1.0, ["onesb"])

        for sl in conv_order:
            n, pieces = slabs[sl]
            for pc in pieces:
                if pc[0] == "ab":
                    _, wgu, s_ = pc
                    dstv = scr[sl, :, 0:4096].rearrange("p (kc ab c) -> p kc ab c", kc=8, ab=2)
                    for ab in range(2):
                        src = wgu[:, ab * 2816 + s_ * 256: ab * 2816 + s_ * 256 + 256].rearrange("(kc p) c -> p kc c", p=128)
                        DMA(dstv[:, :, ab, :], src, [], ["scr%d" % sl], "cv%d" % sl, eng="pool")
                else:
                    src, off, nk, ncols = pc
                    dstv = scr[sl, :, off:off + nk * ncols].rearrange("p (kc c) -> p kc c", c=ncols)
                    DMA(dstv, src, [], ["scr%d" % sl], "cv%d" % sl, eng="pool")

        if do_sample:
            for g, Wn in enumerate((128, 512, 2048)):
                for b in range(NSB):
                    DMA(kvs[g][b, 0:Wn - 4, :], cache[g][b, 4:Wn, :], [], [], "bulk", eng="pool")
            DMA(pools[:, 0:11, :], spool[:, 4:15, :], [], [], "bulk", eng="pool")

        n_tiles_total = NSEQ * NT + (1 if do_sample else 0)
        stream = tile_order * n_tiles_total
        wst = {"issued": 0, "consumed": 0}

        def prefetch(upto):
            while wst["issued"] <= upto and wst["issued"] < len(stream):
                k = wst["issued"]
                sl = stream[k]
                slot = k % NSLOT
                n = slabs[sl][0]
                DMA(wring[slot][:, 0:n], scr[sl, :, 0:n], ["scr%d" % sl], ["w%d" % slot], "wl%d" % slot)
                wst["issued"] += 1

        def take(expect):
            k = wst["consumed"]
            assert stream[k] == expect, (k, stream[k], expect)
            prefetch(k + NSLOT - 1)
            wst["consumed"] += 1
            slot = k % NSLOT
            return wring[slot], "w%d" % slot

        def rmsnorm_to_h(Tn, j):
            bn = nb()
            for c in range(8):
                s_ = sqb[c % 2]
                ACT(s_[:, 0:Tn], x[:, c, 0:Tn], AF.Square, ["x%d" % c], ["sqb%d" % (c % 2)])
                MM(ps[bn][:, 0:Tn], onesb[:], s_[:, 0:Tn], c == 0, c == 7, ["onesb", "sqb%d" % (c % 2)], ["ps%d" % bn])
            ACT(rstd[:, 0:Tn], ps[bn][:, 0:Tn], AF.Sqrt, ["ps%d" % bn], ["rstd"], bias=EPS, scale=1.0 / 1024)
            RCP(rstd[:, 0:Tn], rstd[:, 0:Tn], ["rstd"], ["rstd"])
            for c in range(8):
                STT(h[:, c, 0:Tn], x[:, c, 0:Tn], gn[:, j, c:c + 1], rstd[:, 0:Tn], OP.mult, OP.mult,
                    ["x%d" % c, "gn", "rstd"], ["h%d" % c])

        hkeys = ["h%d" % c for c in range(8)]

        def ffn(Tn, j, gus, dns):
            rmsnorm_to_h(Tn, j)
            for s_, sl in enumerate(gus):
                wt, wk = take(sl)
                Wv = wt[:, 0:4096].rearrange("p (kc ab c) -> p kc ab c", kc=8, ab=2)
                for jj in range(2):
                    hj = 2 * s_ + jj
                    ba, bb = nb(), nb()
                    for kc in range(8):
                        MM(ps[ba][:, 0:Tn], Wv[:, kc, 0, jj * 128:(jj + 1) * 128], h[:, kc, 0:Tn], kc == 0, kc == 7,
                           [wk, "h%d" % kc], ["ps%d" % ba])
                    for kc in range(8):
                        MM(ps[bb][:, 0:Tn], Wv[:, kc, 1, jj * 128:(jj + 1) * 128], h[:, kc, 0:Tn], kc == 0, kc == 7,
                           [wk, "h%d" % kc], ["ps%d" % bb])
                    sa_ = sa[hj % 2]
                    ACT(sa_[:, 0:Tn], ps[ba][:, 0:Tn], AF.Silu, ["ps%d" % ba], ["sa%d" % (hj % 2)])
                    TT(R[:, hj, 0:Tn], sa_[:, 0:Tn], ps[bb][:, 0:Tn], OP.mult, ["sa%d" % (hj % 2), "ps%d" % bb], ["R%d" % hj])
            for d, sl in enumerate(dns):
                wt, wk = take(sl)
                Wv = wt[:, 0:5632].rearrange("p (kc c) -> p kc c", c=256)
                for mm in range(2):
                    m = 2 * d + mm
                    bo = nb()
                    for kc in range(22):
                        MM(ps[bo][:, 0:Tn], Wv[:, kc, mm * 128:(mm + 1) * 128], R[:, kc, 0:Tn], kc == 0, kc == 21,
                           [wk, "R%d" % kc], ["ps%d" % bo])
                    STT(x[:, m, 0:Tn], ps[bo][:, 0:Tn], 0.5, x[:, m, 0:Tn], OP.mult, OP.add,
                        ["ps%d" % bo, "x%d" % m], ["x%d" % m])

        def load_x(src_rows, Tn, nrows):
            ntb = max(1, Tn // 128)
            for tb in range(ntb):
                stg = tokst(tb % 2)
                DMA(stg[0:nrows, :], src_rows(tb), [], tokst_keys(tb % 2), "xin%d" % (tb % 2))
                for half in range(2):
                    b_ = nb()
                    for k in range(4):
                        c = 4 * half + k
                        TR(ps[b_][:, k * 128:k * 128 + nrows], stg[0:nrows, c * 128:(c + 1) * 128], ident[0:nrows, 0:nrows],
                           tokst_keys(tb % 2) + ["ident"], ["ps%d" % b_])
                    CP(x[:, 4 * half:4 * half + 4, tb * 128:tb * 128 + nrows],
                       ps[b_][:].rearrange("p (a b) -> p a b", a=4)[:, :, 0:nrows],
                       ["ps%d" % b_], ["x%d" % c for c in range(4 * half, 4 * half + 4)], eng=("act" if half else "dve"))

        def store_y(dst_rows, Tn, nrows):
            ntb = max(1, Tn // 128)
            for tb in range(ntb):
                stg = tokst(tb % 2)
                for half in range(2):
                    b_ = nb()
                    for k in range(4):
                        c = 4 * half + k
                        TR(ps[b_][0:nrows, k * 128:(k + 1) * 128], x[:, c, tb * 128:tb * 128 + nrows], ident[:],
                           ["x%d" % c, "ident"], ["ps%d" % b_])
                    CP(stg[0:nrows, half * 512:(half + 1) * 512], ps[b_][0:nrows, :], ["ps%d" % b_],
                       tokst_keys(tb % 2), eng=("act" if half else "dve"))
                DMA(dst_rows(tb), stg[0:nrows, :], tokst_keys(tb % 2), [], "yout%d" % (tb % 2))

        kvst_rr = {"i": 0}

        def next_kvst():
            i = kvst_rr["i"] % 4
            kvst_rr["i"] += 1
            return kvst[i], "kvst%d" % i

        def proj_tok(wt, wk, tok_ap_fn, M):
            b_ = nb()
            Wv = wt[:, 0:4096].rearrange("p (kc c) -> p kc c", c=512)
            for kc in range(8):
                MM(ps[b_][0:M, :], tok_ap_fn(kc), Wv[:, kc, :], kc == 0, kc == 7, [wk, "h%d" % kc], ["ps%d" % b_])
            return b_

        def head_norm(b_, M, gain_ap, gain_key):
            i = b_ % 2
            ACT(sq[i][0:M, :], ps[b_][0:M, :], AF.Square, ["ps%d" % b_], ["sq%d" % i])
            col = 8 * i
            RED(small[0:M, col:col + 8], sq[i][0:M, :].rearrange("p (a b) -> p a b", b=64), ["sq%d" % i], ["small%d" % i])
            ACT(small[0:M, col:col + 8], small[0:M, col:col + 8], AF.Sqrt, ["small%d" % i], ["small%d" % i], bias=EPS, scale=1.0 / 64)
            RCP(small[0:M, col:col + 8], small[0:M, col:col + 8], ["small%d" % i], ["small%d" % i])
            stg, sk = next_kvst()
            TT(stg[0:M, :].rearrange("p (a b) -> p a b", b=64), ps[b_][0:M, :].rearrange("p (a b) -> p a b", b=64),
               small[0:M, col:col + 8].unsqueeze(2).to_broadcast([M, 8, 64]), OP.mult, ["ps%d" % b_, "small%d" % i], [sk])
            if gain_ap is not None:
                TT(stg[0:M, :], stg[0:M, :], gain_ap, OP.mult, [sk, gain_key], [sk])
            return stg, sk

        def transpose_to_feat(stg, sk, M, out_ap3, out_keys, gain3=None, eng="act"):
            b_ = nb()
            for cc in range(4):
                TR(ps[b_][:, cc * 128:cc * 128 + M], stg[0:M, cc * 128:(cc + 1) * 128], ident[0:M, 0:M], [sk, "ident"], ["ps%d" % b_])
            src = ps[b_][:].rearrange("p (a b) -> p a b", a=4)[:, :, 0:M]
            if gain3 is not None:
                TT(out_ap3, src, gain3.unsqueeze(2).to_broadcast([128, 4, M]), OP.mult, ["ps%d" % b_, "gqT"], out_keys)
            else:
                CP(out_ap3, src, ["ps%d" % b_], out_keys, eng=eng)

        def prompt_tile(s, i):
            tok0 = i * T
            bank_state["set"] = list(range(8))
            load_x(lambda tb: xp[s, tok0 + tb * 128: tok0 + (tb + 1) * 128, :], T, 128)
            if STAGE == 1:
                return
            ffn(T, 0, f1gu, f1dn)
            if STAGE == 2:
                store_y(lambda tb: yp[s, tok0 + tb * 128: tok0 + (tb + 1) * 128, :], T, 128)
                return
            rmsnorm_to_h(T, 1)
            wt, wk = take(sl_u)
            Wv = wt[:, 0:4096].rearrange("p (kc c) -> p kc c", c=512)
            if i == 0:
                MSET(ub[:, :, 0:16], 0.0, ["ub%d" % g for g in range(4)])
            for g in range(4):
                b_ = nb()
                for kc in range(8):
                    MM(ps[b_][:], Wv[:, kc, g * 128:(g + 1) * 128], h[:, kc, :], kc == 0, kc == 7, [wk, "h%d" % kc], ["ps%d" % b_])
                CP(ub[:, g, 16:528], ps[b_][:], ["ps%d" % b_], ["ub%d" % g], eng="act")
            for g in range(4):
                ug = ub[:, g, :]
                uk = "ub%d" % g
                TT(la[:, 2:528], ug[:, 2:528], ug[:, 1:527], OP.add, [uk], ["la"])
                cur, ck = la, "la"
                if g >= 1:
                    TT(lb[:, 4:528], la[:, 4:528], la[:, 2:526], OP.add, ["la"], ["lb"])
                    cur, ck = lb, "lb"
                if g >= 2:
                    TT(la[:, 8:528], lb[:, 8:528], lb[:, 4:524], OP.add, ["lb"], ["la"])
                    cur, ck = la, "la"
                if g >= 3:
                    TT(lb[:, 16:528], la[:, 16:528], la[:, 8:520], OP.add, ["la"], ["lb"])
                    cur, ck = lb, "lb"
                w = POOL_W[g]
                STT(pl[:, g, :], cur[:, 16:528], 1.0 / w, ug[:, 16:528], OP.mult, OP.subtract, [ck, uk], ["pl%d" % g])
                if i == 0:
                    TT(cur[:, 16:32], cur[:, 16:32], invc[:, g * 16:(g + 1) * 16], OP.mult, [ck, "invc"], [ck])
                    TT(pl[:, g, 0:16], cur[:, 16:32], ug[:, 16:32], OP.subtract, [ck, uk], ["pl%d" % g])
            if i == NT - 1:
                b_ = nb()
                for g in range(4):
                    TR(ps[b_][0:16, g * 128:(g + 1) * 128], ub[:, g, 512:528], ident[:], ["ub%d" % g, "ident"], ["ps%d" % b_])
                CP(pst[:], ps[b_][0:16, :], ["ps%d" % b_], ["pst"], eng="act")
                DMA(poolp[s, :, :], pst[1:16, :], ["pst"], [], "pout")
            for g in range(4):
                CP(ub[:, g, 0:16], ub[:, g, 512:528], ["ub%d" % g], ["ub%d" % g])
            for g in range(4):
                b_ = nb()
                MM(ps[b_][:], pwb[:, g, :], pl[:, g, :], True, True, ["pwb", "pl%d" % g], ["ps%d" % b_])
                TS(py(g), ps[b_][:], psc[:, g:g + 1], None, OP.mult, None, ["ps%d" % b_, "psc"], ["R%d" % (16 + g)])
            if STAGE == 3:
                store_y(lambda tb: yp[s, tok0 + tb * 128: tok0 + (tb + 1) * 128, :], T, 128)
                return
            for g in range(3):
                slot_kt = (i % 2) if g < 2 else i
                ktoff = slot_kt * T
                wt, wk = take(sl_q[g])
                for tb in range(4):
                    b_ = proj_tok(wt, wk, lambda kc, tb=tb: h[:, kc, tb * 128:(tb + 1) * 128], 128)
                    stg, sk = head_norm(b_, 128, None, None)
                    transpose_to_feat(stg, sk, 128, R[:, 4 * g:4 * g + 4, tb * 128:(tb + 1) * 128],
                                      ["R%d" % c for c in range(4 * g, 4 * g + 4)], gain3=gqT[:, 4 * g:4 * g + 4])
                if STAGE == 41:
                    store_y(lambda tb: yp[s, tok0 + tb * 128: tok0 + (tb + 1) * 128, :], T, 128)
                    return
                wt, wk = take(sl_k[g])
                for tb in range(4):
                    b_ = proj_tok(wt, wk, lambda kc, tb=tb: h[:, kc, tb * 128:(tb + 1) * 128], 128)
                    stg, sk = head_norm(b_, 128, gk[:, g * 512:(g + 1) * 512], "gk")
                    Wg = min((128, 512, 2048)[g], SEQ)
                    keep = (tok0 + tb * 128) >= SEQ - Wg
                    if keep:
                        row0 = tok0 + tb * 128 - (SEQ - Wg)
                        DMA(kvp[g][s, row0:row0 + 128, 0:512], stg[:, :], [sk], [], "ko" + sk)
                    transpose_to_feat(stg, sk, 128, KT[g][:, :, ktoff + tb * 128: ktoff + (tb + 1) * 128],
                                      ["KT%d_%d_%d" % (g, slot_kt, tb)], eng=("act" if tb % 2 else "dve"))
                if STAGE == 42:
                    store_y(lambda tb: yp[s, tok0 + tb * 128: tok0 + (tb + 1) * 128, :], T, 128)
                    return
                wt, wk = take(sl_v[g])
                if g == 0:
                    for w in range(4):
                        b_ = proj_tok(wt, wk, lambda kc, w=w: h[:, kc, w * 128:(w + 1) * 128], 128)
                        slotv = (4 * i + w) % 8
                        CP(VH[:, slotv, :], ps[b_][:], ["ps%d" % b_], ["V0_%d" % slotv])
                        if i == NT - 1 and w == 3:
                            stg, sk = next_kvst()
                            CP(stg[:, :], ps[b_][:], ["ps%d" % b_], [sk], eng="act")
                            DMA(kvp[0][s, 0:128, 512:1024], stg[:, :], [sk], [], "ko" + sk)
                elif g == 1:
                    for r4 in range(4):
                        b_ = proj_tok(wt, wk, lambda kc, r4=r4: h[:, kc, r4:T:4], 128)
                        slotv = 8 + 2 * r4 + (i % 2)
                        CP(VH[:, slotv, :], ps[b_][:], ["ps%d" % b_], ["V1_%d" % slotv])
                        if i == NT - 1:
                            stg, sk = next_kvst()
                            CP(stg[:, :], ps[b_][:], ["ps%d" % b_], [sk], eng="act")
                            DMA(kvp[1][s, r4:512:4, 512:1024], stg[:, :], [sk], [], "ko" + sk)
                else:
                    for r16 in range(16):
                        b_ = nb()
                        Wv = wt[:, 0:4096].rearrange("p (kc c) -> p kc c", c=512)
                        po = 32 * i
                        for kc in range(8):
                            MM(ps[b_][po:po + 32, :], h[:, kc, r16:T:16], Wv[:, kc, :], kc == 0, kc == 7,
                               [wk, "h%d" % kc], ["ps%d" % b_], tile_position=(0, po))
                        CP(VH[po:po + 32, 16 + r16, :], ps[b_][po:po + 32, :], ["ps%d" % b_], ["V2_%d" % r16])
                        stg, sk = next_kvst()
                        CP(stg[po:po + 32, :], ps[b_][po:po + 32, :], ["ps%d" % b_], [sk], eng="act")
                        DMA(kvp[2][s, tok0 + r16: tok0 + T: 16, 512:1024], stg[po:po + 32, :], [sk], [], "ko" + sk)
                if STAGE == 43 + g:
                    store_y(lambda tb: yp[s, tok0 + tb * 128: tok0 + (tb + 1) * 128, :], T, 128)
                    return
            if STAGE == 4:
                store_y(lambda tb: yp[s, tok0 + tb * 128: tok0 + (tb + 1) * 128, :], T, 128)
                return
            bank_state["set"] = [0, 1, 2, 3]
            bank_state["i"] = 0
            for hp in range(4):
                bo, bd = (4, 5) if hp % 2 == 0 else (6, 7)
                MSET(ps[bo][:], 0.0, ["ps%d" % bo])
                MSET(ps[bd][:], 0.0, ["ps%d" % bd])
                for hh in range(2):
                    hd = 2 * hp + hh
                    p0 = 64 * hh
                    tp = (0, p0)

                    def softmax_pv(bS, kparts, c0, c1, maskap, pv_list):
                        pt = Pt[bS % 3]
                        pk = "Pt%d" % (bS % 3)
                        ACT(pt[0:kparts, c0:c1], ps[bS][0:kparts, c0:c1], AF.Exp, ["ps%d" % bS], [pk], scale=0.125)
                        TT(pt[0:kparts, c0:c1].rearrange("p (a b) -> p a b", b=maskap.shape[-1]),
                           pt[0:kparts, c0:c1].rearrange("p (a b) -> p a b", b=maskap.shape[-1]),
                           maskap.unsqueeze(1).to_broadcast([kparts, (c1 - c0) // maskap.shape[-1], maskap.shape[-1]]),
                           OP.mult, [pk, "maskb"], [pk])
                        for (vap, vkey, pc, oc) in pv_list:
                            MM(ps[bo][p0:p0 + 64, oc], vap, pt[0:kparts, pc], False, False, [vkey, pk, "ps%d" % bo], ["ps%d" % bo],
                               skip_group_check=True, tile_position=tp)
                            MM(ps[bd][p0:p0 + 64, oc], onesb[0:kparts, 0:64], pt[0:kparts, pc], False, False,
                               ["onesb", pk, "ps%d" % bd], ["ps%d" % bd], skip_group_check=True, tile_position=tp)

                    qc = 0 * 4 + hp
                    bS = nb()
                    pvl = []
                    for w in range(4):
                        U = 4 * i + w
                        kslot, ktb = (U // 4) % 2, U % 4
                        MM(ps[bS][:, w * 128:(w + 1) * 128], KT[0][p0:p0 + 64, hp, kslot * T + ktb * 128: kslot * T + (ktb + 1) * 128],
                           R[p0:p0 + 64, qc, w * 128:(w + 1) * 128], True, True,
                           ["KT0_%d_%d" % (kslot, ktb), "R%d" % qc], ["ps%d" % bS])
                        pvl.append((VH[:, U % 8, hd * 64:(hd + 1) * 64], "V0_%d" % (U % 8),
                                    slice(w * 128, (w + 1) * 128), slice(w * 128, (w + 1) * 128)))
                    softmax_pv(bS, 128, 0, 512, maskb[:, 0:128], pvl)
                    bS = nb()
                    pvl = []
                    w0 = 1 if i == 0 else 0
                    for w in range(w0, 4):
                        U = 4 * i + w - 1
                        kslot, ktb = (U // 4) % 2, U % 4
                        MM(ps[bS][:, w * 128:(w + 1) * 128], KT[0][p0:p0 + 64, hp, kslot * T + ktb * 128: kslot * T + (ktb + 1) * 128],
                           R[p0:p0 + 64, qc, w * 128:(w + 1) * 128], True, True,
                           ["KT0_%d_%d" % (kslot, ktb), "R%d" % qc], ["ps%d" % bS])
                        pvl.append((VH[:, U % 8, hd * 64:(hd + 1) * 64], "V0_%d" % (U % 8),
                                    slice(w * 128, (w + 1) * 128), slice(w * 128, (w + 1) * 128)))
                    softmax_pv(bS, 128, w0 * 128, 512, maskb[:, 128:256], pvl)
                    qc = 4 + hp
                    for prev in (0, 1):
                        if prev and i == 0:
                            continue
                        kslot = (i - prev) % 2
                        bS = nb()
                        pvl = []
                        for r4 in range(4):
                            MM(ps[bS][:, r4 * 128:(r4 + 1) * 128], KT[1][p0:p0 + 64, hp, kslot * T + r4: (kslot + 1) * T: 4],
                               R[p0:p0 + 64, qc, r4:T:4], True, True,
                               ["KT1_%d_%d" % (kslot, tb) for tb in range(4)] + ["R%d" % qc], ["ps%d" % bS])
                            slotv = 8 + 2 * r4 + kslot
                            pvl.append((VH[:, slotv, hd * 64:(hd + 1) * 64], "V1_%d" % slotv,
                                        slice(r4 * 128, (r4 + 1) * 128), slice(r4, T, 4)))
                        softmax_pv(bS, 128, 0, 512, maskb[:, 128:256] if prev else maskb[:, 0:128], pvl)
                    qc = 8 + hp
                    nk = 32 * (i + 1)
                    bS = nb()
                    pvl = []
                    for r16 in range(16):
                        MM(ps[bS][0:nk, r16 * 32:(r16 + 1) * 32], KT[2][p0:p0 + 64, hp, r16:(i + 1) * T:16],
                           R[p0:p0 + 64, qc, r16:T:16], True, True,
                           ["KT2_%d_%d" % (ii, tb) for ii in range(i + 1) for tb in range(4)] + ["R%d" % qc], ["ps%d" % bS])
                        pvl.append((VH[0:nk, 16 + r16, hd * 64:(hd + 1) * 64], "V2_%d" % r16,
                                    slice(r16 * 32, (r16 + 1) * 32), slice(r16, T, 16)))
                    softmax_pv(bS, nk, 0, 512, maskb[0:nk, 256 + 32 * i:256 + 32 * (i + 1)], pvl)
                RCP(gt[2][:], ps[bd][:], ["ps%d" % bd], ["gt2"])
                TT(attT(hp), ps[bo][:], gt[2][:], OP.mult, ["ps%d" % bo, "gt2"], ["R%d" % (12 + hp)])
            bank_state["set"] = list(range(8))
            if STAGE == 5:
                store_y(lambda tb: yp[s, tok0 + tb * 128: tok0 + (tb + 1) * 128, :], T, 128)
                return
            merge_and_out(T)
            ffn(T, 2, f2gu, f2dn)
            store_y(lambda tb: yp[s, tok0 + tb * 128: tok0 + (tb + 1) * 128, :], T, 128)

        def merge_and_out(Tn):
            for m in range(8):
                wt, wk = take(sl_m[m])
                Wg = wt[:, 0:2048].rearrange("p (s kc c) -> p s kc c", s=2, kc=8)
                Wb = wt[:, 2048:3072].rearrange("p (s kc c) -> p s kc c", s=2, kc=4)
                bgp, bbp, bga, bba = nb(), nb(), nb(), nb()
                for kc in range(8):
                    MM(ps[bgp][:, 0:Tn], Wg[:, 0, kc, :], h[:, kc, 0:Tn], kc == 0, kc == 7, [wk, "h%d" % kc], ["ps%d" % bgp])
                for kc in range(4):
                    MM(ps[bbp][:, 0:Tn], Wb[:, 0, kc, :], R[:, 16 + kc, 0:Tn], kc == 0, kc == 3, [wk, "R%d" % (16 + kc)], ["ps%d" % bbp])
                for kc in range(8):
                    MM(ps[bga][:, 0:Tn], Wg[:, 1, kc, :], h[:, kc, 0:Tn], kc == 0, kc == 7, [wk, "h%d" % kc], ["ps%d" % bga])
                for kc in range(4):
                    MM(ps[bba][:, 0:Tn], Wb[:, 1, kc, :], R[:, 12 + kc, 0:Tn], kc == 0, kc == 3, [wk, "R%d" % (12 + kc)], ["ps%d" % bba])
                ACT(gt[0][:, 0:Tn], ps[bgp][:, 0:Tn], AF.Sigmoid, ["ps%d" % bgp], ["gt0"])
                ACT(gt[1][:, 0:Tn], ps[bga][:, 0:Tn], AF.Sigmoid, ["ps%d" % bga], ["gt1"])
                TT(gt[0][:, 0:Tn], gt[0][:, 0:Tn], ps[bbp][:, 0:Tn], OP.mult, ["gt0", "ps%d" % bbp], ["gt0"])
                TT(gt[1][:, 0:Tn], gt[1][:, 0:Tn], ps[bba][:, 0:Tn], OP.mult, ["gt1", "ps%d" % bba], ["gt1"])
                TT(R[:, m, 0:Tn], gt[0][:, 0:Tn], gt[1][:, 0:Tn], OP.add, ["gt0", "gt1"], ["R%d" % m])
            for o in range(2):
                wt, wk = take(sl_o[o])
                Wv = wt[:, 0:4096].rearrange("p (kc c) -> p kc c", c=512)
                for mm in range(4):
                    m = 4 * o + mm
                    b_ = nb()
                    for kc in range(8):
                        MM(ps[b_][:, 0:Tn], Wv[:, kc, mm * 128:(mm + 1) * 128], R[:, kc, 0:Tn], kc == 0, kc == 7,
                           [wk, "R%d" % kc], ["ps%d" % b_])
                    TT(x[:, m, 0:Tn], ps[b_][:, 0:Tn], x[:, m, 0:Tn], OP.add, ["ps%d" % b_, "x%d" % m], ["x%d" % m])

        def sample_tile():
            Tn = NTOKS
            S.barrier()
            bank_state["set"] = list(range(6))
            Hf = HIST[:].bitcast(F32)
            qn = Hf[:, 0:1536]
            kn = Hf[:, 1536:3072]
            vv = Hf[:, 3072:4608]
            gq = Hf[:, 4608:6144]
            tA = Hf[:, 6144:7680]
            Oacc = Hf[:, 7680:8192]
            VHb = VH[:].rearrange("p a b -> p (a b)")
            Vf = VHb[:, 0:7168].bitcast(F32)
            CKb = [Vf[:, 1024 * j: 1024 * (j + 1)] for j in range(3)]
            Dacc = Vf[:, 3072:3080]
            pself = Vf[:, 3088:3112]
            p8 = Vf[:, 3120:3128]
            sred = Vf[:, 3136:3160]
            zf = Vf[:, 3168:3295]
            sh = Vf[:, 3296:3488]
            vld = Vf[:, 3488:3491]
            m0 = Vf[:, 3492:3496]
            qnb = VHb[:, 7168:8704]
            tB = VHb[:, 8704:9216]
            p8b = VHb[:, 9216:9224]
            zb = VHb[:, 9232:9359]
            identb = VHb[:, 9360:9488]
            ue = ub[:].rearrange("p a b -> p (a b)")[:, 0:1280].rearrange("p (g b t) -> p g b t", g=4, b=16)
            l1 = la[:, 0:320].rearrange("p (b t) -> p b t", t=20)
            l2 = lb[:, 0:320].rearrange("p (b t) -> p b t", t=20)
            ust = gt[2]
            stS = sq[0]
            DMA(gq, bass.AP(Wd["q_norm"].tensor, 0, [[0, 128], [1, 1536]]), [], ["gq"], "s0")
            DMA(zf, c_z, [], ["zf"], "s1")
            DMA(sh[0:64, :], c_shift, [], ["sh"], "s2")
            DMA(vld[0:64, :], c_valid, [], ["vld"], "s3")
            DMA(m0, c_m0, [], ["m0"], "s4")
            CP(zb, zf, ["zf"], ["zb"])
            CP(identb, ident[:], ["ident"], ["identb"])
            load_x(lambda tb: xs[:, :], Tn, Tn)
            ffn(Tn, 0, f1gu, f1dn)
            rmsnorm_to_h(Tn, 1)
            wt, wk = take(sl_u)
            Wv = wt[:, 0:4096].rearrange("p (kc c) -> p kc c", c=512)
            nblk = (NSB + 7) // 8
            for blk in range(nblk):
                nb_ = min(8, NSB - 8 * blk)
                nr = nb_ * 15
                DMA(stS[0:nr, 0:512], spool[8 * blk:8 * blk + nb_, :, :].rearrange("b r c -> (b r) c"), [], ["sq0"], "s5")
                b_ = nb()
                for g in range(4):
                    TR(ps[b_][:, g * 128:g * 128 + nr], stS[0:nr, g * 128:(g + 1) * 128], ident[0:nr, 0:nr], ["sq0", "ident"], ["ps%d" % b_])
                for g in range(4):
                    CP(ue[:, g, 8 * blk:8 * blk + nb_, 1:16], ps[b_][:, g * 128:g * 128 + nr].rearrange("p (b r) -> p b r", r=15),
                       ["ps%d" % b_], ["ub%d" % g], eng=("act" if g % 2 else "dve"))
            for g in range(4):
                b_ = nb()
                for kc in range(8):
                    MM(ps[b_][:, 0:Tn], Wv[:, kc, g * 128:(g + 1) * 128], h[:, kc, 0:Tn], kc == 0, kc == 7, [wk, "h%d" % kc], ["ps%d" % b_])
                CP(ue[:, g, 0:NSB, 16:20], ps[b_][:, 0:Tn].rearrange("p (b t) -> p b t", t=4), ["ps%d" % b_], ["ub%d" % g], eng="act")
            for g in range(4):
                ug = ue[:, g, 0:NSB, :]
                uk = "ub%d" % g
                A, Bf = l1[:, 0:NSB, :], l2[:, 0:NSB, :]
                TT(A[:, :, 2:20], ug[:, :, 2:20], ug[:, :, 1:19], OP.add, [uk], ["la"])
                cur, ck = A, "la"
                if g >= 1:
                    TT(Bf[:, :, 4:20], A[:, :, 4:20], A[:, :, 2:18], OP.add, ["la"], ["lb"])
                    cur, ck = Bf, "lb"
                if g >= 2:
                    TT(A[:, :, 8:20], Bf[:, :, 8:20], Bf[:, :, 4:16], OP.add, ["lb"], ["la"])
                    cur, ck = A, "la"
                if g >= 3:
                    TT(Bf[:, :, 16:20], A[:, :, 16:20], A[:, :, 8:12], OP.add, ["la"], ["lb"])
                    cur, ck = Bf, "lb"
                STT(pl[:, g, 0:Tn].rearrange("p (b t) -> p b t", t=4), cur[:, :, 16:20], 1.0 / POOL_W[g], ug[:, :, 16:20],
                    OP.mult, OP.subtract, [ck, uk], ["pl%d" % g])
            b_ = nb()
            for g in range(4):
                CP(tA[:, g * 64:g * 64 + Tn].rearrange("p (b t) -> p b t", t=4), ue[:, g, 0:NSB, 16:20], ["ub%d" % g], ["tA"])
            for g in range(4):
                TR(ps[b_][0:Tn, g * 128:(g + 1) * 128], tA[:, g * 64:g * 64 + Tn], ident[:], ["tA", "ident"], ["ps%d" % b_])
            CP(ust[0:Tn, :], ps[b_][0:Tn, :], ["ps%d" % b_], ["gt2"], eng="act")
            DMA(pools[:, 11:15, :], ust[0:Tn, :], ["gt2"], [], "s6")
            for g in range(4):
                b_ = nb()
                MM(ps[b_][:, 0:Tn], pwb[:, g, :], pl[:, g, 0:Tn], True, True, ["pwb", "pl%d" % g], ["ps%d" % b_])
                TS(R[:, 16 + g, 0:Tn], ps[b_][:, 0:Tn], psc[:, g:g + 1], None, OP.mult, None, ["ps%d" % b_, "psc"], ["R%d" % (16 + g)])
            for g in range(3):
                Wn = (128, 512, 2048)[g]
                wt, wk = take(sl_q[g])
                b_ = proj_tok(wt, wk, lambda kc: h[:, kc, 0:Tn], Tn)
                stg, sk = head_norm(b_, Tn, gq[0:Tn, g * 512:(g + 1) * 512], "gq")
                CP(qn[0:Tn, g * 512:(g + 1) * 512], stg[0:Tn, :], [sk], ["qn"])
                wt, wk = take(sl_k[g])
                b_ = proj_tok(wt, wk, lambda kc: h[:, kc, 0:Tn], Tn)
                stg, sk = head_norm(b_, Tn, gk[0:Tn, g * 512:(g + 1) * 512], "gk")
                CP(kn[0:Tn, g * 512:(g + 1) * 512], stg[0:Tn, :], [sk], ["kn"])
                DMA(kvs[g][:, Wn - 4:Wn, 0:512], stg[0:Tn, :], [sk], [], "ko" + sk)
                wt, wk = take(sl_v[g])
                b_ = proj_tok(wt, wk, lambda kc: h[:, kc, 0:Tn], Tn)
                CP(vv[0:Tn, g * 512:(g + 1) * 512], ps[b_][0:Tn, :], ["ps%d" % b_], ["vv"], eng="act")
                DMA(kvs[g][:, Wn - 4:Wn, 512:1024], vv[0:Tn, g * 512:(g + 1) * 512], ["vv"], [], "s7")
            CP(qnb[0:Tn, :], qn[0:Tn, :], ["qn"], ["qnb"])
            TT(tA[0:Tn, :], qn[0:Tn, :], kn[0:Tn, :], OP.mult, ["qn", "kn"], ["tA"])
            RED(sred[0:Tn, :], tA[0:Tn, :].rearrange("p (a b) -> p a b", b=64), ["tA"], ["sred"])
            ACT(pself[0:Tn, :], sred[0:Tn, :], AF.Exp, ["sred"], ["pself"], scale=0.125)
            TT(Dacc[0:Tn, :], pself[0:Tn, 0:8], pself[0:Tn, 8:16], OP.add, ["pself"], ["Dacc"])
            TT(Dacc[0:Tn, :], Dacc[0:Tn, :], pself[0:Tn, 16:24], OP.add, ["pself", "Dacc"], ["Dacc"])
            TT(tA[0:Tn, :].rearrange("p (a b) -> p a b", b=64), vv[0:Tn, :].rearrange("p (a b) -> p a b", b=64),
               pself[0:Tn, :].unsqueeze(2).to_broadcast([Tn, 24, 64]), OP.mult, ["vv", "pself", "tA"], ["tA"])
            TT(Oacc[0:Tn, :], tA[0:Tn, 0:512], tA[0:Tn, 512:1024], OP.add, ["tA"], ["Oacc"])
            TT(Oacc[0:Tn, :], Oacc[0:Tn, :], tA[0:Tn, 1024:1536], OP.add, ["tA", "Oacc"], ["Oacc"])
            for d in range(1, 4):
                bk, bv = nb(), nb()
                MM(ps[bk][0:Tn, :], sh[0:Tn, (d - 1) * 64:(d - 1) * 64 + Tn], kn[0:Tn, 0:512], True, True, ["sh", "kn"], ["ps%d" % bk])
                MM(ps[bv][0:Tn, :], sh[0:Tn, (d - 1) * 64:(d - 1) * 64 + Tn], vv[0:Tn, 0:512], True, True, ["sh", "vv"], ["ps%d" % bv])
                TT(tA[0:Tn, 0:512], qn[0:Tn, 0:512], ps[bk][0:Tn, :], OP.mult, ["qn", "ps%d" % bk], ["tA"])
                RED(sred[0:Tn, 0:8], tA[0:Tn, 0:512].rearrange("p (a b) -> p a b", b=64), ["tA"], ["sred"])
                ACT(p8[0:Tn, :], sred[0:Tn, 0:8], AF.Exp, ["sred"], ["p8"], scale=0.125)
                TS(p8[0:Tn, :], p8[0:Tn, :], vld[0:Tn, d - 1:d], None, OP.mult, None, ["p8", "vld"], ["p8"])
                TT(Dacc[0:Tn, :], Dacc[0:Tn, :], p8[0:Tn, :], OP.add, ["Dacc", "p8"], ["Dacc"])
                TT(tA[0:Tn, 0:512].rearrange("p (a b) -> p a b", b=64), ps[bv][0:Tn, :].rearrange("p (a b) -> p a b", b=64),
                   p8[0:Tn, :].unsqueeze(2).to_broadcast([Tn, 8, 64]), OP.mult, ["ps%d" % bv, "p8", "tA"], ["tA"])
                TT(Oacc[0:Tn, :], Oacc[0:Tn, :], tA[0:Tn, 0:512], OP.add, ["tA", "Oacc"], ["Oacc"])
            BO, BD = 6, 7
            MSET(ps[BO][:], 0.0, ["ps6"])
            MSET(ps[BD][:], 0.0, ["ps7"])
            ckr = {"i": 0}
            for b in range(NSB):
                for g in range(3):
                    dil = (1, 4, 16)[g]
                    for t in range(4):
                        tok = 4 * b + t
                        if g == 0 and t > 0:
                            pass
                        else:
                            j = ckr["i"] % 3
                            ckr["i"] += 1
                            ck_ap, ck_key = CKb[j], "CK%d" % j
                            src = cache[g][b, t:t + 127 * dil + 1:dil, :] if g > 0 else cache[g][b, 0:128, :]
                            DMA(ck_ap, src, [], [ck_key], "ck%d" % j)
                        bq = nb()
                        MM(ps[bq][:], identb[0:Tn, tok:tok + 1].to_broadcast([Tn, 128]), qnb[0:Tn, g * 512:(g + 1) * 512], True, True,
                           ["identb", "qnb"], ["ps%d" % bq])
                        TT(tA[:, 0:512], ck_ap[:, 0:512], ps[bq][:], OP.mult, [ck_key, "ps%d" % bq], ["tA"])
                        RED(sred[:, 0:8], tA[:, 0:512].rearrange("p (a b) -> p a b", b=64), ["tA"], ["sred"])
                        ACT(p8[:, :], sred[:, 0:8], AF.Exp, ["sred"], ["p8"], scale=0.125)
                        if g == 0:
                            TS(p8[:, :], p8[:, :], m0[:, t:t + 1], None, OP.mult, None, ["p8", "m0"], ["p8"])
                        CP(p8b[:, :], p8[:, :], ["p8"], ["p8b"])
                        TT(tB.rearrange("p (a b) -> p a b", b=64), ck_ap[:, 512:1024].rearrange("p (a b) -> p a b", b=64),
                           p8[:, :].unsqueeze(2).to_broadcast([128, 8, 64]), OP.mult, [ck_key, "p8"], ["tB"])
                        MM(ps[BO][0:Tn, :], zb[:, 63 - tok:63 - tok + Tn], tB, False, False, ["zb", "tB", "ps6"], ["ps6"], skip_group_check=True)
                        MM(ps[BD][0:Tn, 0:8], zb[:, 63 - tok:63 - tok + Tn], p8b[:, :], False, False, ["zb", "p8b", "ps7"], ["ps7"], skip_group_check=True)
            TT(Oacc[0:Tn, :], Oacc[0:Tn, :], ps[BO][0:Tn, :], OP.add, ["Oacc", "ps6"], ["Oacc"])
            TT(Dacc[0:Tn, :], Dacc[0:Tn, :], ps[BD][0:Tn, 0:8], OP.add, ["Dacc", "ps7"], ["Dacc"])
            RCP(Dacc[0:Tn, :], Dacc[0:Tn, :], ["Dacc"], ["Dacc"])
            TT(Oacc[0:Tn, :].rearrange("p (a b) -> p a b", b=64), Oacc[0:Tn, :].rearrange("p (a b) -> p a b", b=64),
               Dacc[0:Tn, :].unsqueeze(2).to_broadcast([Tn, 8, 64]), OP.mult, ["Oacc", "Dacc"], ["Oacc"])
            b_ = nb()
            for cc in range(4):
                TR(ps[b_][:, cc * 128:cc * 128 + Tn], Oacc[0:Tn, cc * 128:(cc + 1) * 128], ident[0:Tn, 0:Tn], ["Oacc", "ident"], ["ps%d" % b_])
            CP(R[:, 12:16, 0:Tn], ps[b_][:].rearrange("p (a b) -> p a b", a=4)[:, :, 0:Tn], ["ps%d" % b_],
               ["R%d" % c for c in range(12, 16)])
            bank_state["set"] = list(range(8))
            merge_and_out(Tn)
            ffn(Tn, 2, f2gu, f2dn)
            store_y(lambda tb: ys[:, :], Tn, Tn)

        for s in range(NSEQ):
            for i in range(NT):
                prompt_tile(s, i)
        if do_sample:
            sample_tile()
        S.emit(nc)
    return nc


_PROG = {}


def kernel(x_prompt, x_sample, cache_kv_w128, cache_kv_w512, cache_kv_w2048, state_pool,
           ffn1_norm, ffn1_w_gu, ffn1_w_down, mix_norm, w_in, q_norm, k_norm, pool_w,
           pool_scale, w_branch_pool, w_branch_att, w_out, ffn2_norm, ffn2_w_gu, ffn2_w_down):
    f = lambda a: np.ascontiguousarray(np.asarray(a, dtype=np.float32))
    B, SEQ, D = x_prompt.shape
    DB = x_sample.shape[0]
    NSEQ, NSB = B // NCORES, DB // NCORES
    if "nc" not in _PROG:
        _PROG["nc"] = build_program(NSEQ=NSEQ, NSB=NSB, SEQ=SEQ)
    nc = _PROG["nc"]
    shared = {
        "ffn1_norm": f(ffn1_norm[0]), "ffn1_w_gu": f(ffn1_w_gu[0]), "ffn1_w_down": f(ffn1_w_down[0]),
        "mix_norm": f(mix_norm[0]), "w_in": f(w_in[0]), "q_norm": f(q_norm[0]).reshape(1536),
        "k_norm": f(k_norm[0]).reshape(1536), "pool_w": f(pool_w[0]), "pool_scale": f(pool_scale[0]),
        "w_branch_pool": f(w_branch_pool[0]), "w_branch_att": f(w_branch_att[0]), "w_out": f(w_out[0]),
        "ffn2_norm": f(ffn2_norm[0]), "ffn2_w_gu": f(ffn2_w_gu[0]), "ffn2_w_down": f(ffn2_w_down[0]),
    }
    shared.update(_const_tables())
    in_maps = []
    for c in range(NCORES):
        m = dict(shared)
        m["xp"] = f(x_prompt[c * NSEQ:(c + 1) * NSEQ])
        m["xs"] = f(x_sample[c * NSB:(c + 1) * NSB]).reshape(NSB * 4, D)
        m["c128"] = f(cache_kv_w128[0, c * NSB:(c + 1) * NSB]).reshape(NSB, 128, 1024)
        m["c512"] = f(cache_kv_w512[0, c * NSB:(c + 1) * NSB]).reshape(NSB, 512, 1024)
        m["c2048"] = f(cache_kv_w2048[0, c * NSB:(c + 1) * NSB]).reshape(NSB, 2048, 1024)
        m["spool"] = f(state_pool[0, c * NSB:(c + 1) * NSB])
        in_maps.append(m)
    res = run_bass_kernel_spmd(nc, in_maps, core_ids=list(range(NCORES)))
    r = res.results
    cat = lambda k: np.concatenate([np.asarray(r[c][k]) for c in range(NCORES)], axis=0)
    y_prompt = cat("yp")
    y_sample = cat("ys").reshape(DB, 4, D)
    kv128p = cat("kv128p").reshape(1, B, 128, 2, 8, 64)
    kv512p = cat("kv512p").reshape(1, B, 512, 2, 8, 64)
    kv2048p = cat("kv2048p").reshape(1, B, SEQ, 2, 8, 64)
    poolp = cat("poolp").reshape(1, B, 15, 512)
    kv128s = cat("kv128s").reshape(1, DB, 128, 2, 8, 64)
    kv512s = cat("kv512s").reshape(1, DB, 512, 2, 8, 64)
    kv2048s = cat("kv2048s").reshape(1, DB, 2048, 2, 8, 64)
    pools_ = cat("pools").reshape(1, DB, 15, 512)
    return (y_prompt, y_sample, kv128p, kv512p, kv2048p, poolp, kv128s, kv512s, kv2048s, pools_)
```

```python
import contextlib
import numpy as np
import concourse.bass as bass
import concourse.mybir as mybir
from concourse.bass_utils import run_bass_kernel_spmd

F32 = mybir.dt.float32
BF = mybir.dt.bfloat16
AF = mybir.ActivationFunctionType
OP = mybir.AluOpType
AX = mybir.AxisListType

NCORES = 8
EPS = 1e-6
SLOT = 5632
NSLOT = 3
POOL_W = (2, 4, 8, 16)


class _Op:
    __slots__ = ("eng", "fn", "deps", "sig", "dma_sem", "val", "idx")

    def __init__(self, eng, fn, dma_sem):
        self.eng, self.fn, self.dma_sem = eng, fn, dma_sem
        self.deps, self.sig, self.val, self.idx = set(), False, None, None


class Sched:
    ENGS = ("pe", "act", "dve", "pool", "sp")

    def __init__(self):
        self.ops, self.last_w, self.readers = [], {}, {}
        self.bar = set()
        self.last_eng = {}
        self.last_dma = {}

    def op(self, eng, fn, reads=(), writes=(), dma_sem=None):
        o = _Op(eng, fn, dma_sem)
        o.idx = len(self.ops)
        deps = set(self.bar)
        for k in reads:
            w = self.last_w.get(k)
            if w is not None:
                deps.add(w)
            if k.startswith("ps"):
                deps.update(r for r in self.readers.get(k, ()) if self.ops[r].eng != eng)
        for k in writes:
            w = self.last_w.get(k)
            if w is not None:
                deps.add(w)
            deps.update(self.readers.get(k, ()))
        for k in reads:
            self.readers.setdefault(k, []).append(o.idx)
        for k in writes:
            self.last_w[k] = o.idx
            self.readers[k] = []
        if eng == "pe" and dma_sem is None:
            deps = {d for d in deps if not (self.ops[d].eng == "pe" and self.ops[d].dma_sem is None)}
        o.deps = deps
        self.ops.append(o)
        if dma_sem is None:
            self.last_eng[eng] = o.idx
        else:
            self.last_dma[dma_sem] = o.idx
        return o

    def barrier(self):
        self.bar = set(self.last_eng.values()) | set(self.last_dma.values())

    def emit(self, nc):
        ops = self.ops
        for o in ops:
            for d in o.deps:
                ops[d].sig = True
        cnt = {e: 0 for e in self.ENGS}
        dma_names, dma_cnt, dma_eng = [], {}, {}
        for o in ops:
            if o.dma_sem is not None:
                if o.dma_sem not in dma_cnt:
                    dma_cnt[o.dma_sem] = 0
                    dma_names.append(o.dma_sem)
                    dma_eng[o.dma_sem] = o.eng
                assert dma_eng[o.dma_sem] == o.eng
                dma_cnt[o.dma_sem] += 16
                o.val = dma_cnt[o.dma_sem]
            elif o.sig:
                cnt[o.eng] += 1
                o.val = cnt[o.eng]
        with contextlib.ExitStack() as st:
            sems = {}
            for e in self.ENGS:
                sems[("eng", e)] = st.enter_context(nc.semaphore("s_" + e))
            for n in dma_names:
                sems[("dma", n)] = st.enter_context(nc.semaphore("d_" + str(n)))
            block = st.enter_context(nc.Block())

            def semkey(o):
                return ("dma", o.dma_sem) if o.dma_sem is not None else ("eng", o.eng)

            def run(engname, engobj):
                known = {}
                for o in ops:
                    if o.eng != engname:
                        continue
                    need = {}
                    for d in o.deps:
                        p = ops[d]
                        k = semkey(p)
                        if p.val > need.get(k, 0):
                            need[k] = p.val
                    for k, v in need.items():
                        if known.get(k, 0) < v:
                            engobj.wait_ge(sems[k], v)
                            known[k] = v
                    ins = o.fn(engobj)
                    if o.dma_sem is not None:
                        ins.then_inc(sems[("dma", o.dma_sem)], 16)
                    elif o.sig:
                        ins.then_inc(sems[("eng", engname)], 1)
                for n in dma_names:
                    if dma_eng[n] == engname and known.get(("dma", n), 0) < dma_cnt[n]:
                        engobj.wait_ge(sems[("dma", n)], dma_cnt[n])

            @block.tensor
            def _(e):
                run("pe", e)

            @block.scalar
            def _(e):
                run("act", e)

            @block.vector
            def _(e):
                run("dve", e)

            @block.gpsimd
            def _(e):
                run("pool", e)

            @block.sync
            def _(e):
                run("sp", e)


def _const_tables():
    k = np.arange(128)[:, None]
    q = np.arange(128)[None, :]
    m = np.zeros((128, 384), np.float32)
    m[:, 0:128] = (k <= q)
    m[:, 128:256] = (k >= q)
    for i in range(4):
        c = np.arange(32)[None, :]
        blk = np.where(k < 32 * i, 1.0, np.where(k < 32 * (i + 1), ((k - 32 * i) <= c) * 1.0, 0.0))
        m[:, 256 + 32 * i:256 + 32 * (i + 1)] = blk
    invc = np.zeros((128, 4, 16), np.float32)
    for g, w in enumerate(POOL_W):
        invc[:, g, :] = 1.0 / np.minimum(w, np.arange(16) + 1)[None, :]
    sh = np.zeros((64, 3, 64), np.float32)
    vd = np.zeros((64, 3), np.float32)
    for d in range(1, 4):
        for dst in range(64):
            if dst % 4 >= d:
                sh[dst - d, d - 1, dst] = 1.0
                vd[dst, d - 1] = 1.0
    m0 = np.zeros((128, 4), np.float32)
    for t in range(4):
        m0[:, t] = (np.arange(128) >= t)
    z = np.zeros((128, 127), np.float32)
    z[:, 63] = 1.0
    return {"c_ident": np.eye(128, dtype=np.float32), "c_mask": m, "c_invc": invc.reshape(128, 64),
            "c_shift": sh.reshape(64, 192), "c_valid": vd, "c_m0": m0, "c_z": z}


def build_program(NSEQ=4, NSB=16, SEQ=2048, do_sample=True, STAGE=99):
    T = 512
    NT = SEQ // T
    NTOKS = NSB * 4
    nc = bass.Bass("TRN2", target_bir_lowering=False)
    S = Sched()

    def din(name, shape):
        return nc.dram_tensor(name, list(shape), F32, kind="ExternalInput").ap()

    def dout(name, shape):
        return nc.dram_tensor(name, list(shape), F32, kind="ExternalOutput").ap()

    xp = din("xp", [NSEQ, SEQ, 1024])
    xs = din("xs", [NTOKS, 1024])
    cache = [din("c128", [NSB, 128, 1024]), din("c512", [NSB, 512, 1024]), din("c2048", [NSB, 2048, 1024])]
    spool = din("spool", [NSB, 15, 512])
    Wd = {}
    for nm, shp in (("ffn1_norm", [1024]), ("ffn1_w_gu", [1024, 5632]), ("ffn1_w_down", [2816, 1024]),
                    ("mix_norm", [1024]), ("w_in", [1024, 7168]), ("q_norm", [1536]), ("k_norm", [1536]),
                    ("pool_w", [4, 128, 128]), ("pool_scale", [512]), ("w_branch_pool", [512, 1024]),
                    ("w_branch_att", [512, 1024]), ("w_out", [1024, 1024]), ("ffn2_norm", [1024]),
                    ("ffn2_w_gu", [1024, 5632]), ("ffn2_w_down", [2816, 1024])):
        Wd[nm] = din(nm, shp)
    c_ident = din("c_ident", [128, 128])
    c_mask = din("c_mask", [128, 384])
    c_invc = din("c_invc", [128, 64])
    c_shift = din("c_shift", [64, 192])
    c_valid = din("c_valid", [64, 3])
    c_m0 = din("c_m0", [128, 4])
    c_z = din("c_z", [128, 127])

    yp = dout("yp", [NSEQ, SEQ, 1024])
    ys = dout("ys", [NTOKS, 1024])
    kvp = [dout("kv128p", [NSEQ, 128, 1024]), dout("kv512p", [NSEQ, 512, 1024]), dout("kv2048p", [NSEQ, SEQ, 1024])]
    poolp = dout("poolp", [NSEQ, 15, 512])
    kvs = [dout("kv128s", [NSB, 128, 1024]), dout("kv512s", [NSB, 512, 1024]), dout("kv2048s", [NSB, 2048, 1024])]
    pools = dout("pools", [NSB, 15, 512])

    slabs = []

    def piece(w, col0, ncols, nk, off):
        return (w[0:nk * 128, col0:col0 + ncols].rearrange("(kc p) c -> p kc c", p=128), off, nk, ncols)

    def add_slab(n, pieces):
        slabs.append((n, pieces))
        return len(slabs) - 1

    def ffn_slabs(wgu, wdn):
        gu = []
        for s in range(11):
            gu.append(add_slab(4096, [("ab", wgu, s)]))
        dn = [add_slab(5632, [piece(wdn, d * 256, 256, 22, 0)]) for d in range(4)]
        return gu, dn

    f1gu, f1dn = ffn_slabs(Wd["ffn1_w_gu"], Wd["ffn1_w_down"])
    sl_u = add_slab(4096, [piece(Wd["w_in"], 0, 512, 8, 0)])
    sl_q = [add_slab(4096, [piece(Wd["w_in"], 512 + g * 512, 512, 8, 0)]) for g in range(3)]
    sl_k = [add_slab(4096, [piece(Wd["w_in"], 2048 + g * 512, 512, 8, 0)]) for g in range(3)]
    sl_v = [add_slab(4096, [piece(Wd["w_in"], 3584 + g * 512, 512, 8, 0)]) for g in range(3)]
    sl_m = [add_slab(3072, [piece(Wd["w_in"], 5120 + m * 128, 128, 8, 0),
                            piece(Wd["w_in"], 6144 + m * 128, 128, 8, 1024),
                            piece(Wd["w_branch_pool"], m * 128, 128, 4, 2048),
                            piece(Wd["w_branch_att"], m * 128, 128, 4, 2560)]) for m in range(8)]
    sl_o = [add_slab(4096, [piece(Wd["w_out"], o * 512, 512, 8, 0)]) for o in range(2)]
    f2gu, f2dn = ffn_slabs(Wd["ffn2_w_gu"], Wd["ffn2_w_down"])
    NSLAB = len(slabs)
    scr = nc.dram_tensor("wscr", [NSLAB, 128, SLOT], BF).ap()

    tile_order = (f1gu + f1dn + [sl_u] + [x for g in range(3) for x in (sl_q[g], sl_k[g], sl_v[g])]
                  + sl_m + sl_o + f2gu + f2dn)
    conv_order = tile_order

    with contextlib.ExitStack() as st:
        E = st.enter_context

        def sb(name, shape, dt=F32):
            return E(nc.sbuf_tensor(name, list(shape), dt))

        x = sb("x", [128, 8, T])
        h = sb("h", [128, 8, T], BF)
        R = sb("R", [128, 22, T], BF)
        wring = [sb("wr%d" % i, [128, SLOT], BF) for i in range(NSLOT)]
        HIST = sb("HIST", [128, 16384], BF)
        VH = sb("VH", [128, 32, 512], BF)
        ub = sb("ub", [128, 4, 528])
        la = sb("la", [128, 528])
        lb = sb("lb", [128, 528])
        pl = sb("pl", [128, 4, T], BF)
        kvst = [sb("kvst%d" % i, [128, 512]) for i in range(6)]
        sq = [sb("sq%d" % i, [128, 512]) for i in range(2)]
        sqb = [sb("sqb%d" % i, [128, 512], BF) for i in range(2)]
        rstd = sb("rstd", [128, 512])
        Pt = [sb("Pt%d" % i, [128, 512], BF) for i in range(4)]
        sa = [sb("sa%d" % i, [128, 512], BF) for i in range(2)]
        gt = [sb("gt%d" % i, [128, 512]) for i in range(3)]
        small = sb("small", [128, 64])
        ident = sb("ident", [128, 128])
        onesb = sb("onesb", [128, 128], BF)
        maskf = sb("maskf", [128, 384])
        maskb = sb("maskb", [128, 384], BF)
        gk = sb("gk", [128, 1536])
        gqT = sb("gqT", [128, 12])
        gn = sb("gn", [128, 3, 8])
        psc = sb("psc", [128, 4])
        pwf = sb("pwf", [128, 4, 128])
        pwb = sb("pwb", [128, 4, 128], BF)
        invc = sb("invc", [128, 64])
        pst = sb("pst", [16, 512])
        ps = [E(nc.psum_tensor("ps%d" % i, [128, 512], F32)) for i in range(8)]
        import os as _os
        if _os.environ.get("K_VERBOSE"):
            print("SBUF bytes remaining", nc.sbuf_bytes_remaining)

        def Rblk(j0, n):
            return R[:, j0:j0 + n, :]
        QT = lambda c0, n: Rblk(c0, n)
        attT = lambda c: R[:, 12 + c, :]
        py = lambda c: R[:, 16 + c, :]
        mg = lambda c: R[:, c, :]

        Rf = R[:].rearrange("p a b -> p (a b)").bitcast(F32)

        def tokst(i):
            return Rf[:, i * 1024:(i + 1) * 1024]

        def tokst_keys(i):
            return ["R%d" % j for j in range(4 * i, 4 * i + 4)]

        KT = [HIST[:, 0:4096].rearrange("p (c t) -> p c t", c=4),
              HIST[:, 4096:8192].rearrange("p (c t) -> p c t", c=4),
              HIST[:, 8192:16384].rearrange("p (c t) -> p c t", c=4)]

        bank_state = {"i": 0, "set": list(range(8))}

        def nb():
            s_ = bank_state["set"]
            b = s_[bank_state["i"] % len(s_)]
            bank_state["i"] += 1
            return b

        def MM(out, lhsT, rhs, start, stop, reads, writes, **kw):
            S.op("pe", lambda e: e.matmul(out, lhsT=lhsT, rhs=rhs, start=start, stop=stop, **kw), reads, writes)

        def TR(out, in_, idn, reads, writes):
            S.op("pe", lambda e: e.transpose(out, in_, idn), reads, writes)

        def ACT(out, in_, func, reads, writes, **kw):
            S.op("act", lambda e: e.activation(out=out, in_=in_, func=func, **kw), reads, writes)

        def TT(out, in0, in1, op, reads, writes, eng="dve"):
            S.op(eng, lambda e: e.tensor_tensor(out=out, in0=in0, in1=in1, op=op), reads, writes)

        def TS(out, in0, s1, s2, op0, op1, reads, writes, eng="dve"):
            if op1 is None:
                S.op(eng, lambda e: e.tensor_scalar(out=out, in0=in0, scalar1=s1, scalar2=None, op0=op0), reads, writes)
            else:
                S.op(eng, lambda e: e.tensor_scalar(out=out, in0=in0, scalar1=s1, scalar2=s2, op0=op0, op1=op1), reads, writes)

        def STT(out, in0, scalar, in1, op0, op1, reads, writes):
            S.op("dve", lambda e: e.scalar_tensor_tensor(out=out, in0=in0, scalar=scalar, in1=in1, op0=op0, op1=op1),
                 reads, writes)

        def CP(out, in_, reads, writes, eng="dve"):
            if eng == "act":
                S.op("act", lambda e: e.activation(out=out, in_=in_, func=AF.Copy), reads, writes)
            else:
                S.op(eng, lambda e: e.tensor_copy(out=out, in_=in_), reads, writes)

        def RCP(out, in_, reads, writes):
            S.op("dve", lambda e: e.reciprocal(out=out, in_=in_), reads, writes)

        def RED(out, in_, reads, writes):
            S.op("dve", lambda e: e.tensor_reduce(out=out, in_=in_, axis=AX.X, op=OP.add), reads, writes)

        def MSET(ap, v, writes, eng="dve"):
            S.op(eng, lambda e: e.memset(ap, v), (), writes)

        def DMA(out, in_, reads, writes, sem, eng="sp", **kw):
            S.op(eng, lambda e: e.dma_start(out=out, in_=in_, **kw), reads, writes, dma_sem=sem)

        DMA(ident[:], c_ident, [], ["ident"], "c0")
        DMA(maskf[:], c_mask, [], ["maskf"], "c1")
        DMA(invc[:], c_invc, [], ["invc"], "c2")
        DMA(gk[:], bass.AP(Wd["k_norm"].tensor, 0, [[0, 128], [1, 1536]]), [], ["gk"], "c3")
        DMA(pwf[:], Wd["pool_w"].rearrange("g c d -> c g d"), [], ["pwf"], "c4")
        for j, nm in enumerate(("ffn1_norm", "mix_norm", "ffn2_norm")):
            DMA(gn[:, j, :], Wd[nm].rearrange("(c p) -> p c", p=128), [], ["gn"], "c5", allow_slow_non_contiguous=True)
        DMA(gqT[:], Wd["q_norm"].rearrange("(c p) -> p c", p=128), [], ["gqT"], "c6", allow_slow_non_contiguous=True)
        DMA(psc[:], Wd["pool_scale"].rearrange("(g p) -> p g", p=128), [], ["psc"], "c7", allow_slow_non_contiguous=True)
        CP(maskb[:], maskf[:], ["maskf"], ["maskb"])
        CP(pwb[:], pwf[:], ["pwf"], ["pwb"])
        MSET(onesb[:], 1.0, ["onesb"])

        for sl in conv_order:
            n, pieces = slabs[sl]
            for pc in pieces:
                if pc[0] == "ab":
                    _, wgu, s_ = pc
                    dstv = scr[sl, :, 0:4096].rearrange("p (kc ab c) -> p kc ab c", kc=8, ab=2)
                    for ab in range(2):
                        src = wgu[:, ab * 2816 + s_ * 256: ab * 2816 + s_ * 256 + 256].rearrange("(kc p) c -> p kc c", p=128)
                        DMA(dstv[:, :, ab, :], src, [], ["scr%d" % sl], "cv%d" % sl, eng="pool")
                else:
                    src, off, nk, ncols = pc
                    dstv = scr[sl, :, off:off + nk * ncols].rearrange("p (kc c) -> p kc c", c=ncols)
                    DMA(dstv, src, [], ["scr%d" % sl], "cv%d" % sl, eng="pool")

        bulk = []
        if do_sample:
            for g, Wn in enumerate((128, 512, 2048)):
                for b in range(NSB):
                    bulk.append((kvs[g][b, 0:Wn - 4, :], cache[g][b, 4:Wn, :]))
            bulk.append((pools[:, 0:11, :], spool[:, 4:15, :]))

        def issue_bulk(k):
            for _ in range(k):
                if bulk:
                    o_, i_ = bulk.pop()
                    DMA(o_, i_, ["tilestart"], [], "bulk", eng="pool")

        n_tiles_total = NSEQ * NT + (1 if do_sample else 0)
        stream = tile_order * n_tiles_total
        wst = {"issued": 0, "consumed": 0}

        def prefetch(upto):
            while wst["issued"] <= upto and wst["issued"] < len(stream):
                k = wst["issued"]
                sl = stream[k]
                slot = k % NSLOT
                n = slabs[sl][0]
                DMA(wring[slot][:, 0:n], scr[sl, :, 0:n], ["scr%d" % sl], ["w%d" % slot], "wl%d" % slot)
                wst["issued"] += 1

        def take(expect):
            k = wst["consumed"]
            assert stream[k] == expect, (k, stream[k], expect)
            prefetch(k + NSLOT - 1)
            wst["consumed"] += 1
            slot = k % NSLOT
            return wring[slot], "w%d" % slot

        def rmsnorm_to_h(Tn, j):
            bn = nb()
            for c in range(8):
                s_ = sqb[c % 2]
                ACT(s_[:, 0:Tn], x[:, c, 0:Tn], AF.Square, ["x%d" % c], ["sqb%d" % (c % 2)])
                MM(ps[bn][:, 0:Tn], onesb[:], s_[:, 0:Tn], c == 0, c == 7, ["onesb", "sqb%d" % (c % 2)], ["ps%d" % bn])
            ACT(rstd[:, 0:Tn], ps[bn][:, 0:Tn], AF.Sqrt, ["ps%d" % bn], ["rstd"], bias=EPS, scale=1.0 / 1024)
            RCP(rstd[:, 0:Tn], rstd[:, 0:Tn], ["rstd"], ["rstd"])
            for c in range(8):
                STT(h[:, c, 0:Tn], x[:, c, 0:Tn], gn[:, j, c:c + 1], rstd[:, 0:Tn], OP.mult, OP.mult,
                    ["x%d" % c, "gn", "rstd"], ["h%d" % c])

        hkeys = ["h%d" % c for c in range(8)]

        def ffn(Tn, j, gus, dns):
            rmsnorm_to_h(Tn, j)
            for s_, sl in enumerate(gus):
                wt, wk = take(sl)
                Wv = wt[:, 0:4096].rearrange("p (kc ab c) -> p kc ab c", kc=8, ab=2)
                for jj in range(2):
                    hj = 2 * s_ + jj
                    ba, bb = nb(), nb()
                    for kc in range(8):
                        MM(ps[ba][:, 0:Tn], Wv[:, kc, 0, jj * 128:(jj + 1) * 128], h[:, kc, 0:Tn], kc == 0, kc == 7,
                           [wk, "h%d" % kc], ["ps%d" % ba])
                    for kc in range(8):
                        MM(ps[bb][:, 0:Tn], Wv[:, kc, 1, jj * 128:(jj + 1) * 128], h[:, kc, 0:Tn], kc == 0, kc == 7,
                           [wk, "h%d" % kc], ["ps%d" % bb])
                    sa_ = sa[hj % 2]
                    ACT(sa_[:, 0:Tn], ps[ba][:, 0:Tn], AF.Silu, ["ps%d" % ba], ["sa%d" % (hj % 2)])
                    TT(R[:, hj, 0:Tn], sa_[:, 0:Tn], ps[bb][:, 0:Tn], OP.mult, ["sa%d" % (hj % 2), "ps%d" % bb], ["R%d" % hj])
            for d, sl in enumerate(dns):
                wt, wk = take(sl)
                Wv = wt[:, 0:5632].rearrange("p (kc c) -> p kc c", c=256)
                for mm in range(2):
                    m = 2 * d + mm
                    bo = nb()
                    for kc in range(22):
                        MM(ps[bo][:, 0:Tn], Wv[:, kc, mm * 128:(mm + 1) * 128], R[:, kc, 0:Tn], kc == 0, kc == 21,
                           [wk, "R%d" % kc], ["ps%d" % bo])
                    STT(x[:, m, 0:Tn], ps[bo][:, 0:Tn], 0.5, x[:, m, 0:Tn], OP.mult, OP.add,
                        ["ps%d" % bo, "x%d" % m], ["x%d" % m])

        def load_x(src_rows, Tn, nrows):
            ntb = max(1, Tn // 128)
            for tb in range(ntb):
                stg = tokst(tb % 2)
                DMA(stg[0:nrows, :], src_rows(tb), [], tokst_keys(tb % 2) + (["tilestart"] if tb == 0 else []), "xin%d" % (tb % 2))
                for half in range(2):
                    b_ = nb()
                    for k in range(4):
                        c = 4 * half + k
                        TR(ps[b_][:, k * 128:k * 128 + nrows], stg[0:nrows, c * 128:(c + 1) * 128], ident[0:nrows, 0:nrows],
                           tokst_keys(tb % 2) + ["ident"], ["ps%d" % b_])
                    CP(x[:, 4 * half:4 * half + 4, tb * 128:tb * 128 + nrows],
                       ps[b_][:].rearrange("p (a b) -> p a b", a=4)[:, :, 0:nrows],
                       ["ps%d" % b_], ["x%d" % c for c in range(4 * half, 4 * half + 4)], eng=("act" if half else "dve"))

        def store_y(dst_rows, Tn, nrows):
            ntb = max(1, Tn // 128)
            for tb in range(ntb):
                stg = tokst(tb % 2)
                for half in range(2):
                    b_ = nb()
                    for k in range(4):
                        c = 4 * half + k
                        TR(ps[b_][0:nrows, k * 128:(k + 1) * 128], x[:, c, tb * 128:tb * 128 + nrows], ident[:],
                           ["x%d" % c, "ident"], ["ps%d" % b_])
                    CP(stg[0:nrows, half * 512:(half + 1) * 512], ps[b_][0:nrows, :], ["ps%d" % b_],
                       tokst_keys(tb % 2), eng=("act" if half else "dve"))
                DMA(dst_rows(tb), stg[0:nrows, :], tokst_keys(tb % 2), [], "yout%d" % (tb % 2))

        kvst_rr = {"i": 0}

        pend = []

        def flush(keep):
            while len(pend) > keep:
                pend.pop(0)[0]()

        def next_kvst():
            i = kvst_rr["i"] % 6
            kvst_rr["i"] += 1
            while any(k == "kvst%d" % i for (_, k) in pend):
                pend.pop(0)[0]()
            return kvst[i], "kvst%d" % i

        def proj_tok(wt, wk, tok_ap_fn, M):
            b_ = nb()
            Wv = wt[:, 0:4096].rearrange("p (kc c) -> p kc c", c=512)
            for kc in range(8):
                MM(ps[b_][0:M, :], tok_ap_fn(kc), Wv[:, kc, :], kc == 0, kc == 7, [wk, "h%d" % kc], ["ps%d" % b_])
            return b_

        def head_norm(b_, M, gain_ap, gain_key):
            i = b_ % 2
            ACT(sq[i][0:M, :], ps[b_][0:M, :], AF.Square, ["ps%d" % b_], ["sq%d" % i])
            col = 8 * i
            RED(small[0:M, col:col + 8], sq[i][0:M, :].rearrange("p (a b) -> p a b", b=64), ["sq%d" % i], ["small%d" % i])
            ACT(small[0:M, col:col + 8], small[0:M, col:col + 8], AF.Sqrt, ["small%d" % i], ["small%d" % i], bias=EPS, scale=1.0 / 64)
            RCP(small[0:M, col:col + 8], small[0:M, col:col + 8], ["small%d" % i], ["small%d" % i])
            stg, sk = next_kvst()
            TT(stg[0:M, :].rearrange("p (a b) -> p a b", b=64), ps[b_][0:M, :].rearrange("p (a b) -> p a b", b=64),
               small[0:M, col:col + 8].unsqueeze(2).to_broadcast([M, 8, 64]), OP.mult, ["ps%d" % b_, "small%d" % i], [sk])
            if gain_ap is not None:
                TT(stg[0:M, :], stg[0:M, :], gain_ap, OP.mult, [sk, gain_key], [sk])
            return stg, sk

        def transpose_to_feat(stg, sk, M, out_ap3, out_keys, gain3=None, eng="act"):
            b_ = nb()
            for cc in range(4):
                TR(ps[b_][:, cc * 128:cc * 128 + M], stg[0:M, cc * 128:(cc + 1) * 128], ident[0:M, 0:M], [sk, "ident"], ["ps%d" % b_])
            src = ps[b_][:].rearrange("p (a b) -> p a b", a=4)[:, :, 0:M]
            if gain3 is not None:
                TT(out_ap3, src, gain3.unsqueeze(2).to_broadcast([128, 4, M]), OP.mult, ["ps%d" % b_, "gqT"], out_keys)
            else:
                CP(out_ap3, src, ["ps%d" % b_], out_keys, eng=eng)

        def prompt_tile(s, i):
            tok0 = i * T
            bank_state["set"] = list(range(8))
            load_x(lambda tb: xp[s, tok0 + tb * 128: tok0 + (tb + 1) * 128, :], T, 128)
            if STAGE == 1:
                return
            ffn(T, 0, f1gu, f1dn)
            if STAGE == 2:
                store_y(lambda tb: yp[s, tok0 + tb * 128: tok0 + (tb + 1) * 128, :], T, 128)
                return
            rmsnorm_to_h(T, 1)
            wt, wk = take(sl_u)
            Wv = wt[:, 0:4096].rearrange("p (kc c) -> p kc c", c=512)
            if i == 0:
                MSET(ub[:, :, 0:16], 0.0, ["ub%d" % g for g in range(4)])
            for g in range(4):
                b_ = nb()
                for kc in range(8):
                    MM(ps[b_][:], Wv[:, kc, g * 128:(g + 1) * 128], h[:, kc, :], kc == 0, kc == 7, [wk, "h%d" % kc], ["ps%d" % b_])
                CP(ub[:, g, 16:528], ps[b_][:], ["ps%d" % b_], ["ub%d" % g], eng="act")
            for g in range(4):
                ug = ub[:, g, :]
                uk = "ub%d" % g
                TT(la[:, 2:528], ug[:, 2:528], ug[:, 1:527], OP.add, [uk], ["la"])
                cur, ck = la, "la"
                if g >= 1:
                    TT(lb[:, 4:528], la[:, 4:528], la[:, 2:526], OP.add, ["la"], ["lb"])
                    cur, ck = lb, "lb"
                if g >= 2:
                    TT(la[:, 8:528], lb[:, 8:528], lb[:, 4:524], OP.add, ["lb"], ["la"])
                    cur, ck = la, "la"
                if g >= 3:
                    TT(lb[:, 16:528], la[:, 16:528], la[:, 8:520], OP.add, ["la"], ["lb"])
                    cur, ck = lb, "lb"
                w = POOL_W[g]
                STT(pl[:, g, :], cur[:, 16:528], 1.0 / w, ug[:, 16:528], OP.mult, OP.subtract, [ck, uk], ["pl%d" % g])
                if i == 0:
                    TT(cur[:, 16:32], cur[:, 16:32], invc[:, g * 16:(g + 1) * 16], OP.mult, [ck, "invc"], [ck])
                    TT(pl[:, g, 0:16], cur[:, 16:32], ug[:, 16:32], OP.subtract, [ck, uk], ["pl%d" % g])
            if i == NT - 1:
                b_ = nb()
                for g in range(4):
                    TR(ps[b_][0:16, g * 128:(g + 1) * 128], ub[:, g, 512:528], ident[:], ["ub%d" % g, "ident"], ["ps%d" % b_])
                CP(pst[:], ps[b_][0:16, :], ["ps%d" % b_], ["pst"], eng="act")
                DMA(poolp[s, :, :], pst[1:16, :], ["pst"], [], "pout")
            for g in range(4):
                CP(ub[:, g, 0:16], ub[:, g, 512:528], ["ub%d" % g], ["ub%d" % g])
            for g in range(4):
                b_ = nb()
                MM(ps[b_][:], pwb[:, g, :], pl[:, g, :], True, True, ["pwb", "pl%d" % g], ["ps%d" % b_])
                TS(py(g), ps[b_][:], psc[:, g:g + 1], None, OP.mult, None, ["ps%d" % b_, "psc"], ["R%d" % (16 + g)])
            for g in range(3):
                slot_kt = (i % 2) if g < 2 else i
                ktoff = slot_kt * T
                wt, wk = take(sl_q[g])
                for tb in range(4):
                    b_ = proj_tok(wt, wk, lambda kc, tb=tb: h[:, kc, tb * 128:(tb + 1) * 128], 128)
                    stg, sk = head_norm(b_, 128, None, None)
                    pend.append((lambda stg=stg, sk=sk, g=g, tb=tb: transpose_to_feat(
                        stg, sk, 128, R[:, 4 * g:4 * g + 4, tb * 128:(tb + 1) * 128],
                        ["R%d" % c for c in range(4 * g, 4 * g + 4)], gain3=gqT[:, 4 * g:4 * g + 4]), sk))
                    flush(2)
                wt, wk = take(sl_k[g])
                for tb in range(4):
                    b_ = proj_tok(wt, wk, lambda kc, tb=tb: h[:, kc, tb * 128:(tb + 1) * 128], 128)
                    stg, sk = head_norm(b_, 128, gk[:, g * 512:(g + 1) * 512], "gk")
                    Wg = min((128, 512, 2048)[g], SEQ)
                    keep = (tok0 + tb * 128) >= SEQ - Wg
                    if keep:
                        row0 = tok0 + tb * 128 - (SEQ - Wg)
                        DMA(kvp[g][s, row0:row0 + 128, 0:512], stg[:, :], [sk], [], "ko" + sk)
                    pend.append((lambda stg=stg, sk=sk, g=g, tb=tb, ktoff=ktoff, slot_kt=slot_kt: transpose_to_feat(
                        stg, sk, 128, KT[g][:, :, ktoff + tb * 128: ktoff + (tb + 1) * 128],
                        ["KT%d_%d_%d" % (g, slot_kt, tb)], eng=("act" if tb % 2 else "dve")), sk))
                    flush(2)
                wt, wk = take(sl_v[g])
                if g == 0:
                    for w in range(4):
                        b_ = proj_tok(wt, wk, lambda kc, w=w: h[:, kc, w * 128:(w + 1) * 128], 128)
                        slotv = (4 * i + w) % 8
                        CP(VH[:, slotv, :], ps[b_][:], ["ps%d" % b_], ["V0_%d" % slotv])
                        if i == NT - 1 and w == 3:
                            stg, sk = next_kvst()
                            CP(stg[:, :], ps[b_][:], ["ps%d" % b_], [sk], eng="act")
                            DMA(kvp[0][s, 0:128, 512:1024], stg[:, :], [sk], [], "ko" + sk)
                        flush(2)
                elif g == 1:
                    for r4 in range(4):
                        b_ = proj_tok(wt, wk, lambda kc, r4=r4: h[:, kc, r4:T:4], 128)
                        slotv = 8 + 2 * r4 + (i % 2)
                        CP(VH[:, slotv, :], ps[b_][:], ["ps%d" % b_], ["V1_%d" % slotv])
                        if i == NT - 1:
                            stg, sk = next_kvst()
                            CP(stg[:, :], ps[b_][:], ["ps%d" % b_], [sk], eng="act")
                            DMA(kvp[1][s, r4:512:4, 512:1024], stg[:, :], [sk], [], "ko" + sk)
                        flush(2)
                else:
                    for r16 in range(16):
                        b_ = nb()
                        Wv = wt[:, 0:4096].rearrange("p (kc c) -> p kc c", c=512)
                        po = 32 * i
                        for kc in range(8):
                            MM(ps[b_][po:po + 32, :], h[:, kc, r16:T:16], Wv[:, kc, :], kc == 0, kc == 7,
                               [wk, "h%d" % kc], ["ps%d" % b_], tile_position=(0, po))
                        CP(VH[po:po + 32, 16 + r16, :], ps[b_][po:po + 32, :], ["ps%d" % b_], ["V2_%d" % r16])
                        stg, sk = next_kvst()
                        CP(stg[po:po + 32, :], ps[b_][po:po + 32, :], ["ps%d" % b_], [sk], eng="act")
                        DMA(kvp[2][s, tok0 + r16: tok0 + T: 16, 512:1024], stg[po:po + 32, :], [sk], [], "ko" + sk)
                        flush(1 if r16 < 2 else 0)
            flush(0)
            steps = []
            for hp in range(4):
                bo, bd = (4, 5) if hp % 2 == 0 else (6, 7)
                pair_steps = []
                for hh in range(2):
                    hd = 2 * hp + hh
                    p0 = 64 * hh
                    for prev in (0, 1):
                        qc = hp
                        w0 = 1 if (prev and i == 0) else 0
                        qk, pvl = [], []
                        for w in range(w0, 4):
                            U = 4 * i + w - prev
                            kslot, ktb = (U // 4) % 2, U % 4
                            qk.append((128, slice(w * 128, (w + 1) * 128),
                                       KT[0][p0:p0 + 64, hp, kslot * T + ktb * 128: kslot * T + (ktb + 1) * 128],
                                       R[p0:p0 + 64, qc, w * 128:(w + 1) * 128], ["KT0_%d_%d" % (kslot, ktb), "R%d" % qc]))
                            pvl.append((VH[:, U % 8, hd * 64:(hd + 1) * 64], "V0_%d" % (U % 8),
                                        slice(w * 128, (w + 1) * 128), slice(w * 128, (w + 1) * 128)))
                        pair_steps.append(dict(qk=qk, kp=128, c0=w0 * 128, c1=512,
                                               mask=(maskb[:, 128:256] if prev else maskb[:, 0:128]), pv=pvl, p0=p0))
                    qc = 4 + hp
                    for prev in (0, 1):
                        if prev and i == 0:
                            continue
                        kslot = (i - prev) % 2
                        qk, pvl = [], []
                        for r4 in range(4):
                            qk.append((128, slice(r4 * 128, (r4 + 1) * 128),
                                       KT[1][p0:p0 + 64, hp, kslot * T + r4: (kslot + 1) * T: 4], R[p0:p0 + 64, qc, r4:T:4],
                                       ["KT1_%d_%d" % (kslot, tb) for tb in range(4)] + ["R%d" % qc]))
                            slotv = 8 + 2 * r4 + kslot
                            pvl.append((VH[:, slotv, hd * 64:(hd + 1) * 64], "V1_%d" % slotv,
                                        slice(r4 * 128, (r4 + 1) * 128), slice(r4, T, 4)))
                        pair_steps.append(dict(qk=qk, kp=128, c0=0, c1=512,
                                               mask=(maskb[:, 128:256] if prev else maskb[:, 0:128]), pv=pvl, p0=p0))
                    qc = 8 + hp
                    nk = 32 * (i + 1)
                    qk, pvl = [], []
                    for r16 in range(16):
                        qk.append((nk, slice(r16 * 32, (r16 + 1) * 32), KT[2][p0:p0 + 64, hp, r16:(i + 1) * T:16],
                                   R[p0:p0 + 64, qc, r16:T:16],
                                   ["KT2_%d_%d" % (ii, tb) for ii in range(i + 1) for tb in range(4)] + ["R%d" % qc]))
                        pvl.append((VH[0:nk, 16 + r16, hd * 64:(hd + 1) * 64], "V2_%d" % r16,
                                    slice(r16 * 32, (r16 + 1) * 32), slice(r16, T, 16)))
                    pair_steps.append(dict(qk=qk, kp=nk, c0=0, c1=512, mask=maskb[0:nk, 256 + 32 * i:256 + 32 * (i + 1)],
                                           pv=pvl, p0=p0))
                for j_, st_ in enumerate(pair_steps):
                    st_.update(bo=bo, bd=bd, hp=hp, first=(j_ == 0), last=(j_ == len(pair_steps) - 1))
                steps.extend(pair_steps)

            def att_front(st_, n):
                bS = n % 4
                pt, pk = Pt[n % 4], "Pt%d" % (n % 4)
                kp = st_["kp"]
                for (kparts, cols, lhsT, rhs, rd) in st_["qk"]:
                    MM(ps[bS][0:kparts, cols], lhsT, rhs, True, True, rd, ["ps%d" % bS])
                c0, c1, maskap = st_["c0"], st_["c1"], st_["mask"]
                mb = maskap.shape[-1]
                ACT(pt[0:kp, c0:c1], ps[bS][0:kp, c0:c1], AF.Exp, ["ps%d" % bS], [pk], scale=0.125)
                TT(pt[0:kp, c0:c1].rearrange("p (a b) -> p a b", b=mb), pt[0:kp, c0:c1].rearrange("p (a b) -> p a b", b=mb),
                   maskap.unsqueeze(1).to_broadcast([kp, (c1 - c0) // mb, mb]), OP.mult, [pk, "maskb"], [pk])

            def att_back(st_, n):
                pt, pk = Pt[n % 4], "Pt%d" % (n % 4)
                kp, bo, bd, p0 = st_["kp"], st_["bo"], st_["bd"], st_["p0"]
                if st_["first"]:
                    MSET(ps[bo][:], 0.0, ["ps%d" % bo])
                    MSET(ps[bd][:], 0.0, ["ps%d" % bd])
                for (vap, vkey, pc, oc) in st_["pv"]:
                    MM(ps[bo][p0:p0 + 64, oc], vap, pt[0:kp, pc], False, False, [vkey, pk, "ps%d" % bo], ["ps%d" % bo],
                       skip_group_check=True, tile_position=(0, p0))
                    MM(ps[bd][p0:p0 + 64, oc], onesb[0:kp, 0:64], pt[0:kp, pc], False, False,
                       ["onesb", pk, "ps%d" % bd], ["ps%d" % bd], skip_group_check=True, tile_position=(0, p0))
                if st_["last"]:
                    RCP(gt[2][:], ps[bd][:], ["ps%d" % bd], ["gt2"])
                    TT(attT(st_["hp"]), ps[bo][:], gt[2][:], OP.mult, ["ps%d" % bo, "gt2"], ["R%d" % (12 + st_["hp"])])

            LOOK = 2
            for n in range(len(steps) + LOOK):
                if n < len(steps):
                    att_front(steps[n], n)
                if n - LOOK >= 0:
                    att_back(steps[n - LOOK], n - LOOK)

            bank_state["set"] = list(range(8))
            if STAGE == 5:
                store_y(lambda tb: yp[s, tok0 + tb * 128: tok0 + (tb + 1) * 128, :], T, 128)
                return
            merge_and_out(T)
            ffn(T, 2, f2gu, f2dn)
            store_y(lambda tb: yp[s, tok0 + tb * 128: tok0 + (tb + 1) * 128, :], T, 128)

        def merge_and_out(Tn):
            for m in range(8):
                wt, wk = take(sl_m[m])
                Wg = wt[:, 0:2048].rearrange("p (s kc c) -> p s kc c", s=2, kc=8)
                Wb = wt[:, 2048:3072].rearrange("p (s kc c) -> p s kc c", s=2, kc=4)
                bgp, bbp, bga, bba = nb(), nb(), nb(), nb()
                for kc in range(8):
                    MM(ps[bgp][:, 0:Tn], Wg[:, 0, kc, :], h[:, kc, 0:Tn], kc == 0, kc == 7, [wk, "h%d" % kc], ["ps%d" % bgp])
                for kc in range(4):
                    MM(ps[bbp][:, 0:Tn], Wb[:, 0, kc, :], R[:, 16 + kc, 0:Tn], kc == 0, kc == 3, [wk, "R%d" % (16 + kc)], ["ps%d" % bbp])
                for kc in range(8):
                    MM(ps[bga][:, 0:Tn], Wg[:, 1, kc, :], h[:, kc, 0:Tn], kc == 0, kc == 7, [wk, "h%d" % kc], ["ps%d" % bga])
                for kc in range(4):
                    MM(ps[bba][:, 0:Tn], Wb[:, 1, kc, :], R[:, 12 + kc, 0:Tn], kc == 0, kc == 3, [wk, "R%d" % (12 + kc)], ["ps%d" % bba])
                ACT(gt[0][:, 0:Tn], ps[bgp][:, 0:Tn], AF.Sigmoid, ["ps%d" % bgp], ["gt0"])
                ACT(gt[1][:, 0:Tn], ps[bga][:, 0:Tn], AF.Sigmoid, ["ps%d" % bga], ["gt1"])
                TT(gt[0][:, 0:Tn], gt[0][:, 0:Tn], ps[bbp][:, 0:Tn], OP.mult, ["gt0", "ps%d" % bbp], ["gt0"])
                TT(gt[1][:, 0:Tn], gt[1][:, 0:Tn], ps[bba][:, 0:Tn], OP.mult, ["gt1", "ps%d" % bba], ["gt1"])
                TT(R[:, m, 0:Tn], gt[0][:, 0:Tn], gt[1][:, 0:Tn], OP.add, ["gt0", "gt1"], ["R%d" % m])
            for o in range(2):
                wt, wk = take(sl_o[o])
                Wv = wt[:, 0:4096].rearrange("p (kc c) -> p kc c", c=512)
                for mm in range(4):
                    m = 4 * o + mm
                    b_ = nb()
                    for kc in range(8):
                        MM(ps[b_][:, 0:Tn], Wv[:, kc, mm * 128:(mm + 1) * 128], R[:, kc, 0:Tn], kc == 0, kc == 7,
                           [wk, "R%d" % kc], ["ps%d" % b_])
                    TT(x[:, m, 0:Tn], ps[b_][:, 0:Tn], x[:, m, 0:Tn], OP.add, ["ps%d" % b_, "x%d" % m], ["x%d" % m])

        def sample_tile():
            Tn = NTOKS
            S.barrier()
            bank_state["set"] = list(range(6))
            Hf = HIST[:].bitcast(F32)
            qn = Hf[:, 0:1536]
            kn = Hf[:, 1536:3072]
            vv = Hf[:, 3072:4608]
            gq = Hf[:, 4608:6144]
            tA = Hf[:, 6144:7680]
            Oacc = Hf[:, 7680:8192]
            VHb = VH[:].rearrange("p a b -> p (a b)")
            Vf = VHb[:, 0:7168].bitcast(F32)
            CKb = [Vf[:, 1024 * j: 1024 * (j + 1)] for j in range(3)]
            Dacc = Vf[:, 3072:3080]
            pself = Vf[:, 3088:3112]
            p8 = Vf[:, 3120:3128]
            sred = Vf[:, 3136:3160]
            zf = Vf[:, 3168:3295]
            sh = Vf[:, 3296:3488]
            vld = Vf[:, 3488:3491]
            m0 = Vf[:, 3492:3496]
            qnb = VHb[:, 7168:8704]
            tB = VHb[:, 8704:9216]
            p8b = VHb[:, 9216:9224]
            zb = VHb[:, 9232:9359]
            identb = VHb[:, 9360:9488]
            ue = ub[:].rearrange("p a b -> p (a b)")[:, 0:1280].rearrange("p (g b t) -> p g b t", g=4, b=16)
            l1 = la[:, 0:320].rearrange("p (b t) -> p b t", t=20)
            l2 = lb[:, 0:320].rearrange("p (b t) -> p b t", t=20)
            ust = gt[2]
            stS = sq[0]
            DMA(gq, bass.AP(Wd["q_norm"].tensor, 0, [[0, 128], [1, 1536]]), [], ["gq"], "s0")
            DMA(zf, c_z, [], ["zf"], "s1")
            DMA(sh[0:64, :], c_shift, [], ["sh"], "s2")
            DMA(vld[0:64, :], c_valid, [], ["vld"], "s3")
            DMA(m0, c_m0, [], ["m0"], "s4")
            CP(zb, zf, ["zf"], ["zb"])
            CP(identb, ident[:], ["ident"], ["identb"])
            load_x(lambda tb: xs[:, :], Tn, Tn)
            ffn(Tn, 0, f1gu, f1dn)
            rmsnorm_to_h(Tn, 1)
            wt, wk = take(sl_u)
            Wv = wt[:, 0:4096].rearrange("p (kc c) -> p kc c", c=512)
            nblk = (NSB + 7) // 8
            for blk in range(nblk):
                nb_ = min(8, NSB - 8 * blk)
                nr = nb_ * 15
                DMA(stS[0:nr, 0:512], spool[8 * blk:8 * blk + nb_, :, :].rearrange("b r c -> (b r) c"), [], ["sq0"], "s5")
                b_ = nb()
                for g in range(4):
                    TR(ps[b_][:, g * 128:g * 128 + nr], stS[0:nr, g * 128:(g + 1) * 128], ident[0:nr, 0:nr], ["sq0", "ident"], ["ps%d" % b_])
                for g in range(4):
                    CP(ue[:, g, 8 * blk:8 * blk + nb_, 1:16], ps[b_][:, g * 128:g * 128 + nr].rearrange("p (b r) -> p b r", r=15),
                       ["ps%d" % b_], ["ub%d" % g], eng=("act" if g % 2 else "dve"))
            for g in range(4):
                b_ = nb()
                for kc in range(8):
                    MM(ps[b_][:, 0:Tn], Wv[:, kc, g * 128:(g + 1) * 128], h[:, kc, 0:Tn], kc == 0, kc == 7, [wk, "h%d" % kc], ["ps%d" % b_])
                CP(ue[:, g, 0:NSB, 16:20], ps[b_][:, 0:Tn].rearrange("p (b t) -> p b t", t=4), ["ps%d" % b_], ["ub%d" % g], eng="act")
            for g in range(4):
                ug = ue[:, g, 0:NSB, :]
                uk = "ub%d" % g
                A, Bf = l1[:, 0:NSB, :], l2[:, 0:NSB, :]
                TT(A[:, :, 2:20], ug[:, :, 2:20], ug[:, :, 1:19], OP.add, [uk], ["la"])
                cur, ck = A, "la"
                if g >= 1:
                    TT(Bf[:, :, 4:20], A[:, :, 4:20], A[:, :, 2:18], OP.add, ["la"], ["lb"])
                    cur, ck = Bf, "lb"
                if g >= 2:
                    TT(A[:, :, 8:20], Bf[:, :, 8:20], Bf[:, :, 4:16], OP.add, ["lb"], ["la"])
                    cur, ck = A, "la"
                if g >= 3:
                    TT(Bf[:, :, 16:20], A[:, :, 16:20], A[:, :, 8:12], OP.add, ["la"], ["lb"])
                    cur, ck = Bf, "lb"
                STT(pl[:, g, 0:Tn].rearrange("p (b t) -> p b t", t=4), cur[:, :, 16:20], 1.0 / POOL_W[g], ug[:, :, 16:20],
                    OP.mult, OP.subtract, [ck, uk], ["pl%d" % g])
            b_ = nb()
            for g in range(4):
                CP(tA[:, g * 64:g * 64 + Tn].rearrange("p (b t) -> p b t", t=4), ue[:, g, 0:NSB, 16:20], ["ub%d" % g], ["tA"])
            for g in range(4):
                TR(ps[b_][0:Tn, g * 128:(g + 1) * 128], tA[:, g * 64:g * 64 + Tn], ident[:], ["tA", "ident"], ["ps%d" % b_])
            CP(ust[0:Tn, :], ps[b_][0:Tn, :], ["ps%d" % b_], ["gt2"], eng="act")
            DMA(pools[:, 11:15, :], ust[0:Tn, :], ["gt2"], [], "s6")
            for g in range(4):
                b_ = nb()
                MM(ps[b_][:, 0:Tn], pwb[:, g, :], pl[:, g, 0:Tn], True, True, ["pwb", "pl%d" % g], ["ps%d" % b_])
                TS(R[:, 16 + g, 0:Tn], ps[b_][:, 0:Tn], psc[:, g:g + 1], None, OP.mult, None, ["ps%d" % b_, "psc"], ["R%d" % (16 + g)])
            for g in range(3):
                Wn = (128, 512, 2048)[g]
                wt, wk = take(sl_q[g])
                b_ = proj_tok(wt, wk, lambda kc: h[:, kc, 0:Tn], Tn)
                stg, sk = head_norm(b_, Tn, gq[0:Tn, g * 512:(g + 1) * 512], "gq")
                CP(qn[0:Tn, g * 512:(g + 1) * 512], stg[0:Tn, :], [sk], ["qn"])
                wt, wk = take(sl_k[g])
                b_ = proj_tok(wt, wk, lambda kc: h[:, kc, 0:Tn], Tn)
                stg, sk = head_norm(b_, Tn, gk[0:Tn, g * 512:(g + 1) * 512], "gk")
                CP(kn[0:Tn, g * 512:(g + 1) * 512], stg[0:Tn, :], [sk], ["kn"])
                DMA(kvs[g][:, Wn - 4:Wn, 0:512], stg[0:Tn, :], [sk], [], "ko" + sk)
                wt, wk = take(sl_v[g])
                b_ = proj_tok(wt, wk, lambda kc: h[:, kc, 0:Tn], Tn)
                CP(vv[0:Tn, g * 512:(g + 1) * 512], ps[b_][0:Tn, :], ["ps%d" % b_], ["vv"], eng="act")
                DMA(kvs[g][:, Wn - 4:Wn, 512:1024], vv[0:Tn, g * 512:(g + 1) * 512], ["vv"], [], "s7")
            CP(qnb[0:Tn, :], qn[0:Tn, :], ["qn"], ["qnb"])
            TT(tA[0:Tn, :], qn[0:Tn, :], kn[0:Tn, :], OP.mult, ["qn", "kn"], ["tA"])
            RED(sred[0:Tn, :], tA[0:Tn, :].rearrange("p (a b) -> p a b", b=64), ["tA"], ["sred"])
            ACT(pself[0:Tn, :], sred[0:Tn, :], AF.Exp, ["sred"], ["pself"], scale=0.125)
            TT(Dacc[0:Tn, :], pself[0:Tn, 0:8], pself[0:Tn, 8:16], OP.add, ["pself"], ["Dacc"])
            TT(Dacc[0:Tn, :], Dacc[0:Tn, :], pself[0:Tn, 16:24], OP.add, ["pself", "Dacc"], ["Dacc"])
            TT(tA[0:Tn, :].rearrange("p (a b) -> p a b", b=64), vv[0:Tn, :].rearrange("p (a b) -> p a b", b=64),
               pself[0:Tn, :].unsqueeze(2).to_broadcast([Tn, 24, 64]), OP.mult, ["vv", "pself", "tA"], ["tA"])
            TT(Oacc[0:Tn, :], tA[0:Tn, 0:512], tA[0:Tn, 512:1024], OP.add, ["tA"], ["Oacc"])
            TT(Oacc[0:Tn, :], Oacc[0:Tn, :], tA[0:Tn, 1024:1536], OP.add, ["tA", "Oacc"], ["Oacc"])
            for d in range(1, 4):
                bk, bv = nb(), nb()
                MM(ps[bk][0:Tn, :], sh[0:Tn, (d - 1) * 64:(d - 1) * 64 + Tn], kn[0:Tn, 0:512], True, True, ["sh", "kn"], ["ps%d" % bk])
                MM(ps[bv][0:Tn, :], sh[0:Tn, (d - 1) * 64:(d - 1) * 64 + Tn], vv[0:Tn, 0:512], True, True, ["sh", "vv"], ["ps%d" % bv])
                TT(tA[0:Tn, 0:512], qn[0:Tn, 0:512], ps[bk][0:Tn, :], OP.mult, ["qn", "ps%d" % bk], ["tA"])
                RED(sred[0:Tn, 0:8], tA[0:Tn, 0:512].rearrange("p (a b) -> p a b", b=64), ["tA"], ["sred"])
                ACT(p8[0:Tn, :], sred[0:Tn, 0:8], AF.Exp, ["sred"], ["p8"], scale=0.125)
                TS(p8[0:Tn, :], p8[0:Tn, :], vld[0:Tn, d - 1:d], None, OP.mult, None, ["p8", "vld"], ["p8"])
                TT(Dacc[0:Tn, :], Dacc[0:Tn, :], p8[0:Tn, :], OP.add, ["Dacc", "p8"], ["Dacc"])
                TT(tA[0:Tn, 0:512].rearrange("p (a b) -> p a b", b=64), ps[bv][0:Tn, :].rearrange("p (a b) -> p a b", b=64),
                   p8[0:Tn, :].unsqueeze(2).to_broadcast([Tn, 8, 64]), OP.mult, ["ps%d" % bv, "p8", "tA"], ["tA"])
                TT(Oacc[0:Tn, :], Oacc[0:Tn, :], tA[0:Tn, 0:512], OP.add, ["tA", "Oacc"], ["Oacc"])
            BO, BD = 6, 7
            MSET(ps[BO][:], 0.0, ["ps6"])
            MSET(ps[BD][:], 0.0, ["ps7"])
            VF2 = VHb[:, 9600:12800].bitcast(F32)
            tA3 = [VF2[:, 512 * j:512 * (j + 1)] for j in range(3)]
            sred3 = [VF2[:, 1536 + 8 * j:1544 + 8 * j] for j in range(3)]
            p83 = [VF2[:, 1568 + 8 * j:1576 + 8 * j] for j in range(3)]
            tB3 = [VHb[:, 12800 + 512 * j:12800 + 512 * (j + 1)] for j in range(3)]
            p8b3 = [VHb[:, 14336 + 8 * j:14344 + 8 * j] for j in range(3)]
            passes = []
            ckr = {"i": 0}
            cur_ck = None
            for b in range(NSB):
                for g in range(3):
                    dil = (1, 4, 16)[g]
                    for t in range(4):
                        load = None
                        if not (g == 0 and t > 0):
                            j = ckr["i"] % 3
                            ckr["i"] += 1
                            cur_ck = (CKb[j], "CK%d" % j, "ck%d" % j)
                            load = cache[g][b, t:t + 127 * dil + 1:dil, :] if g > 0 else cache[g][b, 0:128, :]
                        passes.append((4 * b + t, g, t, cur_ck, load))

            def s_front(n):
                tok, g, t, (ck_ap, ck_key, ck_sem), load = passes[n]
                if load is not None:
                    DMA(ck_ap, load, [], [ck_key], ck_sem)
                bq = n % 6
                MM(ps[bq][:], identb[0:Tn, tok:tok + 1].to_broadcast([Tn, 128]), qnb[0:Tn, g * 512:(g + 1) * 512], True, True,
                   ["identb", "qnb"], ["ps%d" % bq])

            def s_back(n):
                tok, g, t, (ck_ap, ck_key, ck_sem), load = passes[n]
                bq = n % 6
                j = n % 3
                TT(tA3[j], ck_ap[:, 0:512], ps[bq][:], OP.mult, [ck_key, "ps%d" % bq], ["tA3_%d" % j])
                RED(sred3[j], tA3[j].rearrange("p (a b) -> p a b", b=64), ["tA3_%d" % j], ["sred3_%d" % j])
                ACT(p83[j], sred3[j], AF.Exp, ["sred3_%d" % j], ["p83_%d" % j], scale=0.125)
                if g == 0:
                    TS(p83[j], p83[j], m0[:, t:t + 1], None, OP.mult, None, ["p83_%d" % j, "m0"], ["p83_%d" % j])
                CP(p8b3[j], p83[j], ["p83_%d" % j], ["p8b3_%d" % j], eng="act")
                TT(tB3[j].rearrange("p (a b) -> p a b", b=64), ck_ap[:, 512:1024].rearrange("p (a b) -> p a b", b=64),
                   p83[j].unsqueeze(2).to_broadcast([128, 8, 64]), OP.mult, [ck_key, "p83_%d" % j], ["tB3_%d" % j], eng="pool")
                MM(ps[BO][0:Tn, :], zb[:, 63 - tok:63 - tok + Tn], tB3[j], False, False, ["zb", "tB3_%d" % j, "ps6"], ["ps6"],
                   skip_group_check=True)
                MM(ps[BD][0:Tn, 0:8], zb[:, 63 - tok:63 - tok + Tn], p8b3[j], False, False, ["zb", "p8b3_%d" % j, "ps7"], ["ps7"],
                   skip_group_check=True)

            SL = 2
            for n in range(len(passes) + SL):
                if n < len(passes):
                    s_front(n)
                if n - SL >= 0:
                    s_back(n - SL)
            TT(Oacc[0:Tn, :], Oacc[0:Tn, :], ps[BO][0:Tn, :], OP.add, ["Oacc", "ps6"], ["Oacc"])
            TT(Dacc[0:Tn, :], Dacc[0:Tn, :], ps[BD][0:Tn, 0:8], OP.add, ["Dacc", "ps7"], ["Dacc"])
            RCP(Dacc[0:Tn, :], Dacc[0:Tn, :], ["Dacc"], ["Dacc"])
            TT(Oacc[0:Tn, :].rearrange("p (a b) -> p a b", b=64), Oacc[0:Tn, :].rearrange("p (a b) -> p a b", b=64),
               Dacc[0:Tn, :].unsqueeze(2).to_broadcast([Tn, 8, 64]), OP.mult, ["Oacc", "Dacc"], ["Oacc"])
            b_ = nb()
            for cc in range(4):
                TR(ps[b_][:, cc * 128:cc * 128 + Tn], Oacc[0:Tn, cc * 128:(cc + 1) * 128], ident[0:Tn, 0:Tn], ["Oacc", "ident"], ["ps%d" % b_])
            CP(R[:, 12:16, 0:Tn], ps[b_][:].rearrange("p (a b) -> p a b", a=4)[:, :, 0:Tn], ["ps%d" % b_],
               ["R%d" % c for c in range(12, 16)])
            bank_state["set"] = list(range(8))
            merge_and_out(Tn)
            ffn(Tn, 2, f2gu, f2dn)
            store_y(lambda tb: ys[:, :], Tn, Tn)

        n_pt = NSEQ * NT
        per_tile = -(-len(bulk) // max(1, n_pt - 1))
        for s in range(NSEQ):
            for i in range(NT):
                prompt_tile(s, i)
                if s * NT + i >= 0:
                    issue_bulk(per_tile if (s * NT + i) < n_pt - 1 else len(bulk))
        if do_sample:
            issue_bulk(len(bulk))
            sample_tile()
        S.emit(nc)
    return nc


_PROG = {}


def kernel(x_prompt, x_sample, cache_kv_w128, cache_kv_w512, cache_kv_w2048, state_pool,
           ffn1_norm, ffn1_w_gu, ffn1_w_down, mix_norm, w_in, q_norm, k_norm, pool_w,
           pool_scale, w_branch_pool, w_branch_att, w_out, ffn2_norm, ffn2_w_gu, ffn2_w_down):
    f = lambda a: np.ascontiguousarray(np.asarray(a, dtype=np.float32))
    B, SEQ, D = x_prompt.shape
    DB = x_sample.shape[0]
    NSEQ, NSB = B // NCORES, DB // NCORES
    if "nc" not in _PROG:
        _PROG["nc"] = build_program(NSEQ=NSEQ, NSB=NSB, SEQ=SEQ)
    nc = _PROG["nc"]
    shared = {
        "ffn1_norm": f(ffn1_norm[0]), "ffn1_w_gu": f(ffn1_w_gu[0]), "ffn1_w_down": f(ffn1_w_down[0]),
        "mix_norm": f(mix_norm[0]), "w_in": f(w_in[0]), "q_norm": f(q_norm[0]).reshape(1536),
        "k_norm": f(k_norm[0]).reshape(1536), "pool_w": f(pool_w[0]), "pool_scale": f(pool_scale[0]),
        "w_branch_pool": f(w_branch_pool[0]), "w_branch_att": f(w_branch_att[0]), "w_out": f(w_out[0]),
        "ffn2_norm": f(ffn2_norm[0]), "ffn2_w_gu": f(ffn2_w_gu[0]), "ffn2_w_down": f(ffn2_w_down[0]),
    }
    shared.update(_const_tables())
    in_maps = []
    for c in range(NCORES):
        m = dict(shared)
        m["xp"] = f(x_prompt[c * NSEQ:(c + 1) * NSEQ])
        m["xs"] = f(x_sample[c * NSB:(c + 1) * NSB]).reshape(NSB * 4, D)
        m["c128"] = f(cache_kv_w128[0, c * NSB:(c + 1) * NSB]).reshape(NSB, 128, 1024)
        m["c512"] = f(cache_kv_w512[0, c * NSB:(c + 1) * NSB]).reshape(NSB, 512, 1024)
        m["c2048"] = f(cache_kv_w2048[0, c * NSB:(c + 1) * NSB]).reshape(NSB, 2048, 1024)
        m["spool"] = f(state_pool[0, c * NSB:(c + 1) * NSB])
        in_maps.append(m)
    res = run_bass_kernel_spmd(nc, in_maps, core_ids=list(range(NCORES)))
    r = res.results
    cat = lambda k: np.concatenate([np.asarray(r[c][k]) for c in range(NCORES)], axis=0)
    y_prompt = cat("yp")
    y_sample = cat("ys").reshape(DB, 4, D)
    kv128p = cat("kv128p").reshape(1, B, 128, 2, 8, 64)
    kv512p = cat("kv512p").reshape(1, B, 512, 2, 8, 64)
    kv2048p = cat("kv2048p").reshape(1, B, SEQ, 2, 8, 64)
    poolp = cat("poolp").reshape(1, B, 15, 512)
    kv128s = cat("kv128s").reshape(1, DB, 128, 2, 8, 64)
    kv512s = cat("kv512s").reshape(1, DB, 512, 2, 8, 64)
    kv2048s = cat("kv2048s").reshape(1, DB, 2048, 2, 8, 64)
    pools_ = cat("pools").reshape(1, DB, 15, 512)
    return (y_prompt, y_sample, kv128p, kv512p, kv2048p, poolp, kv128s, kv512s, kv2048s, pools_)
```

```python
import contextlib
import numpy as np
import concourse.bass as bass
import concourse.mybir as mybir
from concourse.bass_utils import run_bass_kernel_spmd

F32 = mybir.dt.float32
BF = mybir.dt.bfloat16
AF = mybir.ActivationFunctionType
OP = mybir.AluOpType
AX = mybir.AxisListType

NCORES = 8
EPS = 1e-6
SLOT = 5632
NSLOT = 3
POOL_W = (2, 4, 8, 16)


class _Op:
    __slots__ = ("eng", "fn", "deps", "sig", "dma_sem", "val", "idx", "burst")

    def __init__(self, eng, fn, dma_sem, burst=1):
        self.eng, self.fn, self.dma_sem, self.burst = eng, fn, dma_sem, burst
        self.deps, self.sig, self.val, self.idx = set(), False, None, None


class Sched:
    ENGS = ("pe", "act", "dve", "pool", "sp")

    def __init__(self):
        self.ops, self.last_w, self.readers = [], {}, {}
        self.bar = set()
        self.last_eng = {}
        self.last_dma = {}

    def op(self, eng, fn, reads=(), writes=(), dma_sem=None, burst=1):
        o = _Op(eng, fn, dma_sem, burst)
        o.idx = len(self.ops)
        deps = set(self.bar)
        for k in reads:
            w = self.last_w.get(k)
            if w is not None:
                deps.add(w)
            if k.startswith("ps"):
                deps.update(r for r in self.readers.get(k, ()) if self.ops[r].eng != eng)
        for k in writes:
            w = self.last_w.get(k)
            if w is not None:
                deps.add(w)
            deps.update(self.readers.get(k, ()))
        for k in reads:
            self.readers.setdefault(k, []).append(o.idx)
        for k in writes:
            self.last_w[k] = o.idx
            self.readers[k] = []
        if eng == "pe" and dma_sem is None:
            deps = {d for d in deps if not (self.ops[d].eng == "pe" and self.ops[d].dma_sem is None)}
        o.deps = deps
        self.ops.append(o)
        if dma_sem is None:
            self.last_eng[eng] = o.idx
        else:
            self.last_dma[dma_sem] = o.idx
        return o

    def barrier(self):
        self.bar = set(self.last_eng.values()) | set(self.last_dma.values())

    def emit(self, nc):
        ops = self.ops
        for o in ops:
            for d in o.deps:
                ops[d].sig = True
        cnt = {e: 0 for e in self.ENGS}
        dma_names, dma_cnt, dma_eng = [], {}, {}
        for o in ops:
            if o.dma_sem is not None:
                if o.dma_sem not in dma_cnt:
                    dma_cnt[o.dma_sem] = 0
                    dma_names.append(o.dma_sem)
                    dma_eng[o.dma_sem] = o.eng
                assert dma_eng[o.dma_sem] == o.eng
                dma_cnt[o.dma_sem] += 16
                q = 16 * o.burst
                o.val = -(-dma_cnt[o.dma_sem] // q) * q
            elif o.sig:
                cnt[o.eng] += 1
                o.val = cnt[o.eng]
        with contextlib.ExitStack() as st:
            sems = {}
            for e in self.ENGS:
                sems[("eng", e)] = st.enter_context(nc.semaphore("s_" + e))
            for n in dma_names:
                sems[("dma", n)] = st.enter_context(nc.semaphore("d_" + str(n)))
            block = st.enter_context(nc.Block())

            def semkey(o):
                return ("dma", o.dma_sem) if o.dma_sem is not None else ("eng", o.eng)

            def run(engname, engobj):
                known = {}
                for o in ops:
                    if o.eng != engname:
                        continue
                    need = {}
                    for d in o.deps:
                        p = ops[d]
                        k = semkey(p)
                        if p.val > need.get(k, 0):
                            need[k] = p.val
                    for k, v in need.items():
                        if known.get(k, 0) < v:
                            engobj.wait_ge(sems[k], v)
                            known[k] = v
                    ins = o.fn(engobj)
                    if o.dma_sem is not None:
                        ins.then_inc(sems[("dma", o.dma_sem)], 16)
                    elif o.sig:
                        ins.then_inc(sems[("eng", engname)], 1)
                for n in dma_names:
                    if dma_eng[n] == engname and known.get(("dma", n), 0) < dma_cnt[n]:
                        engobj.wait_ge(sems[("dma", n)], dma_cnt[n])

            @block.tensor
            def _(e):
                run("pe", e)

            @block.scalar
            def _(e):
                run("act", e)

            @block.vector
            def _(e):
                run("dve", e)

            @block.gpsimd
            def _(e):
                run("pool", e)

            @block.sync
            def _(e):
                run("sp", e)


def _const_tables():
    k = np.arange(128)[:, None]
    q = np.arange(128)[None, :]
    m = np.zeros((128, 384), np.float32)
    m[:, 0:128] = (k <= q)
    m[:, 128:256] = (k >= q)
    for i in range(4):
        c = np.arange(32)[None, :]
        blk = np.where(k < 32 * i, 1.0, np.where(k < 32 * (i + 1), ((k - 32 * i) <= c) * 1.0, 0.0))
        m[:, 256 + 32 * i:256 + 32 * (i + 1)] = blk
    invc = np.zeros((128, 4, 16), np.float32)
    for g, w in enumerate(POOL_W):
        invc[:, g, :] = 1.0 / np.minimum(w, np.arange(16) + 1)[None, :]
    sh = np.zeros((64, 3, 64), np.float32)
    vd = np.zeros((64, 3), np.float32)
    for d in range(1, 4):
        for dst in range(64):
            if dst % 4 >= d:
                sh[dst - d, d - 1, dst] = 1.0
                vd[dst, d - 1] = 1.0
    m0 = np.zeros((128, 4), np.float32)
    for t in range(4):
        m0[:, t] = (np.arange(128) >= t)
    z = np.zeros((128, 127), np.float32)
    z[:, 63] = 1.0
    return {"c_ident": np.eye(128, dtype=np.float32), "c_mask": m, "c_invc": invc.reshape(128, 64),
            "c_shift": sh.reshape(64, 192), "c_valid": vd, "c_m0": m0, "c_z": z}


def build_program(NSEQ=4, NSB=16, SEQ=2048, do_sample=True, STAGE=99):
    T = 512
    NT = SEQ // T
    NTOKS = NSB * 4
    nc = bass.Bass("TRN2", target_bir_lowering=False)
    S = Sched()

    def din(name, shape):
        return nc.dram_tensor(name, list(shape), F32, kind="ExternalInput").ap()

    def dout(name, shape):
        return nc.dram_tensor(name, list(shape), F32, kind="ExternalOutput").ap()

    xp = din("xp", [NSEQ, SEQ, 1024])
    xs = din("xs", [NTOKS, 1024])
    cache = [din("c128", [NSB, 128, 1024]), din("c512", [NSB, 512, 1024]), din("c2048", [NSB, 2048, 1024])]
    spool = din("spool", [NSB, 15, 512])
    Wd = {}
    for nm, shp in (("ffn1_norm", [1024]), ("ffn1_w_gu", [1024, 5632]), ("ffn1_w_down", [2816, 1024]),
                    ("mix_norm", [1024]), ("w_in", [1024, 7168]), ("q_norm", [1536]), ("k_norm", [1536]),
                    ("pool_w", [4, 128, 128]), ("pool_scale", [512]), ("w_branch_pool", [512, 1024]),
                    ("w_branch_att", [512, 1024]), ("w_out", [1024, 1024]), ("ffn2_norm", [1024]),
                    ("ffn2_w_gu", [1024, 5632]), ("ffn2_w_down", [2816, 1024])):
        Wd[nm] = din(nm, shp)
    c_ident = din("c_ident", [128, 128])
    c_mask = din("c_mask", [128, 384])
    c_invc = din("c_invc", [128, 64])
    c_shift = din("c_shift", [64, 192])
    c_valid = din("c_valid", [64, 3])
    c_m0 = din("c_m0", [128, 4])
    c_z = din("c_z", [128, 127])

    yp = dout("yp", [NSEQ, SEQ, 1024])
    ys = dout("ys", [NTOKS, 1024])
    kvp = [dout("kv128p", [NSEQ, 128, 1024]), dout("kv512p", [NSEQ, 512, 1024]), dout("kv2048p", [NSEQ, SEQ, 1024])]
    poolp = dout("poolp", [NSEQ, 15, 512])
    kvs = [dout("kv128s", [NSB, 128, 1024]), dout("kv512s", [NSB, 512, 1024]), dout("kv2048s", [NSB, 2048, 1024])]
    pools = dout("pools", [NSB, 15, 512])

    slabs = []

    def piece(w, col0, ncols, nk, off):
        return (w[0:nk * 128, col0:col0 + ncols].rearrange("(kc p) c -> p kc c", p=128), off, nk, ncols)

    def add_slab(n, pieces):
        slabs.append((n, pieces))
        return len(slabs) - 1

    def ffn_slabs(wgu, wdn):
        gu = []
        for s in range(11):
            gu.append(add_slab(4096, [("ab", wgu, s)]))
        dn = [add_slab(5632, [piece(wdn, d * 256, 256, 22, 0)]) for d in range(4)]
        return gu, dn

    f1gu, f1dn = ffn_slabs(Wd["ffn1_w_gu"], Wd["ffn1_w_down"])
    sl_u = add_slab(4096, [piece(Wd["w_in"], 0, 512, 8, 0)])
    sl_q = [add_slab(4096, [piece(Wd["w_in"], 512 + g * 512, 512, 8, 0)]) for g in range(3)]
    sl_k = [add_slab(4096, [piece(Wd["w_in"], 2048 + g * 512, 512, 8, 0)]) for g in range(3)]
    sl_v = [add_slab(4096, [piece(Wd["w_in"], 3584 + g * 512, 512, 8, 0)]) for g in range(3)]
    sl_m = [add_slab(3072, [piece(Wd["w_in"], 5120 + m * 128, 128, 8, 0),
                            piece(Wd["w_in"], 6144 + m * 128, 128, 8, 1024),
                            piece(Wd["w_branch_pool"], m * 128, 128, 4, 2048),
                            piece(Wd["w_branch_att"], m * 128, 128, 4, 2560)]) for m in range(8)]
    sl_o = [add_slab(4096, [piece(Wd["w_out"], o * 512, 512, 8, 0)]) for o in range(2)]
    f2gu, f2dn = ffn_slabs(Wd["ffn2_w_gu"], Wd["ffn2_w_down"])
    NSLAB = len(slabs)
    scr = nc.dram_tensor("wscr", [NSLAB, 128, SLOT], BF).ap()

    tile_order = (f1gu + f1dn + [sl_u] + [x for g in range(3) for x in (sl_q[g], sl_k[g], sl_v[g])]
                  + sl_m + sl_o + f2gu + f2dn)
    conv_order = tile_order

    with contextlib.ExitStack() as st:
        E = st.enter_context

        def sb(name, shape, dt=F32):
            return E(nc.sbuf_tensor(name, list(shape), dt))

        x = sb("x", [128, 8, T])
        h = sb("h", [128, 8, T], BF)
        R = sb("R", [128, 22, T], BF)
        wring = [sb("wr%d" % i, [128, SLOT], BF) for i in range(NSLOT)]
        HIST = sb("HIST", [128, 16384], BF)
        VH = sb("VH", [128, 32, 512], BF)
        ub = sb("ub", [128, 4, 528])
        la = sb("la", [128, 528])
        lb = sb("lb", [128, 528])
        pl = sb("pl", [128, 4, T], BF)
        kvst = [sb("kvst%d" % i, [128, 512]) for i in range(6)]
        v2st = [sb("v2st%d" % i, [128, 512], BF) for i in range(2)]
        sq = [sb("sq%d" % i, [128, 512]) for i in range(2)]
        sqb = [sb("sqb%d" % i, [128, 512], BF) for i in range(2)]
        rstd = sb("rstd", [128, 512])
        Pt = [sb("Pt%d" % i, [128, 512], BF) for i in range(4)]
        sa = [sb("sa%d" % i, [128, 512], BF) for i in range(2)]
        gt = [sb("gt%d" % i, [128, 512]) for i in range(3)]
        small = sb("small", [128, 64])
        ident = sb("ident", [128, 128])
        onesb = sb("onesb", [128, 128], BF)
        maskb = sb("maskb", [128, 384], BF)
        gk = sb("gk", [128, 1536])
        gqT = sb("gqT", [128, 12])
        gn = sb("gn", [128, 3, 8])
        psc = sb("psc", [128, 4])
        pwb = sb("pwb", [128, 4, 128], BF)
        invc = sb("invc", [128, 64])
        pst = sb("pst", [16, 512])
        ps = [E(nc.psum_tensor("ps%d" % i, [128, 512], F32)) for i in range(8)]
        import os as _os
        if _os.environ.get("K_VERBOSE"):
            print("SBUF bytes remaining", nc.sbuf_bytes_remaining)

        def Rblk(j0, n):
            return R[:, j0:j0 + n, :]
        QT = lambda c0, n: Rblk(c0, n)
        attT = lambda c: R[:, 12 + c, :]
        py = lambda c: R[:, 16 + c, :]
        mg = lambda c: R[:, c, :]

        Rf = R[:].rearrange("p a b -> p (a b)").bitcast(F32)

        def tokst(i):
            return Rf[:, i * 1024:(i + 1) * 1024]

        def tokst_keys(i):
            return ["R%d" % j for j in range(4 * i, 4 * i + 4)]

        KT = [HIST[:, 0:4096].rearrange("p (c t) -> p c t", c=4),
              HIST[:, 4096:8192].rearrange("p (c t) -> p c t", c=4),
              HIST[:, 8192:16384].rearrange("p (c t) -> p c t", c=4)]

        bank_state = {"i": 0, "set": list(range(8))}

        def nb():
            s_ = bank_state["set"]
            b = s_[bank_state["i"] % len(s_)]
            bank_state["i"] += 1
            return b

        def MM(out, lhsT, rhs, start, stop, reads, writes, **kw):
            S.op("pe", lambda e: e.matmul(out, lhsT=lhsT, rhs=rhs, start=start, stop=stop, **kw), reads, writes)

        def TR(out, in_, idn, reads, writes):
            S.op("pe", lambda e: e.transpose(out, in_, idn), reads, writes)

        def ACT(out, in_, func, reads, writes, **kw):
            S.op("act", lambda e: e.activation(out=out, in_=in_, func=func, **kw), reads, writes)

        def TT(out, in0, in1, op, reads, writes, eng="dve"):
            S.op(eng, lambda e: e.tensor_tensor(out=out, in0=in0, in1=in1, op=op), reads, writes)

        def TS(out, in0, s1, s2, op0, op1, reads, writes, eng="dve"):
            if op1 is None:
                S.op(eng, lambda e: e.tensor_scalar(out=out, in0=in0, scalar1=s1, scalar2=None, op0=op0), reads, writes)
            else:
                S.op(eng, lambda e: e.tensor_scalar(out=out, in0=in0, scalar1=s1, scalar2=s2, op0=op0, op1=op1), reads, writes)

        def STT(out, in0, scalar, in1, op0, op1, reads, writes):
            S.op("dve", lambda e: e.scalar_tensor_tensor(out=out, in0=in0, scalar=scalar, in1=in1, op0=op0, op1=op1),
                 reads, writes)

        def CP(out, in_, reads, writes, eng="dve"):
            if eng == "act":
                S.op("act", lambda e: e.activation(out=out, in_=in_, func=AF.Copy), reads, writes)
            else:
                S.op(eng, lambda e: e.tensor_copy(out=out, in_=in_), reads, writes)

        def RCP(out, in_, reads, writes):
            S.op("dve", lambda e: e.reciprocal(out=out, in_=in_), reads, writes)

        def RED(out, in_, reads, writes):
            S.op("dve", lambda e: e.tensor_reduce(out=out, in_=in_, axis=AX.X, op=OP.add), reads, writes)

        def MSET(ap, v, writes, eng="dve"):
            S.op(eng, lambda e: e.memset(ap, v), (), writes)

        def DMA(out, in_, reads, writes, sem, eng="sp", burst=1, **kw):
            S.op(eng, lambda e: e.dma_start(out=out, in_=in_, **kw), reads, writes, dma_sem=sem, burst=burst)

        maskf = gt[0][:, 0:384]
        pwf = gt[1][:, 0:512].rearrange("p (g d) -> p g d", g=4)
        DMA(ident[:], c_ident, [], ["ident"], "c0")
        DMA(maskf, c_mask, [], ["gt0"], "c1")
        DMA(invc[:], c_invc, [], ["invc"], "c2")
        DMA(gk[:], bass.AP(Wd["k_norm"].tensor, 0, [[0, 128], [1, 1536]]), [], ["gk"], "c3")
        DMA(pwf, Wd["pool_w"].rearrange("g c d -> c g d"), [], ["gt1"], "c4")
        for j, nm in enumerate(("ffn1_norm", "mix_norm", "ffn2_norm")):
            DMA(gn[:, j, :], Wd[nm].rearrange("(c p) -> p c", p=128), [], ["gn"], "c5", allow_slow_non_contiguous=True)
        DMA(gqT[:], Wd["q_norm"].rearrange("(c p) -> p c", p=128), [], ["gqT"], "c6", allow_slow_non_contiguous=True)
        DMA(psc[:], Wd["pool_scale"].rearrange("(g p) -> p g", p=128), [], ["psc"], "c7", allow_slow_non_contiguous=True)
        CP(maskb[:], maskf, ["gt0"], ["maskb"])
        CP(pwb[:], pwf, ["gt1"], ["pwb"])
        MSET(onesb[:], 1.0, ["onesb"])

        grp_sizes = [1] * 11 + [2, 2] + [2] * 5 + [4, 4] + [2] + [3, 3, 3, 2] + [4]
        assert sum(grp_sizes) == len(conv_order)
        pos = 0
        for gi, gs in enumerate(grp_sizes):
            members = conv_order[pos:pos + gs]
            pos += gs
            ndma = sum((2 if pc[0] == "ab" else 1) for sl in members for pc in slabs[sl][1])
            for sl in members:
                n, pieces = slabs[sl]
                for pc in pieces:
                    if pc[0] == "ab":
                        _, wgu, s_ = pc
                        dstv = scr[sl, :, 0:4096].rearrange("p (kc ab c) -> p kc ab c", kc=8, ab=2)
                        for ab in range(2):
                            src = wgu[:, ab * 2816 + s_ * 256: ab * 2816 + s_ * 256 + 256].rearrange("(kc p) c -> p kc c", p=128)
                            DMA(dstv[:, :, ab, :], src, [], ["scr%d_%d" % (sl, ab)], "cv%d" % gi, eng="pool", burst=ndma)
                    else:
                        src, off, nk, ncols = pc
                        dstv = scr[sl, :, off:off + nk * ncols].rearrange("p (kc c) -> p kc c", c=ncols)
                        DMA(dstv, src, [], ["scr%d_%d" % (sl, off)], "cv%d" % gi, eng="pool", burst=ndma)

        bulk = []
        if do_sample:
            for g, Wn in enumerate((128, 512, 2048)):
                for b in range(NSB):
                    bulk.append((kvs[g][b, 0:Wn - 4, :], cache[g][b, 4:Wn, :]))
            bulk.append((pools[:, 0:11, :], spool[:, 4:15, :]))

        def issue_bulk(k):
            for _ in range(k):
                if bulk:
                    o_, i_ = bulk.pop()
                    DMA(o_, i_, ["tilestart"], [], "bulk", eng="pool")

        scr_keys = {}
        for sl, (n_, pieces_) in enumerate(slabs):
            scr_keys[sl] = [("scr%d_%d" % (sl, ab)) for pc in pieces_ if pc[0] == "ab" for ab in range(2)] + \
                           [("scr%d_%d" % (sl, pc[1])) for pc in pieces_ if pc[0] != "ab"]

        n_tiles_total = NSEQ * NT + (1 if do_sample else 0)
        stream = tile_order * n_tiles_total
        wst = {"issued": 0, "consumed": 0}

        def prefetch(upto):
            while wst["issued"] <= upto and wst["issued"] < len(stream):
                k = wst["issued"]
                sl = stream[k]
                slot = k % NSLOT
                n = slabs[sl][0]
                DMA(wring[slot][:, 0:n], scr[sl, :, 0:n], scr_keys[sl], ["w%d" % slot], "wl%d" % slot)
                wst["issued"] += 1

        def take(expect):
            k = wst["consumed"]
            assert stream[k] == expect, (k, stream[k], expect)
            prefetch(k + NSLOT - 1)
            wst["consumed"] += 1
            slot = k % NSLOT
            return wring[slot], "w%d" % slot

        def rmsnorm_to_h(Tn, j):
            bn = nb()
            for c in range(8):
                s_ = sqb[c % 2]
                ACT(s_[:, 0:Tn], x[:, c, 0:Tn], AF.Square, ["x%d" % c], ["sqb%d" % (c % 2)])
                MM(ps[bn][:, 0:Tn], onesb[:], s_[:, 0:Tn], c == 0, c == 7, ["onesb", "sqb%d" % (c % 2)], ["ps%d" % bn])
            ACT(rstd[:, 0:Tn], ps[bn][:, 0:Tn], AF.Sqrt, ["ps%d" % bn], ["rstd"], bias=EPS, scale=1.0 / 1024)
            RCP(rstd[:, 0:Tn], rstd[:, 0:Tn], ["rstd"], ["rstd"])
            for c in range(8):
                STT(h[:, c, 0:Tn], x[:, c, 0:Tn], gn[:, j, c:c + 1], rstd[:, 0:Tn], OP.mult, OP.mult,
                    ["x%d" % c, "gn", "rstd"], ["h%d" % c])

        hkeys = ["h%d" % c for c in range(8)]

        def ffn(Tn, j, gus, dns):
            rmsnorm_to_h(Tn, j)
            for s_, sl in enumerate(gus):
                wt, wk = take(sl)
                Wv = wt[:, 0:4096].rearrange("p (kc ab c) -> p kc ab c", kc=8, ab=2)
                for jj in range(2):
                    hj = 2 * s_ + jj
                    ba, bb = nb(), nb()
                    for kc in range(8):
                        MM(ps[ba][:, 0:Tn], Wv[:, kc, 0, jj * 128:(jj + 1) * 128], h[:, kc, 0:Tn], kc == 0, kc == 7,
                           [wk, "h%d" % kc], ["ps%d" % ba])
                    for kc in range(8):
                        MM(ps[bb][:, 0:Tn], Wv[:, kc, 1, jj * 128:(jj + 1) * 128], h[:, kc, 0:Tn], kc == 0, kc == 7,
                           [wk, "h%d" % kc], ["ps%d" % bb])
                    sa_ = sa[hj % 2]
                    ACT(sa_[:, 0:Tn], ps[ba][:, 0:Tn], AF.Silu, ["ps%d" % ba], ["sa%d" % (hj % 2)])
                    TT(R[:, hj, 0:Tn], sa_[:, 0:Tn], ps[bb][:, 0:Tn], OP.mult, ["sa%d" % (hj % 2), "ps%d" % bb], ["R%d" % hj])
            for d, sl in enumerate(dns):
                wt, wk = take(sl)
                Wv = wt[:, 0:5632].rearrange("p (kc c) -> p kc c", c=256)
                for mm in range(2):
                    m = 2 * d + mm
                    bo = nb()
                    for kc in range(22):
                        MM(ps[bo][:, 0:Tn], Wv[:, kc, mm * 128:(mm + 1) * 128], R[:, kc, 0:Tn], kc == 0, kc == 21,
                           [wk, "R%d" % kc], ["ps%d" % bo])
                    STT(x[:, m, 0:Tn], ps[bo][:, 0:Tn], 0.5, x[:, m, 0:Tn], OP.mult, OP.add,
                        ["ps%d" % bo, "x%d" % m], ["x%d" % m])

        def load_x(src_rows, Tn, nrows):
            ntb = max(1, Tn // 128)
            for tb in range(ntb):
                DMA(tokst(tb)[0:nrows, :], src_rows(tb), [], tokst_keys(tb) + (["tilestart"] if tb == 0 else []), "xin%d" % tb)
            for tb in range(ntb):
                stg = tokst(tb)
                for half in range(2):
                    b_ = nb()
                    for k in range(4):
                        c = 4 * half + k
                        TR(ps[b_][:, k * 128:k * 128 + nrows], stg[0:nrows, c * 128:(c + 1) * 128], ident[0:nrows, 0:nrows],
                           tokst_keys(tb) + ["ident"], ["ps%d" % b_])
                    CP(x[:, 4 * half:4 * half + 4, tb * 128:tb * 128 + nrows],
                       ps[b_][:].rearrange("p (a b) -> p a b", a=4)[:, :, 0:nrows],
                       ["ps%d" % b_], ["x%d" % c for c in range(4 * half, 4 * half + 4)], eng=("act" if half else "dve"))

        def store_y(dst_rows, Tn, nrows):
            ntb = max(1, Tn // 128)
            for tb in range(ntb):
                si = 2 + tb % 2
                stg = tokst(si)
                for half in range(2):
                    b_ = nb()
                    for k in range(4):
                        c = 4 * half + k
                        TR(ps[b_][0:nrows, k * 128:(k + 1) * 128], x[:, c, tb * 128:tb * 128 + nrows], ident[:],
                           ["x%d" % c, "ident"], ["ps%d" % b_])
                    CP(stg[0:nrows, half * 512:(half + 1) * 512], ps[b_][0:nrows, :], ["ps%d" % b_],
                       tokst_keys(si), eng=("act" if half else "dve"))
                DMA(dst_rows(tb), stg[0:nrows, :], tokst_keys(si), [], "yout%d" % (tb % 2))

        kvst_rr = {"i": 0}

        pend = []

        def flush(keep):
            while len(pend) > keep:
                pend.pop(0)[0]()

        def next_kvst():
            i = kvst_rr["i"] % 6
            kvst_rr["i"] += 1
            while any(k == "kvst%d" % i for (_, k) in pend):
                pend.pop(0)[0]()
            return kvst[i], "kvst%d" % i

        def proj_tok(wt, wk, tok_ap_fn, M):
            b_ = nb()
            Wv = wt[:, 0:4096].rearrange("p (kc c) -> p kc c", c=512)
            for kc in range(8):
                MM(ps[b_][0:M, :], tok_ap_fn(kc), Wv[:, kc, :], kc == 0, kc == 7, [wk, "h%d" % kc], ["ps%d" % b_])
            return b_

        def head_norm(b_, M, gain_ap, gain_key):
            i = b_ % 2
            ACT(sq[i][0:M, :], ps[b_][0:M, :], AF.Square, ["ps%d" % b_], ["sq%d" % i])
            col = 8 * i
            RED(small[0:M, col:col + 8], sq[i][0:M, :].rearrange("p (a b) -> p a b", b=64), ["sq%d" % i], ["small%d" % i])
            ACT(small[0:M, col:col + 8], small[0:M, col:col + 8], AF.Sqrt, ["small%d" % i], ["small%d" % i], bias=EPS, scale=1.0 / 64)
            RCP(small[0:M, col:col + 8], small[0:M, col:col + 8], ["small%d" % i], ["small%d" % i])
            stg, sk = next_kvst()
            TT(stg[0:M, :].rearrange("p (a b) -> p a b", b=64), ps[b_][0:M, :].rearrange("p (a b) -> p a b", b=64),
               small[0:M, col:col + 8].unsqueeze(2).to_broadcast([M, 8, 64]), OP.mult, ["ps%d" % b_, "small%d" % i], [sk])
            if gain_ap is not None:
                TT(stg[0:M, :], stg[0:M, :], gain_ap, OP.mult, [sk, gain_key], [sk])
            return stg, sk

        def transpose_to_feat(stg, sk, M, out_ap3, out_keys, gain3=None, eng="act"):
            b_ = nb()
            for cc in range(4):
                TR(ps[b_][:, cc * 128:cc * 128 + M], stg[0:M, cc * 128:(cc + 1) * 128], ident[0:M, 0:M], [sk, "ident"], ["ps%d" % b_])
            src = ps[b_][:].rearrange("p (a b) -> p a b", a=4)[:, :, 0:M]
            if gain3 is not None:
                TT(out_ap3, src, gain3.unsqueeze(2).to_broadcast([128, 4, M]), OP.mult, ["ps%d" % b_, "gqT"], out_keys)
            else:
                CP(out_ap3, src, ["ps%d" % b_], out_keys, eng=eng)

        def prompt_tile(s, i):
            tok0 = i * T
            bank_state["set"] = list(range(8))
            load_x(lambda tb: xp[s, tok0 + tb * 128: tok0 + (tb + 1) * 128, :], T, 128)
            if STAGE == 1:
                return
            ffn(T, 0, f1gu, f1dn)
            if STAGE == 2:
                store_y(lambda tb: yp[s, tok0 + tb * 128: tok0 + (tb + 1) * 128, :], T, 128)
                return
            rmsnorm_to_h(T, 1)
            wt, wk = take(sl_u)
            Wv = wt[:, 0:4096].rearrange("p (kc c) -> p kc c", c=512)
            if i == 0:
                MSET(ub[:, :, 0:16], 0.0, ["ub%d" % g for g in range(4)])
            for g in range(4):
                b_ = nb()
                for kc in range(8):
                    MM(ps[b_][:], Wv[:, kc, g * 128:(g + 1) * 128], h[:, kc, :], kc == 0, kc == 7, [wk, "h%d" % kc], ["ps%d" % b_])
                CP(ub[:, g, 16:528], ps[b_][:], ["ps%d" % b_], ["ub%d" % g], eng="act")
            for g in range(4):
                ug = ub[:, g, :]
                uk = "ub%d" % g
                TT(la[:, 2:528], ug[:, 2:528], ug[:, 1:527], OP.add, [uk], ["la"])
                cur, ck = la, "la"
                if g >= 1:
                    TT(lb[:, 4:528], la[:, 4:528], la[:, 2:526], OP.add, ["la"], ["lb"])
                    cur, ck = lb, "lb"
                if g >= 2:
                    TT(la[:, 8:528], lb[:, 8:528], lb[:, 4:524], OP.add, ["lb"], ["la"])
                    cur, ck = la, "la"
                if g >= 3:
                    TT(lb[:, 16:528], la[:, 16:528], la[:, 8:520], OP.add, ["la"], ["lb"])
                    cur, ck = lb, "lb"
                w = POOL_W[g]
                STT(pl[:, g, :], cur[:, 16:528], 1.0 / w, ug[:, 16:528], OP.mult, OP.subtract, [ck, uk], ["pl%d" % g])
                if i == 0:
                    TT(cur[:, 16:32], cur[:, 16:32], invc[:, g * 16:(g + 1) * 16], OP.mult, [ck, "invc"], [ck])
                    TT(pl[:, g, 0:16], cur[:, 16:32], ug[:, 16:32], OP.subtract, [ck, uk], ["pl%d" % g])
            if i == NT - 1:
                b_ = nb()
                for g in range(4):
                    TR(ps[b_][0:16, g * 128:(g + 1) * 128], ub[:, g, 512:528], ident[:], ["ub%d" % g, "ident"], ["ps%d" % b_])
                CP(pst[:], ps[b_][0:16, :], ["ps%d" % b_], ["pst"], eng="act")
                DMA(poolp[s, :, :], pst[1:16, :], ["pst"], [], "pout")
            for g in range(4):
                CP(ub[:, g, 0:16], ub[:, g, 512:528], ["ub%d" % g], ["ub%d" % g])
            for g in range(4):
                b_ = nb()
                MM(ps[b_][:], pwb[:, g, :], pl[:, g, :], True, True, ["pwb", "pl%d" % g], ["ps%d" % b_])
                TS(py(g), ps[b_][:], psc[:, g:g + 1], None, OP.mult, None, ["ps%d" % b_, "psc"], ["R%d" % (16 + g)])
            for g in range(3):
                slot_kt = (i % 2) if g < 2 else i
                ktoff = slot_kt * T
                wt, wk = take(sl_q[g])
                for tb in range(4):
                    b_ = proj_tok(wt, wk, lambda kc, tb=tb: h[:, kc, tb * 128:(tb + 1) * 128], 128)
                    stg, sk = head_norm(b_, 128, None, None)
                    pend.append((lambda stg=stg, sk=sk, g=g, tb=tb: transpose_to_feat(
                        stg, sk, 128, R[:, 4 * g:4 * g + 4, tb * 128:(tb + 1) * 128],
                        ["R%d" % c for c in range(4 * g, 4 * g + 4)], gain3=gqT[:, 4 * g:4 * g + 4]), sk))
                    flush(2)
                wt, wk = take(sl_k[g])
                for tb in range(4):
                    b_ = proj_tok(wt, wk, lambda kc, tb=tb: h[:, kc, tb * 128:(tb + 1) * 128], 128)
                    stg, sk = head_norm(b_, 128, gk[:, g * 512:(g + 1) * 512], "gk")
                    Wg = min((128, 512, 2048)[g], SEQ)
                    keep = (tok0 + tb * 128) >= SEQ - Wg
                    if keep:
                        row0 = tok0 + tb * 128 - (SEQ - Wg)
                        DMA(kvp[g][s, row0:row0 + 128, 0:512], stg[:, :], [sk], [], "ko" + sk)
                    pend.append((lambda stg=stg, sk=sk, g=g, tb=tb, ktoff=ktoff, slot_kt=slot_kt: transpose_to_feat(
                        stg, sk, 128, KT[g][:, :, ktoff + tb * 128: ktoff + (tb + 1) * 128],
                        ["KT%d_%d_%d" % (g, slot_kt, tb)], eng=("act" if tb % 2 else "dve")), sk))
                    flush(2)
                wt, wk = take(sl_v[g])
                if g == 0:
                    for w in range(4):
                        b_ = proj_tok(wt, wk, lambda kc, w=w: h[:, kc, w * 128:(w + 1) * 128], 128)
                        slotv = (4 * i + w) % 8
                        CP(VH[:, slotv, :], ps[b_][:], ["ps%d" % b_], ["V0_%d" % slotv])
                        if i == NT - 1 and w == 3:
                            stg, sk = next_kvst()
                            CP(stg[:, :], ps[b_][:], ["ps%d" % b_], [sk], eng="act")
                            DMA(kvp[0][s, 0:128, 512:1024], stg[:, :], [sk], [], "ko" + sk)
                        flush(2)
                elif g == 1:
                    for r4 in range(4):
                        b_ = proj_tok(wt, wk, lambda kc, r4=r4: h[:, kc, r4:T:4], 128)
                        slotv = 8 + 2 * r4 + (i % 2)
                        CP(VH[:, slotv, :], ps[b_][:], ["ps%d" % b_], ["V1_%d" % slotv])
                        if i == NT - 1:
                            stg, sk = next_kvst()
                            CP(stg[:, :], ps[b_][:], ["ps%d" % b_], [sk], eng="act")
                            DMA(kvp[1][s, r4:512:4, 512:1024], stg[:, :], [sk], [], "ko" + sk)
                        flush(2)
                else:
                    po = 32 * i
                    for r4 in range(4):
                        b_ = proj_tok(wt, wk, lambda kc, r4=r4: h[:, kc, r4:T:4], 128)
                        vb, vbk = v2st[r4 % 2], "v2st%d" % (r4 % 2)
                        CP(vb[:, :], ps[b_][:], ["ps%d" % b_], [vbk])
                        stg, sk = next_kvst()
                        CP(stg[:, :], ps[b_][:], ["ps%d" % b_], [sk], eng="act")
                        DMA(kvp[2][s, tok0 + r4: tok0 + T: 4, 512:1024], stg[:, :], [sk], [], "ko" + sk)
                        for j in range(4):
                            r16 = r4 + 4 * j
                            DMA(VH[po:po + 32, 16 + r16, :], vb[j:128:4, :], [vbk], ["V2_%d" % r16], "vsh" + vbk, burst=4)
                        flush(1 if r4 < 1 else 0)
            flush(0)
            steps = []
            for hp in range(4):
                bo, bd = (4, 5) if hp % 2 == 0 else (6, 7)
                pair_steps = []
                for hh in range(2):
                    hd = 2 * hp + hh
                    p0 = 64 * hh
                    for prev in (0, 1):
                        qc = hp
                        w0 = 1 if (prev and i == 0) else 0
                        qk, pvl = [], []
                        for w in range(w0, 4):
                            U = 4 * i + w - prev
                            kslot, ktb = (U // 4) % 2, U % 4
                            qk.append((128, slice(w * 128, (w + 1) * 128),
                                       KT[0][p0:p0 + 64, hp, kslot * T + ktb * 128: kslot * T + (ktb + 1) * 128],
                                       R[p0:p0 + 64, qc, w * 128:(w + 1) * 128], ["KT0_%d_%d" % (kslot, ktb), "R%d" % qc]))
                            pvl.append((VH[:, U % 8, hd * 64:(hd + 1) * 64], "V0_%d" % (U % 8),
                                        slice(w * 128, (w + 1) * 128), slice(w * 128, (w + 1) * 128)))
                        pair_steps.append(dict(qk=qk, kp=128, c0=w0 * 128, c1=512,
                                               mask=(maskb[:, 128:256] if prev else maskb[:, 0:128]), pv=pvl, p0=p0))
                    qc = 4 + hp
                    for prev in (0, 1):
                        if prev and i == 0:
                            continue
                        kslot = (i - prev) % 2
                        qk, pvl = [], []
                        for r4 in range(4):
                            qk.append((128, slice(r4 * 128, (r4 + 1) * 128),
                                       KT[1][p0:p0 + 64, hp, kslot * T + r4: (kslot + 1) * T: 4], R[p0:p0 + 64, qc, r4:T:4],
                                       ["KT1_%d_%d" % (kslot, tb) for tb in range(4)] + ["R%d" % qc]))
                            slotv = 8 + 2 * r4 + kslot
                            pvl.append((VH[:, slotv, hd * 64:(hd + 1) * 64], "V1_%d" % slotv,
                                        slice(r4 * 128, (r4 + 1) * 128), slice(r4, T, 4)))
                        pair_steps.append(dict(qk=qk, kp=128, c0=0, c1=512,
                                               mask=(maskb[:, 128:256] if prev else maskb[:, 0:128]), pv=pvl, p0=p0))
                    qc = 8 + hp
                    nk = 32 * (i + 1)
                    qk, pvl = [], []
                    for r16 in range(16):
                        qk.append((nk, slice(r16 * 32, (r16 + 1) * 32), KT[2][p0:p0 + 64, hp, r16:(i + 1) * T:16],
                                   R[p0:p0 + 64, qc, r16:T:16],
                                   ["KT2_%d_%d" % (ii, tb) for ii in range(i + 1) for tb in range(4)] + ["R%d" % qc]))
                        pvl.append((VH[0:nk, 16 + r16, hd * 64:(hd + 1) * 64], "V2_%d" % r16,
                                    slice(r16 * 32, (r16 + 1) * 32), slice(r16, T, 16)))
                    pair_steps.append(dict(qk=qk, kp=nk, c0=0, c1=512, mask=maskb[0:nk, 256 + 32 * i:256 + 32 * (i + 1)],
                                           pv=pvl, p0=p0))
                for j_, st_ in enumerate(pair_steps):
                    st_.update(bo=bo, bd=bd, hp=hp, first=(j_ == 0), last=(j_ == len(pair_steps) - 1))
                steps.extend(pair_steps)

            def att_front(st_, n):
                bS = n % 4
                pt, pk = Pt[n % 4], "Pt%d" % (n % 4)
                kp = st_["kp"]
                for (kparts, cols, lhsT, rhs, rd) in st_["qk"]:
                    MM(ps[bS][0:kparts, cols], lhsT, rhs, True, True, rd, ["ps%d" % bS])
                c0, c1, maskap = st_["c0"], st_["c1"], st_["mask"]
                mb = maskap.shape[-1]
                ACT(pt[0:kp, c0:c1], ps[bS][0:kp, c0:c1], AF.Exp, ["ps%d" % bS], [pk], scale=0.125)
                TT(pt[0:kp, c0:c1].rearrange("p (a b) -> p a b", b=mb), pt[0:kp, c0:c1].rearrange("p (a b) -> p a b", b=mb),
                   maskap.unsqueeze(1).to_broadcast([kp, (c1 - c0) // mb, mb]), OP.mult, [pk, "maskb"], [pk])

            def att_back(st_, n):
                pt, pk = Pt[n % 4], "Pt%d" % (n % 4)
                kp, bo, bd, p0 = st_["kp"], st_["bo"], st_["bd"], st_["p0"]
                if st_["first"]:
                    MSET(ps[bo][:], 0.0, ["ps%d" % bo])
                    MSET(ps[bd][:], 0.0, ["ps%d" % bd])
                for (vap, vkey, pc, oc) in st_["pv"]:
                    MM(ps[bo][p0:p0 + 64, oc], vap, pt[0:kp, pc], False, False, [vkey, pk, "ps%d" % bo], ["ps%d" % bo],
                       skip_group_check=True, tile_position=(0, p0))
                    MM(ps[bd][p0:p0 + 64, oc], onesb[0:kp, 0:64], pt[0:kp, pc], False, False,
                       ["onesb", pk, "ps%d" % bd], ["ps%d" % bd], skip_group_check=True, tile_position=(0, p0))
                if st_["last"]:
                    RCP(gt[2][:], ps[bd][:], ["ps%d" % bd], ["gt2"])
                    TT(attT(st_["hp"]), ps[bo][:], gt[2][:], OP.mult, ["ps%d" % bo, "gt2"], ["R%d" % (12 + st_["hp"])])

            LOOK = 2
            for n in range(len(steps) + LOOK):
                if n < len(steps):
                    att_front(steps[n], n)
                if n - LOOK >= 0:
                    att_back(steps[n - LOOK], n - LOOK)

            bank_state["set"] = list(range(8))
            if STAGE == 5:
                store_y(lambda tb: yp[s, tok0 + tb * 128: tok0 + (tb + 1) * 128, :], T, 128)
                return
            merge_and_out(T)
            ffn(T, 2, f2gu, f2dn)
            store_y(lambda tb: yp[s, tok0 + tb * 128: tok0 + (tb + 1) * 128, :], T, 128)

        def merge_and_out(Tn):
            for m in range(8):
                wt, wk = take(sl_m[m])
                Wg = wt[:, 0:2048].rearrange("p (s kc c) -> p s kc c", s=2, kc=8)
                Wb = wt[:, 2048:3072].rearrange("p (s kc c) -> p s kc c", s=2, kc=4)
                bgp, bbp, bga, bba = nb(), nb(), nb(), nb()
                for kc in range(8):
                    MM(ps[bgp][:, 0:Tn], Wg[:, 0, kc, :], h[:, kc, 0:Tn], kc == 0, kc == 7, [wk, "h%d" % kc], ["ps%d" % bgp])
                for kc in range(4):
                    MM(ps[bbp][:, 0:Tn], Wb[:, 0, kc, :], R[:, 16 + kc, 0:Tn], kc == 0, kc == 3, [wk, "R%d" % (16 + kc)], ["ps%d" % bbp])
                for kc in range(8):
                    MM(ps[bga][:, 0:Tn], Wg[:, 1, kc, :], h[:, kc, 0:Tn], kc == 0, kc == 7, [wk, "h%d" % kc], ["ps%d" % bga])
                for kc in range(4):
                    MM(ps[bba][:, 0:Tn], Wb[:, 1, kc, :], R[:, 12 + kc, 0:Tn], kc == 0, kc == 3, [wk, "R%d" % (12 + kc)], ["ps%d" % bba])
                ACT(gt[0][:, 0:Tn], ps[bgp][:, 0:Tn], AF.Sigmoid, ["ps%d" % bgp], ["gt0"])
                ACT(gt[1][:, 0:Tn], ps[bga][:, 0:Tn], AF.Sigmoid, ["ps%d" % bga], ["gt1"])
                TT(gt[0][:, 0:Tn], gt[0][:, 0:Tn], ps[bbp][:, 0:Tn], OP.mult, ["gt0", "ps%d" % bbp], ["gt0"])
                TT(gt[1][:, 0:Tn], gt[1][:, 0:Tn], ps[bba][:, 0:Tn], OP.mult, ["gt1", "ps%d" % bba], ["gt1"])
                TT(R[:, m, 0:Tn], gt[0][:, 0:Tn], gt[1][:, 0:Tn], OP.add, ["gt0", "gt1"], ["R%d" % m])
            for o in range(2):
                wt, wk = take(sl_o[o])
                Wv = wt[:, 0:4096].rearrange("p (kc c) -> p kc c", c=512)
                for mm in range(4):
                    m = 4 * o + mm
                    b_ = nb()
                    for kc in range(8):
                        MM(ps[b_][:, 0:Tn], Wv[:, kc, mm * 128:(mm + 1) * 128], R[:, kc, 0:Tn], kc == 0, kc == 7,
                           [wk, "R%d" % kc], ["ps%d" % b_])
                    TT(x[:, m, 0:Tn], ps[b_][:, 0:Tn], x[:, m, 0:Tn], OP.add, ["ps%d" % b_, "x%d" % m], ["x%d" % m])

        def sample_tile():
            Tn = NTOKS
            S.barrier()
            bank_state["set"] = list(range(6))
            Hf = HIST[:].bitcast(F32)
            qn = Hf[:, 0:1536]
            kn = Hf[:, 1536:3072]
            vv = Hf[:, 3072:4608]
            gq = Hf[:, 4608:6144]
            tA = Hf[:, 6144:7680]
            Oacc = Hf[:, 7680:8192]
            VHb = VH[:].rearrange("p a b -> p (a b)")
            Vf = VHb[:, 0:7168].bitcast(F32)
            CKb = [Vf[:, 1024 * j: 1024 * (j + 1)] for j in range(3)]
            Dacc = Vf[:, 3072:3080]
            pself = Vf[:, 3088:3112]
            p8 = Vf[:, 3120:3128]
            sred = Vf[:, 3136:3160]
            zf = Vf[:, 3168:3295]
            sh = Vf[:, 3296:3488]
            vld = Vf[:, 3488:3491]
            m0 = Vf[:, 3492:3496]
            qnb = VHb[:, 7168:8704]
            tB = VHb[:, 8704:9216]
            p8b = VHb[:, 9216:9224]
            zb = VHb[:, 9232:9359]
            identb = VHb[:, 9360:9488]
            ue = ub[:].rearrange("p a b -> p (a b)")[:, 0:1280].rearrange("p (g b t) -> p g b t", g=4, b=16)
            l1 = la[:, 0:320].rearrange("p (b t) -> p b t", t=20)
            l2 = lb[:, 0:320].rearrange("p (b t) -> p b t", t=20)
            ust = gt[2]
            stS = sq[0]
            DMA(gq, bass.AP(Wd["q_norm"].tensor, 0, [[0, 128], [1, 1536]]), [], ["gq"], "s0")
            DMA(zf, c_z, [], ["zf"], "s1")
            DMA(sh[0:64, :], c_shift, [], ["sh"], "s2")
            DMA(vld[0:64, :], c_valid, [], ["vld"], "s3")
            DMA(m0, c_m0, [], ["m0"], "s4")
            CP(zb, zf, ["zf"], ["zb"])
            CP(identb, ident[:], ["ident"], ["identb"])
            load_x(lambda tb: xs[:, :], Tn, Tn)
            ffn(Tn, 0, f1gu, f1dn)
            rmsnorm_to_h(Tn, 1)
            wt, wk = take(sl_u)
            Wv = wt[:, 0:4096].rearrange("p (kc c) -> p kc c", c=512)
            nblk = (NSB + 7) // 8
            for blk in range(nblk):
                nb_ = min(8, NSB - 8 * blk)
                nr = nb_ * 15
                DMA(stS[0:nr, 0:512], spool[8 * blk:8 * blk + nb_, :, :].rearrange("b r c -> (b r) c"), [], ["sq0"], "s5")
                b_ = nb()
                for g in range(4):
                    TR(ps[b_][:, g * 128:g * 128 + nr], stS[0:nr, g * 128:(g + 1) * 128], ident[0:nr, 0:nr], ["sq0", "ident"], ["ps%d" % b_])
                for g in range(4):
                    CP(ue[:, g, 8 * blk:8 * blk + nb_, 1:16], ps[b_][:, g * 128:g * 128 + nr].rearrange("p (b r) -> p b r", r=15),
                       ["ps%d" % b_], ["ub%d" % g], eng=("act" if g % 2 else "dve"))
            for g in range(4):
                b_ = nb()
                for kc in range(8):
                    MM(ps[b_][:, 0:Tn], Wv[:, kc, g * 128:(g + 1) * 128], h[:, kc, 0:Tn], kc == 0, kc == 7, [wk, "h%d" % kc], ["ps%d" % b_])
                CP(ue[:, g, 0:NSB, 16:20], ps[b_][:, 0:Tn].rearrange("p (b t) -> p b t", t=4), ["ps%d" % b_], ["ub%d" % g], eng="act")
            for g in range(4):
                ug = ue[:, g, 0:NSB, :]
                uk = "ub%d" % g
                A, Bf = l1[:, 0:NSB, :], l2[:, 0:NSB, :]
                TT(A[:, :, 2:20], ug[:, :, 2:20], ug[:, :, 1:19], OP.add, [uk], ["la"])
                cur, ck = A, "la"
                if g >= 1:
                    TT(Bf[:, :, 4:20], A[:, :, 4:20], A[:, :, 2:18], OP.add, ["la"], ["lb"])
                    cur, ck = Bf, "lb"
                if g >= 2:
                    TT(A[:, :, 8:20], Bf[:, :, 8:20], Bf[:, :, 4:16], OP.add, ["lb"], ["la"])
                    cur, ck = A, "la"
                if g >= 3:
                    TT(Bf[:, :, 16:20], A[:, :, 16:20], A[:, :, 8:12], OP.add, ["la"], ["lb"])
                    cur, ck = Bf, "lb"
                STT(pl[:, g, 0:Tn].rearrange("p (b t) -> p b t", t=4), cur[:, :, 16:20], 1.0 / POOL_W[g], ug[:, :, 16:20],
                    OP.mult, OP.subtract, [ck, uk], ["pl%d" % g])
            b_ = nb()
            for g in range(4):
                CP(tA[:, g * 64:g * 64 + Tn].rearrange("p (b t) -> p b t", t=4), ue[:, g, 0:NSB, 16:20], ["ub%d" % g], ["tA"])
            for g in range(4):
                TR(ps[b_][0:Tn, g * 128:(g + 1) * 128], tA[:, g * 64:g * 64 + Tn], ident[:], ["tA", "ident"], ["ps%d" % b_])
            CP(ust[0:Tn, :], ps[b_][0:Tn, :], ["ps%d" % b_], ["gt2"], eng="act")
            DMA(pools[:, 11:15, :], ust[0:Tn, :], ["gt2"], [], "s6")
            for g in range(4):
                b_ = nb()
                MM(ps[b_][:, 0:Tn], pwb[:, g, :], pl[:, g, 0:Tn], True, True, ["pwb", "pl%d" % g], ["ps%d" % b_])
                TS(R[:, 16 + g, 0:Tn], ps[b_][:, 0:Tn], psc[:, g:g + 1], None, OP.mult, None, ["ps%d" % b_, "psc"], ["R%d" % (16 + g)])
            for g in range(3):
                Wn = (128, 512, 2048)[g]
                wt, wk = take(sl_q[g])
                b_ = proj_tok(wt, wk, lambda kc: h[:, kc, 0:Tn], Tn)
                stg, sk = head_norm(b_, Tn, gq[0:Tn, g * 512:(g + 1) * 512], "gq")
                CP(qn[0:Tn, g * 512:(g + 1) * 512], stg[0:Tn, :], [sk], ["qn"])
                wt, wk = take(sl_k[g])
                b_ = proj_tok(wt, wk, lambda kc: h[:, kc, 0:Tn], Tn)
                stg, sk = head_norm(b_, Tn, gk[0:Tn, g * 512:(g + 1) * 512], "gk")
                CP(kn[0:Tn, g * 512:(g + 1) * 512], stg[0:Tn, :], [sk], ["kn"])
                DMA(kvs[g][:, Wn - 4:Wn, 0:512], stg[0:Tn, :], [sk], [], "ko" + sk)
                wt, wk = take(sl_v[g])
                b_ = proj_tok(wt, wk, lambda kc: h[:, kc, 0:Tn], Tn)
                CP(vv[0:Tn, g * 512:(g + 1) * 512], ps[b_][0:Tn, :], ["ps%d" % b_], ["vv"], eng="act")
                DMA(kvs[g][:, Wn - 4:Wn, 512:1024], vv[0:Tn, g * 512:(g + 1) * 512], ["vv"], [], "s7")
            CP(qnb[0:Tn, :], qn[0:Tn, :], ["qn"], ["qnb"])
            TT(tA[0:Tn, :], qn[0:Tn, :], kn[0:Tn, :], OP.mult, ["qn", "kn"], ["tA"])
            RED(sred[0:Tn, :], tA[0:Tn, :].rearrange("p (a b) -> p a b", b=64), ["tA"], ["sred"])
            ACT(pself[0:Tn, :], sred[0:Tn, :], AF.Exp, ["sred"], ["pself"], scale=0.125)
            TT(Dacc[0:Tn, :], pself[0:Tn, 0:8], pself[0:Tn, 8:16], OP.add, ["pself"], ["Dacc"])
            TT(Dacc[0:Tn, :], Dacc[0:Tn, :], pself[0:Tn, 16:24], OP.add, ["pself", "Dacc"], ["Dacc"])
            TT(tA[0:Tn, :].rearrange("p (a b) -> p a b", b=64), vv[0:Tn, :].rearrange("p (a b) -> p a b", b=64),
               pself[0:Tn, :].unsqueeze(2).to_broadcast([Tn, 24, 64]), OP.mult, ["vv", "pself", "tA"], ["tA"])
            TT(Oacc[0:Tn, :], tA[0:Tn, 0:512], tA[0:Tn, 512:1024], OP.add, ["tA"], ["Oacc"])
            TT(Oacc[0:Tn, :], Oacc[0:Tn, :], tA[0:Tn, 1024:1536], OP.add, ["tA", "Oacc"], ["Oacc"])
            for d in range(1, 4):
                bk, bv = nb(), nb()
                MM(ps[bk][0:Tn, :], sh[0:Tn, (d - 1) * 64:(d - 1) * 64 + Tn], kn[0:Tn, 0:512], True, True, ["sh", "kn"], ["ps%d" % bk])
                MM(ps[bv][0:Tn, :], sh[0:Tn, (d - 1) * 64:(d - 1) * 64 + Tn], vv[0:Tn, 0:512], True, True, ["sh", "vv"], ["ps%d" % bv])
                TT(tA[0:Tn, 0:512], qn[0:Tn, 0:512], ps[bk][0:Tn, :], OP.mult, ["qn", "ps%d" % bk], ["tA"])
                RED(sred[0:Tn, 0:8], tA[0:Tn, 0:512].rearrange("p (a b) -> p a b", b=64), ["tA"], ["sred"])
                ACT(p8[0:Tn, :], sred[0:Tn, 0:8], AF.Exp, ["sred"], ["p8"], scale=0.125)
                TS(p8[0:Tn, :], p8[0:Tn, :], vld[0:Tn, d - 1:d], None, OP.mult, None, ["p8", "vld"], ["p8"])
                TT(Dacc[0:Tn, :], Dacc[0:Tn, :], p8[0:Tn, :], OP.add, ["Dacc", "p8"], ["Dacc"])
                TT(tA[0:Tn, 0:512].rearrange("p (a b) -> p a b", b=64), ps[bv][0:Tn, :].rearrange("p (a b) -> p a b", b=64),
                   p8[0:Tn, :].unsqueeze(2).to_broadcast([Tn, 8, 64]), OP.mult, ["ps%d" % bv, "p8", "tA"], ["tA"])
                TT(Oacc[0:Tn, :], Oacc[0:Tn, :], tA[0:Tn, 0:512], OP.add, ["tA", "Oacc"], ["Oacc"])
            BO, BD = 6, 7
            MSET(ps[BO][:], 0.0, ["ps6"])
            MSET(ps[BD][:], 0.0, ["ps7"])
            VF2 = VHb[:, 9600:12800].bitcast(F32)
            tA3 = [VF2[:, 512 * j:512 * (j + 1)] for j in range(3)]
            sred3 = [VF2[:, 1536 + 8 * j:1544 + 8 * j] for j in range(3)]
            p83 = [VF2[:, 1568 + 8 * j:1576 + 8 * j] for j in range(3)]
            tB3 = [VHb[:, 12800 + 512 * j:12800 + 512 * (j + 1)] for j in range(3)]
            p8b3 = [VHb[:, 14336 + 8 * j:14344 + 8 * j] for j in range(3)]
            passes = []
            ckr = {"i": 0}
            cur_ck = None
            for b in range(NSB):
                for g in range(3):
                    dil = (1, 4, 16)[g]
                    for t in range(4):
                        load = None
                        if not (g == 0 and t > 0):
                            j = ckr["i"] % 6
                            ckr["i"] += 1
                            if j < 3:
                                cur_ck = (CKb[j], ["CK%d" % j], "ck%d" % j)
                            else:
                                cur_ck = (tokst(j - 3), tokst_keys(j - 3), "ck%d" % j)
                            load = cache[g][b, t:t + 127 * dil + 1:dil, :] if g > 0 else cache[g][b, 0:128, :]
                        passes.append((4 * b + t, g, t, cur_ck, load))

            def s_front(n):
                tok, g, t, (ck_ap, ck_key, ck_sem), load = passes[n]
                if load is not None:
                    DMA(ck_ap, load, [], ck_key, ck_sem)
                bq = n % 6
                MM(ps[bq][:], identb[0:Tn, tok:tok + 1].to_broadcast([Tn, 128]), qnb[0:Tn, g * 512:(g + 1) * 512], True, True,
                   ["identb", "qnb"], ["ps%d" % bq])

            def s_back(n):
                tok, g, t, (ck_ap, ck_key, ck_sem), load = passes[n]
                bq = n % 6
                j = n % 3
                TT(tA3[j], ck_ap[:, 0:512], ps[bq][:], OP.mult, ck_key + ["ps%d" % bq], ["tA3_%d" % j])
                RED(sred3[j], tA3[j].rearrange("p (a b) -> p a b", b=64), ["tA3_%d" % j], ["sred3_%d" % j])
                ACT(p83[j], sred3[j], AF.Exp, ["sred3_%d" % j], ["p83_%d" % j], scale=0.125)
                if g == 0:
                    TS(p83[j], p83[j], m0[:, t:t + 1], None, OP.mult, None, ["p83_%d" % j, "m0"], ["p83_%d" % j])
                CP(p8b3[j], p83[j], ["p83_%d" % j], ["p8b3_%d" % j], eng="act")
                TT(tB3[j].rearrange("p (a b) -> p a b", b=64), ck_ap[:, 512:1024].rearrange("p (a b) -> p a b", b=64),
                   p83[j].unsqueeze(2).to_broadcast([128, 8, 64]), OP.mult, ck_key + ["p83_%d" % j], ["tB3_%d" % j], eng="pool")
                MM(ps[BO][0:Tn, :], zb[:, 63 - tok:63 - tok + Tn], tB3[j], False, False, ["zb", "tB3_%d" % j, "ps6"], ["ps6"],
                   skip_group_check=True)
                MM(ps[BD][0:Tn, 0:8], zb[:, 63 - tok:63 - tok + Tn], p8b3[j], False, False, ["zb", "p8b3_%d" % j, "ps7"], ["ps7"],
                   skip_group_check=True)

            SL = 4
            for n in range(len(passes) + SL):
                if n < len(passes):
                    s_front(n)
                if n - SL >= 0:
                    s_back(n - SL)
            TT(Oacc[0:Tn, :], Oacc[0:Tn, :], ps[BO][0:Tn, :], OP.add, ["Oacc", "ps6"], ["Oacc"])
            TT(Dacc[0:Tn, :], Dacc[0:Tn, :], ps[BD][0:Tn, 0:8], OP.add, ["Dacc", "ps7"], ["Dacc"])
            RCP(Dacc[0:Tn, :], Dacc[0:Tn, :], ["Dacc"], ["Dacc"])
            TT(Oacc[0:Tn, :].rearrange("p (a b) -> p a b", b=64), Oacc[0:Tn, :].rearrange("p (a b) -> p a b", b=64),
               Dacc[0:Tn, :].unsqueeze(2).to_broadcast([Tn, 8, 64]), OP.mult, ["Oacc", "Dacc"], ["Oacc"])
            b_ = nb()
            for cc in range(4):
                TR(ps[b_][:, cc * 128:cc * 128 + Tn], Oacc[0:Tn, cc * 128:(cc + 1) * 128], ident[0:Tn, 0:Tn], ["Oacc", "ident"], ["ps%d" % b_])
            CP(R[:, 12:16, 0:Tn], ps[b_][:].rearrange("p (a b) -> p a b", a=4)[:, :, 0:Tn], ["ps%d" % b_],
               ["R%d" % c for c in range(12, 16)])
            bank_state["set"] = list(range(8))
            merge_and_out(Tn)
            ffn(Tn, 2, f2gu, f2dn)
            store_y(lambda tb: ys[:, :], Tn, Tn)

        n_pt = NSEQ * NT
        per_tile = -(-len(bulk) // max(1, n_pt - 3))
        for s in range(NSEQ):
            for i in range(NT):
                prompt_tile(s, i)
                if s * NT + i >= 2 or n_pt < 4:
                    issue_bulk(per_tile if (s * NT + i) < n_pt - 1 else len(bulk))
        if do_sample:
            issue_bulk(len(bulk))
            sample_tile()
        S.emit(nc)
    return nc


_PROG = {}


def kernel(x_prompt, x_sample, cache_kv_w128, cache_kv_w512, cache_kv_w2048, state_pool,
           ffn1_norm, ffn1_w_gu, ffn1_w_down, mix_norm, w_in, q_norm, k_norm, pool_w,
           pool_scale, w_branch_pool, w_branch_att, w_out, ffn2_norm, ffn2_w_gu, ffn2_w_down):
    f = lambda a: np.ascontiguousarray(np.asarray(a, dtype=np.float32))
    B, SEQ, D = x_prompt.shape
    DB = x_sample.shape[0]
    NSEQ, NSB = B // NCORES, DB // NCORES
    if "nc" not in _PROG:
        _PROG["nc"] = build_program(NSEQ=NSEQ, NSB=NSB, SEQ=SEQ)
    nc = _PROG["nc"]
    shared = {
        "ffn1_norm": f(ffn1_norm[0]), "ffn1_w_gu": f(ffn1_w_gu[0]), "ffn1_w_down": f(ffn1_w_down[0]),
        "mix_norm": f(mix_norm[0]), "w_in": f(w_in[0]), "q_norm": f(q_norm[0]).reshape(1536),
        "k_norm": f(k_norm[0]).reshape(1536), "pool_w": f(pool_w[0]), "pool_scale": f(pool_scale[0]),
        "w_branch_pool": f(w_branch_pool[0]), "w_branch_att": f(w_branch_att[0]), "w_out": f(w_out[0]),
        "ffn2_norm": f(ffn2_norm[0]), "ffn2_w_gu": f(ffn2_w_gu[0]), "ffn2_w_down": f(ffn2_w_down[0]),
    }
    shared.update(_const_tables())
    in_maps = []
    for c in range(NCORES):
        m = dict(shared)
        m["xp"] = f(x_prompt[c * NSEQ:(c + 1) * NSEQ])
        m["xs"] = f(x_sample[c * NSB:(c + 1) * NSB]).reshape(NSB * 4, D)
        m["c128"] = f(cache_kv_w128[0, c * NSB:(c + 1) * NSB]).reshape(NSB, 128, 1024)
        m["c512"] = f(cache_kv_w512[0, c * NSB:(c + 1) * NSB]).reshape(NSB, 512, 1024)
        m["c2048"] = f(cache_kv_w2048[0, c * NSB:(c + 1) * NSB]).reshape(NSB, 2048, 1024)
        m["spool"] = f(state_pool[0, c * NSB:(c + 1) * NSB])
        in_maps.append(m)
    res = run_bass_kernel_spmd(nc, in_maps, core_ids=list(range(NCORES)))
    r = res.results
    cat = lambda k: np.concatenate([np.asarray(r[c][k]) for c in range(NCORES)], axis=0)
    y_prompt = cat("yp")
    y_sample = cat("ys").reshape(DB, 4, D)
    kv128p = cat("kv128p").reshape(1, B, 128, 2, 8, 64)
    kv512p = cat("kv512p").reshape(1, B, 512, 2, 8, 64)
    kv2048p = cat("kv2048p").reshape(1, B, SEQ, 2, 8, 64)
    poolp = cat("poolp").reshape(1, B, 15, 512)
    kv128s = cat("kv128s").reshape(1, DB, 128, 2, 8, 64)
    kv512s = cat("kv512s").reshape(1, DB, 512, 2, 8, 64)
    kv2048s = cat("kv2048s").reshape(1, DB, 2048, 2, 8, 64)
    pools_ = cat("pools").reshape(1, DB, 15, 512)
    return (y_prompt, y_sample, kv128p, kv512p, kv2048p, poolp, kv128s, kv512s, kv2048s, pools_)
```

```python
import contextlib
import numpy as np
import concourse.bass as bass
import concourse.mybir as mybir
from concourse.bass_utils import run_bass_kernel_spmd

F32 = mybir.dt.float32
BF = mybir.dt.bfloat16
AF = mybir.ActivationFunctionType
OP = mybir.AluOpType
AX = mybir.AxisListType

NCORES = 8
EPS = 1e-6
SLOT = 5632
NSLOT = 3
POOL_W = (2, 4, 8, 16)


class _Op:
    __slots__ = ("eng", "fn", "deps", "sig", "dma_sem", "val", "idx", "burst")

    def __init__(self, eng, fn, dma_sem, burst=1):
        self.eng, self.fn, self.dma_sem, self.burst = eng, fn, dma_sem, burst
        self.deps, self.sig, self.val, self.idx = set(), False, None, None


class Sched:
    ENGS = ("pe", "act", "dve", "pool", "sp")

    def __init__(self):
        self.ops, self.last_w, self.readers = [], {}, {}
        self.bar = set()
        self.last_eng = {}
        self.last_dma = {}

    def op(self, eng, fn, reads=(), writes=(), dma_sem=None, burst=1):
        o = _Op(eng, fn, dma_sem, burst)
        o.idx = len(self.ops)
        deps = set(self.bar)
        for k in reads:
            w = self.last_w.get(k)
            if w is not None:
                deps.add(w)
            if k.startswith("ps"):
                deps.update(r for r in self.readers.get(k, ()) if self.ops[r].eng != eng)
        for k in writes:
            w = self.last_w.get(k)
            if w is not None:
                deps.add(w)
            deps.update(self.readers.get(k, ()))
        for k in reads:
            self.readers.setdefault(k, []).append(o.idx)
        for k in writes:
            self.last_w[k] = o.idx
            self.readers[k] = []
        if eng == "pe" and dma_sem is None:
            deps = {d for d in deps if not (self.ops[d].eng == "pe" and self.ops[d].dma_sem is None)}
        o.deps = deps
        self.ops.append(o)
        if dma_sem is None:
            self.last_eng[eng] = o.idx
        else:
            self.last_dma[dma_sem] = o.idx
        return o

    def barrier(self):
        self.bar = set(self.last_eng.values()) | set(self.last_dma.values())

    def emit(self, nc):
        ops = self.ops
        for o in ops:
            for d in o.deps:
                ops[d].sig = True
        cnt = {e: 0 for e in self.ENGS}
        dma_names, dma_cnt, dma_eng = [], {}, {}
        for o in ops:
            if o.dma_sem is not None:
                if o.dma_sem not in dma_cnt:
                    dma_cnt[o.dma_sem] = 0
                    dma_names.append(o.dma_sem)
                    dma_eng[o.dma_sem] = o.eng
                assert dma_eng[o.dma_sem] == o.eng
                dma_cnt[o.dma_sem] += 16
                q = 16 * o.burst
                o.val = -(-dma_cnt[o.dma_sem] // q) * q
            elif o.sig:
                cnt[o.eng] += 1
                o.val = cnt[o.eng]
        with contextlib.ExitStack() as st:
            sems = {}
            for e in self.ENGS:
                sems[("eng", e)] = st.enter_context(nc.semaphore("s_" + e))
            for n in dma_names:
                sems[("dma", n)] = st.enter_context(nc.semaphore("d_" + str(n)))
            block = st.enter_context(nc.Block())

            def semkey(o):
                return ("dma", o.dma_sem) if o.dma_sem is not None else ("eng", o.eng)

            def run(engname, engobj):
                known = {}
                for o in ops:
                    if o.eng != engname:
                        continue
                    need = {}
                    for d in o.deps:
                        p = ops[d]
                        k = semkey(p)
                        if p.val > need.get(k, 0):
                            need[k] = p.val
                    for k, v in need.items():
                        if known.get(k, 0) < v:
                            engobj.wait_ge(sems[k], v)
                            known[k] = v
                    ins = o.fn(engobj)
                    if o.dma_sem is not None:
                        ins.then_inc(sems[("dma", o.dma_sem)], 16)
                    elif o.sig:
                        ins.then_inc(sems[("eng", engname)], 1)
                for n in dma_names:
                    if dma_eng[n] == engname and known.get(("dma", n), 0) < dma_cnt[n]:
                        engobj.wait_ge(sems[("dma", n)], dma_cnt[n])

            @block.tensor
            def _(e):
                run("pe", e)

            @block.scalar
            def _(e):
                run("act", e)

            @block.vector
            def _(e):
                run("dve", e)

            @block.gpsimd
            def _(e):
                run("pool", e)

            @block.sync
            def _(e):
                run("sp", e)


def _const_tables():
    k = np.arange(128)[:, None]
    q = np.arange(128)[None, :]
    m = np.zeros((128, 384), np.float32)
    m[:, 0:128] = (k <= q)
    m[:, 128:256] = (k >= q)
    for i in range(4):
        c = np.arange(32)[None, :]
        blk = np.where(k < 32 * i, 1.0, np.where(k < 32 * (i + 1), ((k - 32 * i) <= c) * 1.0, 0.0))
        m[:, 256 + 32 * i:256 + 32 * (i + 1)] = blk
    invc = np.zeros((128, 4, 16), np.float32)
    for g, w in enumerate(POOL_W):
        invc[:, g, :] = 1.0 / np.minimum(w, np.arange(16) + 1)[None, :]
    sh = np.zeros((64, 3, 64), np.float32)
    vd = np.zeros((64, 3), np.float32)
    for d in range(1, 4):
        for dst in range(64):
            if dst % 4 >= d:
                sh[dst - d, d - 1, dst] = 1.0
                vd[dst, d - 1] = 1.0
    m0 = np.zeros((128, 4), np.float32)
    for t in range(4):
        m0[:, t] = (np.arange(128) >= t)
    z = np.zeros((128, 127), np.float32)
    z[:, 63] = 1.0
    return {"c_ident": np.eye(128, dtype=np.float32), "c_mask": m, "c_invc": invc.reshape(128, 64),
            "c_shift": sh.reshape(64, 192), "c_valid": vd, "c_m0": m0, "c_z": z}


def build_program(NSEQ=4, NSB=16, SEQ=2048, do_sample=True, STAGE=99):
    T = 512
    NT = SEQ // T
    NTOKS = NSB * 4
    nc = bass.Bass("TRN2", target_bir_lowering=False)
    S = Sched()

    def din(name, shape):
        return nc.dram_tensor(name, list(shape), F32, kind="ExternalInput").ap()

    def dout(name, shape):
        return nc.dram_tensor(name, list(shape), F32, kind="ExternalOutput").ap()

    xp = din("xp", [NSEQ, SEQ, 1024])
    xs = din("xs", [NTOKS, 1024])
    cache = [din("c128", [NSB, 128, 1024]), din("c512", [NSB, 512, 1024]), din("c2048", [NSB, 2048, 1024])]
    spool = din("spool", [NSB, 15, 512])
    Wd = {}
    for nm, shp in (("ffn1_norm", [1024]), ("ffn1_w_gu", [1024, 5632]), ("ffn1_w_down", [2816, 1024]),
                    ("mix_norm", [1024]), ("w_in", [1024, 7168]), ("q_norm", [1536]), ("k_norm", [1536]),
                    ("pool_w", [4, 128, 128]), ("pool_scale", [512]), ("w_branch_pool", [512, 1024]),
                    ("w_branch_att", [512, 1024]), ("w_out", [1024, 1024]), ("ffn2_norm", [1024]),
                    ("ffn2_w_gu", [1024, 5632]), ("ffn2_w_down", [2816, 1024])):
        Wd[nm] = din(nm, shp)
    c_ident = din("c_ident", [128, 128])
    c_mask = din("c_mask", [128, 384])
    c_invc = din("c_invc", [128, 64])
    c_shift = din("c_shift", [64, 192])
    c_valid = din("c_valid", [64, 3])
    c_m0 = din("c_m0", [128, 4])
    c_z = din("c_z", [128, 127])

    yp = dout("yp", [NSEQ, SEQ, 1024])
    ys = dout("ys", [NTOKS, 1024])
    kvp = [dout("kv128p", [NSEQ, 128, 1024]), dout("kv512p", [NSEQ, 512, 1024]), dout("kv2048p", [NSEQ, SEQ, 1024])]
    poolp = dout("poolp", [NSEQ, 15, 512])
    kvs = [dout("kv128s", [NSB, 128, 1024]), dout("kv512s", [NSB, 512, 1024]), dout("kv2048s", [NSB, 2048, 1024])]
    pools = dout("pools", [NSB, 15, 512])

    slabs = []

    def piece(w, col0, ncols, nk, off):
        return (w[0:nk * 128, col0:col0 + ncols].rearrange("(kc p) c -> p kc c", p=128), off, nk, ncols)

    def add_slab(n, pieces):
        slabs.append((n, pieces))
        return len(slabs) - 1

    def ffn_slabs(wgu, wdn):
        gu = []
        for s in range(11):
            gu.append(add_slab(4096, [("ab", wgu, s)]))
        dn = [add_slab(5632, [piece(wdn, d * 256, 256, 22, 0)]) for d in range(4)]
        return gu, dn

    f1gu, f1dn = ffn_slabs(Wd["ffn1_w_gu"], Wd["ffn1_w_down"])
    sl_u = add_slab(4096, [piece(Wd["w_in"], 0, 512, 8, 0)])
    sl_q = [add_slab(4096, [piece(Wd["w_in"], 512 + g * 512, 512, 8, 0)]) for g in range(3)]
    sl_k = [add_slab(4096, [piece(Wd["w_in"], 2048 + g * 512, 512, 8, 0)]) for g in range(3)]
    sl_v = [add_slab(4096, [piece(Wd["w_in"], 3584 + g * 512, 512, 8, 0)]) for g in range(3)]
    sl_m = [add_slab(3072, [piece(Wd["w_in"], 5120 + m * 128, 128, 8, 0),
                            piece(Wd["w_in"], 6144 + m * 128, 128, 8, 1024),
                            piece(Wd["w_branch_pool"], m * 128, 128, 4, 2048),
                            piece(Wd["w_branch_att"], m * 128, 128, 4, 2560)]) for m in range(8)]
    sl_o = [add_slab(4096, [piece(Wd["w_out"], o * 512, 512, 8, 0)]) for o in range(2)]
    f2gu, f2dn = ffn_slabs(Wd["ffn2_w_gu"], Wd["ffn2_w_down"])
    NSLAB = len(slabs)
    scr = nc.dram_tensor("wscr", [NSLAB, 128, SLOT], BF).ap()

    tile_order = (f1gu + f1dn + [sl_u] + [x for g in range(3) for x in (sl_q[g], sl_k[g], sl_v[g])]
                  + sl_m + sl_o + f2gu + f2dn)
    conv_order = tile_order

    with contextlib.ExitStack() as st:
        E = st.enter_context

        def sb(name, shape, dt=F32):
            return E(nc.sbuf_tensor(name, list(shape), dt))

        x = sb("x", [128, 8, T])
        h = sb("h", [128, 8, T], BF)
        R = sb("R", [128, 22, T], BF)
        wring = [sb("wr%d" % i, [128, SLOT], BF) for i in range(NSLOT)]
        HIST = sb("HIST", [128, 16384], BF)
        VH = sb("VH", [128, 32, 512], BF)
        ub = sb("ub", [128, 4, 528])
        la = sb("la", [128, 528])
        lb = sb("lb", [128, 528])
        pl = sb("pl", [128, 4, T], BF)
        kvst = [sb("kvst%d" % i, [128, 512]) for i in range(6)]
        v2st = [sb("v2st%d" % i, [128, 512], BF) for i in range(2)]
        sq = [sb("sq%d" % i, [128, 512]) for i in range(2)]
        sqb = [sb("sqb%d" % i, [128, 512], BF) for i in range(2)]
        rstd = sb("rstd", [128, 512])
        Pt = [sb("Pt%d" % i, [128, 512], BF) for i in range(4)]
        sa = [sb("sa%d" % i, [128, 512], BF) for i in range(2)]
        gt = [sb("gt%d" % i, [128, 512]) for i in range(3)]
        small = sb("small", [128, 64])
        ident = sb("ident", [128, 128])
        onesb = sb("onesb", [128, 128], BF)
        maskb = sb("maskb", [128, 384], BF)
        gk = sb("gk", [128, 1536])
        gqT = sb("gqT", [128, 12])
        gn = sb("gn", [128, 3, 8])
        psc = sb("psc", [128, 4])
        pwb = sb("pwb", [128, 4, 128], BF)
        invc = sb("invc", [128, 64])
        pst = sb("pst", [16, 512])
        ps = [E(nc.psum_tensor("ps%d" % i, [128, 512], F32)) for i in range(8)]
        import os as _os
        if _os.environ.get("K_VERBOSE"):
            print("SBUF bytes remaining", nc.sbuf_bytes_remaining)

        def Rblk(j0, n):
            return R[:, j0:j0 + n, :]
        QT = lambda c0, n: Rblk(c0, n)
        attT = lambda c: R[:, 12 + c, :]
        py = lambda c: R[:, 16 + c, :]
        mg = lambda c: R[:, c, :]

        Rf = R[:].rearrange("p a b -> p (a b)").bitcast(F32)

        def tokst(i):
            return Rf[:, i * 1024:(i + 1) * 1024]

        def tokst_keys(i):
            return ["R%d" % j for j in range(4 * i, 4 * i + 4)]

        KT = [HIST[:, 0:4096].rearrange("p (c t) -> p c t", c=4),
              HIST[:, 4096:8192].rearrange("p (c t) -> p c t", c=4),
              HIST[:, 8192:16384].rearrange("p (c t) -> p c t", c=4)]

        bank_state = {"i": 0, "set": list(range(8))}

        def nb():
            s_ = bank_state["set"]
            b = s_[bank_state["i"] % len(s_)]
            bank_state["i"] += 1
            return b

        def MM(out, lhsT, rhs, start, stop, reads, writes, **kw):
            S.op("pe", lambda e: e.matmul(out, lhsT=lhsT, rhs=rhs, start=start, stop=stop, **kw), reads, writes)

        def TR(out, in_, idn, reads, writes):
            S.op("pe", lambda e: e.transpose(out, in_, idn), reads, writes)

        def ACT(out, in_, func, reads, writes, **kw):
            S.op("act", lambda e: e.activation(out=out, in_=in_, func=func, **kw), reads, writes)

        def TT(out, in0, in1, op, reads, writes, eng="dve"):
            S.op(eng, lambda e: e.tensor_tensor(out=out, in0=in0, in1=in1, op=op), reads, writes)

        def TS(out, in0, s1, s2, op0, op1, reads, writes, eng="dve"):
            if op1 is None:
                S.op(eng, lambda e: e.tensor_scalar(out=out, in0=in0, scalar1=s1, scalar2=None, op0=op0), reads, writes)
            else:
                S.op(eng, lambda e: e.tensor_scalar(out=out, in0=in0, scalar1=s1, scalar2=s2, op0=op0, op1=op1), reads, writes)

        def STT(out, in0, scalar, in1, op0, op1, reads, writes):
            S.op("dve", lambda e: e.scalar_tensor_tensor(out=out, in0=in0, scalar=scalar, in1=in1, op0=op0, op1=op1),
                 reads, writes)

        def CP(out, in_, reads, writes, eng="dve"):
            if eng == "act":
                S.op("act", lambda e: e.activation(out=out, in_=in_, func=AF.Copy), reads, writes)
            else:
                S.op(eng, lambda e: e.tensor_copy(out=out, in_=in_), reads, writes)

        def RCP(out, in_, reads, writes):
            S.op("dve", lambda e: e.reciprocal(out=out, in_=in_), reads, writes)

        def RED(out, in_, reads, writes):
            S.op("dve", lambda e: e.tensor_reduce(out=out, in_=in_, axis=AX.X, op=OP.add), reads, writes)

        def MSET(ap, v, writes, eng="dve"):
            S.op(eng, lambda e: e.memset(ap, v), (), writes)

        def DMA(out, in_, reads, writes, sem, eng="sp", burst=1, **kw):
            S.op(eng, lambda e: e.dma_start(out=out, in_=in_, **kw), reads, writes, dma_sem=sem, burst=burst)

        maskf = gt[0][:, 0:384]
        pwf = gt[1][:, 0:512].rearrange("p (g d) -> p g d", g=4)
        DMA(ident[:], c_ident, [], ["ident"], "c0")
        DMA(maskf, c_mask, [], ["gt0"], "c1")
        DMA(invc[:], c_invc, [], ["invc"], "c2")
        DMA(gk[:], bass.AP(Wd["k_norm"].tensor, 0, [[0, 128], [1, 1536]]), [], ["gk"], "c3")
        DMA(pwf, Wd["pool_w"].rearrange("g c d -> c g d"), [], ["gt1"], "c4")
        for j, nm in enumerate(("ffn1_norm", "mix_norm", "ffn2_norm")):
            DMA(gn[:, j, :], Wd[nm].rearrange("(c p) -> p c", p=128), [], ["gn"], "c5", allow_slow_non_contiguous=True)
        DMA(gqT[:], Wd["q_norm"].rearrange("(c p) -> p c", p=128), [], ["gqT"], "c6", allow_slow_non_contiguous=True)
        DMA(psc[:], Wd["pool_scale"].rearrange("(g p) -> p g", p=128), [], ["psc"], "c7", allow_slow_non_contiguous=True)
        CP(maskb[:], maskf, ["gt0"], ["maskb"])
        CP(pwb[:], pwf, ["gt1"], ["pwb"])
        MSET(onesb[:], 1.0, ["onesb"])

        grp_sizes = [1] * 11 + [2, 2] + [2] * 5 + [4, 4] + [2] + [3, 3, 3, 2] + [4]
        assert sum(grp_sizes) == len(conv_order)
        pos = 0
        for gi, gs in enumerate(grp_sizes):
            members = conv_order[pos:pos + gs]
            pos += gs
            ndma = sum((2 if pc[0] == "ab" else 1) for sl in members for pc in slabs[sl][1])
            for sl in members:
                n, pieces = slabs[sl]
                for pc in pieces:
                    if pc[0] == "ab":
                        _, wgu, s_ = pc
                        dstv = scr[sl, :, 0:4096].rearrange("p (kc ab c) -> p kc ab c", kc=8, ab=2)
                        for ab in range(2):
                            src = wgu[:, ab * 2816 + s_ * 256: ab * 2816 + s_ * 256 + 256].rearrange("(kc p) c -> p kc c", p=128)
                            DMA(dstv[:, :, ab, :], src, [], ["scr%d_%d" % (sl, ab)], "cv%d" % gi, eng="pool", burst=ndma)
                    else:
                        src, off, nk, ncols = pc
                        dstv = scr[sl, :, off:off + nk * ncols].rearrange("p (kc c) -> p kc c", c=ncols)
                        DMA(dstv, src, [], ["scr%d_%d" % (sl, off)], "cv%d" % gi, eng="pool", burst=ndma)

        bulk = []
        if do_sample:
            for g, Wn in enumerate((128, 512, 2048)):
                for b in range(NSB):
                    bulk.append((kvs[g][b, 0:Wn - 4, :], cache[g][b, 4:Wn, :]))
            bulk.append((pools[:, 0:11, :], spool[:, 4:15, :]))

        def issue_bulk(k):
            for _ in range(k):
                if bulk:
                    o_, i_ = bulk.pop()
                    DMA(o_, i_, ["tilestart"], [], "bulk", eng="pool")

        scr_keys = {}
        for sl, (n_, pieces_) in enumerate(slabs):
            scr_keys[sl] = [("scr%d_%d" % (sl, ab)) for pc in pieces_ if pc[0] == "ab" for ab in range(2)] + \
                           [("scr%d_%d" % (sl, pc[1])) for pc in pieces_ if pc[0] != "ab"]

        n_tiles_total = NSEQ * NT + (1 if do_sample else 0)
        stream = tile_order * n_tiles_total
        wst = {"issued": 0, "consumed": 0}

        def prefetch(upto):
            while wst["issued"] <= upto and wst["issued"] < len(stream):
                k = wst["issued"]
                sl = stream[k]
                slot = k % NSLOT
                n = slabs[sl][0]
                DMA(wring[slot][:, 0:n], scr[sl, :, 0:n], scr_keys[sl], ["w%d" % slot], "wl%d" % slot)
                wst["issued"] += 1

        def take(expect):
            k = wst["consumed"]
            assert stream[k] == expect, (k, stream[k], expect)
            prefetch(k + NSLOT - 1)
            wst["consumed"] += 1
            slot = k % NSLOT
            return wring[slot], "w%d" % slot

        def rmsnorm_to_h(Tn, j):
            bn = nb()
            for c in range(8):
                s_ = sqb[c % 2]
                ACT(s_[:, 0:Tn], x[:, c, 0:Tn], AF.Square, ["x%d" % c], ["sqb%d" % (c % 2)])
                MM(ps[bn][:, 0:Tn], onesb[:], s_[:, 0:Tn], c == 0, c == 7, ["onesb", "sqb%d" % (c % 2)], ["ps%d" % bn])
            ACT(rstd[:, 0:Tn], ps[bn][:, 0:Tn], AF.Sqrt, ["ps%d" % bn], ["rstd"], bias=EPS, scale=1.0 / 1024)
            RCP(rstd[:, 0:Tn], rstd[:, 0:Tn], ["rstd"], ["rstd"])
            for c in range(8):
                STT(h[:, c, 0:Tn], x[:, c, 0:Tn], gn[:, j, c:c + 1], rstd[:, 0:Tn], OP.mult, OP.mult,
                    ["x%d" % c, "gn", "rstd"], ["h%d" % c])

        hkeys = ["h%d" % c for c in range(8)]

        def ffn(Tn, j, gus, dns):
            rmsnorm_to_h(Tn, j)
            for s_, sl in enumerate(gus):
                wt, wk = take(sl)
                Wv = wt[:, 0:4096].rearrange("p (kc ab c) -> p kc ab c", kc=8, ab=2)
                for jj in range(2):
                    hj = 2 * s_ + jj
                    ba, bb = nb(), nb()
                    for kc in range(8):
                        MM(ps[ba][:, 0:Tn], Wv[:, kc, 0, jj * 128:(jj + 1) * 128], h[:, kc, 0:Tn], kc == 0, kc == 7,
                           [wk, "h%d" % kc], ["ps%d" % ba])
                    for kc in range(8):
                        MM(ps[bb][:, 0:Tn], Wv[:, kc, 1, jj * 128:(jj + 1) * 128], h[:, kc, 0:Tn], kc == 0, kc == 7,
                           [wk, "h%d" % kc], ["ps%d" % bb])
                    sa_ = sa[hj % 2]
                    ACT(sa_[:, 0:Tn], ps[ba][:, 0:Tn], AF.Silu, ["ps%d" % ba], ["sa%d" % (hj % 2)])
                    TT(R[:, hj, 0:Tn], sa_[:, 0:Tn], ps[bb][:, 0:Tn], OP.mult, ["sa%d" % (hj % 2), "ps%d" % bb], ["R%d" % hj])
            for d, sl in enumerate(dns):
                wt, wk = take(sl)
                Wv = wt[:, 0:5632].rearrange("p (kc c) -> p kc c", c=256)
                for mm in range(2):
                    m = 2 * d + mm
                    bo = nb()
                    for kc in range(22):
                        MM(ps[bo][:, 0:Tn], Wv[:, kc, mm * 128:(mm + 1) * 128], R[:, kc, 0:Tn], kc == 0, kc == 21,
                           [wk, "R%d" % kc], ["ps%d" % bo])
                    STT(x[:, m, 0:Tn], ps[bo][:, 0:Tn], 0.5, x[:, m, 0:Tn], OP.mult, OP.add,
                        ["ps%d" % bo, "x%d" % m], ["x%d" % m])

        def load_x(src_rows, Tn, nrows):
            ntb = max(1, Tn // 128)
            for tb in range(ntb):
                DMA(tokst(tb)[0:nrows, :], src_rows(tb), [], tokst_keys(tb) + (["tilestart"] if tb == 0 else []), "xin%d" % tb)
            for tb in range(ntb):
                stg = tokst(tb)
                for half in range(2):
                    b_ = nb()
                    for k in range(4):
                        c = 4 * half + k
                        TR(ps[b_][:, k * 128:k * 128 + nrows], stg[0:nrows, c * 128:(c + 1) * 128], ident[0:nrows, 0:nrows],
                           tokst_keys(tb) + ["ident"], ["ps%d" % b_])
                    CP(x[:, 4 * half:4 * half + 4, tb * 128:tb * 128 + nrows],
                       ps[b_][:].rearrange("p (a b) -> p a b", a=4)[:, :, 0:nrows],
                       ["ps%d" % b_], ["x%d" % c for c in range(4 * half, 4 * half + 4)], eng=("act" if half else "dve"))

        def store_y(dst_rows, Tn, nrows):
            ntb = max(1, Tn // 128)
            for tb in range(ntb):
                si = 2 + tb % 2
                stg = tokst(si)
                for half in range(2):
                    b_ = nb()
                    for k in range(4):
                        c = 4 * half + k
                        TR(ps[b_][0:nrows, k * 128:(k + 1) * 128], x[:, c, tb * 128:tb * 128 + nrows], ident[:],
                           ["x%d" % c, "ident"], ["ps%d" % b_])
                    CP(stg[0:nrows, half * 512:(half + 1) * 512], ps[b_][0:nrows, :], ["ps%d" % b_],
                       tokst_keys(si), eng=("act" if half else "dve"))
                DMA(dst_rows(tb), stg[0:nrows, :], tokst_keys(si), [], "yout%d" % (tb % 2))

        kvst_rr = {"i": 0}

        pend = []

        def flush(keep):
            while len(pend) > keep:
                pend.pop(0)[0]()

        def next_kvst():
            i = kvst_rr["i"] % 6
            kvst_rr["i"] += 1
            while any(k == "kvst%d" % i for (_, k) in pend):
                pend.pop(0)[0]()
            return kvst[i], "kvst%d" % i

        def proj_tok(wt, wk, tok_ap_fn, M):
            b_ = nb()
            Wv = wt[:, 0:4096].rearrange("p (kc c) -> p kc c", c=512)
            for kc in range(8):
                MM(ps[b_][0:M, :], tok_ap_fn(kc), Wv[:, kc, :], kc == 0, kc == 7, [wk, "h%d" % kc], ["ps%d" % b_])
            return b_

        def head_norm(b_, M, gain_ap, gain_key):
            i = b_ % 2
            ACT(sq[i][0:M, :], ps[b_][0:M, :], AF.Square, ["ps%d" % b_], ["sq%d" % i])
            col = 8 * i
            RED(small[0:M, col:col + 8], sq[i][0:M, :].rearrange("p (a b) -> p a b", b=64), ["sq%d" % i], ["small%d" % i])
            ACT(small[0:M, col:col + 8], small[0:M, col:col + 8], AF.Sqrt, ["small%d" % i], ["small%d" % i], bias=EPS, scale=1.0 / 64)
            RCP(small[0:M, col:col + 8], small[0:M, col:col + 8], ["small%d" % i], ["small%d" % i])
            stg, sk = next_kvst()
            TT(stg[0:M, :].rearrange("p (a b) -> p a b", b=64), ps[b_][0:M, :].rearrange("p (a b) -> p a b", b=64),
               small[0:M, col:col + 8].unsqueeze(2).to_broadcast([M, 8, 64]), OP.mult, ["ps%d" % b_, "small%d" % i], [sk])
            if gain_ap is not None:
                TT(stg[0:M, :], stg[0:M, :], gain_ap, OP.mult, [sk, gain_key], [sk])
            return stg, sk

        def transpose_to_feat(stg, sk, M, out_ap3, out_keys, gain3=None, eng="act"):
            b_ = nb()
            for cc in range(4):
                TR(ps[b_][:, cc * 128:cc * 128 + M], stg[0:M, cc * 128:(cc + 1) * 128], ident[0:M, 0:M], [sk, "ident"], ["ps%d" % b_])
            src = ps[b_][:].rearrange("p (a b) -> p a b", a=4)[:, :, 0:M]
            if gain3 is not None:
                TT(out_ap3, src, gain3.unsqueeze(2).to_broadcast([128, 4, M]), OP.mult, ["ps%d" % b_, "gqT"], out_keys)
            else:
                CP(out_ap3, src, ["ps%d" % b_], out_keys, eng=eng)

        def prompt_tile(s, i):
            tok0 = i * T
            bank_state["set"] = list(range(8))
            load_x(lambda tb: xp[s, tok0 + tb * 128: tok0 + (tb + 1) * 128, :], T, 128)
            if STAGE == 1:
                return
            ffn(T, 0, f1gu, f1dn)
            if STAGE == 2:
                store_y(lambda tb: yp[s, tok0 + tb * 128: tok0 + (tb + 1) * 128, :], T, 128)
                return
            rmsnorm_to_h(T, 1)
            wt, wk = take(sl_u)
            Wv = wt[:, 0:4096].rearrange("p (kc c) -> p kc c", c=512)
            if i == 0:
                MSET(ub[:, :, 0:16], 0.0, ["ub%d" % g for g in range(4)])
            for g in range(4):
                b_ = nb()
                for kc in range(8):
                    MM(ps[b_][:], Wv[:, kc, g * 128:(g + 1) * 128], h[:, kc, :], kc == 0, kc == 7, [wk, "h%d" % kc], ["ps%d" % b_])
                CP(ub[:, g, 16:528], ps[b_][:], ["ps%d" % b_], ["ub%d" % g], eng="act")
            for g in range(4):
                ug = ub[:, g, :]
                uk = "ub%d" % g
                TT(la[:, 2:528], ug[:, 2:528], ug[:, 1:527], OP.add, [uk], ["la"])
                cur, ck = la, "la"
                if g >= 1:
                    TT(lb[:, 4:528], la[:, 4:528], la[:, 2:526], OP.add, ["la"], ["lb"])
                    cur, ck = lb, "lb"
                if g >= 2:
                    TT(la[:, 8:528], lb[:, 8:528], lb[:, 4:524], OP.add, ["lb"], ["la"])
                    cur, ck = la, "la"
                if g >= 3:
                    TT(lb[:, 16:528], la[:, 16:528], la[:, 8:520], OP.add, ["la"], ["lb"])
                    cur, ck = lb, "lb"
                w = POOL_W[g]
                STT(pl[:, g, :], cur[:, 16:528], 1.0 / w, ug[:, 16:528], OP.mult, OP.subtract, [ck, uk], ["pl%d" % g])
                if i == 0:
                    TT(cur[:, 16:32], cur[:, 16:32], invc[:, g * 16:(g + 1) * 16], OP.mult, [ck, "invc"], [ck])
                    TT(pl[:, g, 0:16], cur[:, 16:32], ug[:, 16:32], OP.subtract, [ck, uk], ["pl%d" % g])
            if i == NT - 1:
                b_ = nb()
                for g in range(4):
                    TR(ps[b_][0:16, g * 128:(g + 1) * 128], ub[:, g, 512:528], ident[:], ["ub%d" % g, "ident"], ["ps%d" % b_])
                CP(pst[:], ps[b_][0:16, :], ["ps%d" % b_], ["pst"], eng="act")
                DMA(poolp[s, :, :], pst[1:16, :], ["pst"], [], "pout")
            for g in range(4):
                CP(ub[:, g, 0:16], ub[:, g, 512:528], ["ub%d" % g], ["ub%d" % g])
            for g in range(4):
                b_ = nb()
                MM(ps[b_][:], pwb[:, g, :], pl[:, g, :], True, True, ["pwb", "pl%d" % g], ["ps%d" % b_])
                TS(py(g), ps[b_][:], psc[:, g:g + 1], None, OP.mult, None, ["ps%d" % b_, "psc"], ["R%d" % (16 + g)])
            for g in range(3):
                slot_kt = (i % 2) if g < 2 else i
                ktoff = slot_kt * T
                wt, wk = take(sl_q[g])
                for tb in range(4):
                    b_ = proj_tok(wt, wk, lambda kc, tb=tb: h[:, kc, tb * 128:(tb + 1) * 128], 128)
                    stg, sk = head_norm(b_, 128, None, None)
                    pend.append((lambda stg=stg, sk=sk, g=g, tb=tb: transpose_to_feat(
                        stg, sk, 128, R[:, 4 * g:4 * g + 4, tb * 128:(tb + 1) * 128],
                        ["R%d" % c for c in range(4 * g, 4 * g + 4)], gain3=gqT[:, 4 * g:4 * g + 4]), sk))
                    flush(3)
                wt, wk = take(sl_k[g])
                for tb in range(4):
                    b_ = proj_tok(wt, wk, lambda kc, tb=tb: h[:, kc, tb * 128:(tb + 1) * 128], 128)
                    stg, sk = head_norm(b_, 128, gk[:, g * 512:(g + 1) * 512], "gk")
                    Wg = min((128, 512, 2048)[g], SEQ)
                    keep = (tok0 + tb * 128) >= SEQ - Wg
                    if keep:
                        row0 = tok0 + tb * 128 - (SEQ - Wg)
                        DMA(kvp[g][s, row0:row0 + 128, 0:512], stg[:, :], [sk], [], "ko" + sk)
                    pend.append((lambda stg=stg, sk=sk, g=g, tb=tb, ktoff=ktoff, slot_kt=slot_kt: transpose_to_feat(
                        stg, sk, 128, KT[g][:, :, ktoff + tb * 128: ktoff + (tb + 1) * 128],
                        ["KT%d_%d_%d" % (g, slot_kt, tb)], eng=("act" if tb % 2 else "dve")), sk))
                    flush(3)
                wt, wk = take(sl_v[g])
                if g == 0:
                    for w in range(4):
                        b_ = proj_tok(wt, wk, lambda kc, w=w: h[:, kc, w * 128:(w + 1) * 128], 128)
                        slotv = (4 * i + w) % 8
                        CP(VH[:, slotv, :], ps[b_][:], ["ps%d" % b_], ["V0_%d" % slotv])
                        if i == NT - 1 and w == 3:
                            stg, sk = next_kvst()
                            CP(stg[:, :], ps[b_][:], ["ps%d" % b_], [sk], eng="act")
                            DMA(kvp[0][s, 0:128, 512:1024], stg[:, :], [sk], [], "ko" + sk)
                        flush(3)
                elif g == 1:
                    for r4 in range(4):
                        b_ = proj_tok(wt, wk, lambda kc, r4=r4: h[:, kc, r4:T:4], 128)
                        slotv = 8 + 2 * r4 + (i % 2)
                        CP(VH[:, slotv, :], ps[b_][:], ["ps%d" % b_], ["V1_%d" % slotv])
                        if i == NT - 1:
                            stg, sk = next_kvst()
                            CP(stg[:, :], ps[b_][:], ["ps%d" % b_], [sk], eng="act")
                            DMA(kvp[1][s, r4:512:4, 512:1024], stg[:, :], [sk], [], "ko" + sk)
                        flush(3)
                else:
                    po = 32 * i
                    for r4 in range(4):
                        b_ = proj_tok(wt, wk, lambda kc, r4=r4: h[:, kc, r4:T:4], 128)
                        vb, vbk = v2st[r4 % 2], "v2st%d" % (r4 % 2)
                        CP(vb[:, :], ps[b_][:], ["ps%d" % b_], [vbk])
                        stg, sk = next_kvst()
                        CP(stg[:, :], ps[b_][:], ["ps%d" % b_], [sk], eng="act")
                        DMA(kvp[2][s, tok0 + r4: tok0 + T: 4, 512:1024], stg[:, :], [sk], [], "ko" + sk)
                        for j in range(4):
                            r16 = r4 + 4 * j
                            DMA(VH[po:po + 32, 16 + r16, :], vb[j:128:4, :], [vbk], ["V2_%d" % r16], "vsh" + vbk, burst=4)
                        flush(1 if r4 < 1 else 0)
            flush(0)
            steps = []
            for hp in range(4):
                bo, bd = (4, 5) if hp % 2 == 0 else (6, 7)
                pair_steps = []
                for hh in range(2):
                    hd = 2 * hp + hh
                    p0 = 64 * hh
                    for prev in (0, 1):
                        qc = hp
                        w0 = 1 if (prev and i == 0) else 0
                        qk, pvl = [], []
                        for w in range(w0, 4):
                            U = 4 * i + w - prev
                            kslot, ktb = (U // 4) % 2, U % 4
                            qk.append((128, slice(w * 128, (w + 1) * 128),
                                       KT[0][p0:p0 + 64, hp, kslot * T + ktb * 128: kslot * T + (ktb + 1) * 128],
                                       R[p0:p0 + 64, qc, w * 128:(w + 1) * 128], ["KT0_%d_%d" % (kslot, ktb), "R%d" % qc]))
                            pvl.append((VH[:, U % 8, hd * 64:(hd + 1) * 64], "V0_%d" % (U % 8),
                                        slice(w * 128, (w + 1) * 128), slice(w * 128, (w + 1) * 128)))
                        pair_steps.append(dict(qk=qk, kp=128, c0=w0 * 128, c1=512,
                                               mask=(maskb[:, 128:256] if prev else maskb[:, 0:128]), pv=pvl, p0=p0))
                    qc = 4 + hp
                    for prev in (0, 1):
                        if prev and i == 0:
                            continue
                        kslot = (i - prev) % 2
                        qk, pvl = [], []
                        for r4 in range(4):
                            qk.append((128, slice(r4 * 128, (r4 + 1) * 128),
                                       KT[1][p0:p0 + 64, hp, kslot * T + r4: (kslot + 1) * T: 4], R[p0:p0 + 64, qc, r4:T:4],
                                       ["KT1_%d_%d" % (kslot, tb) for tb in range(4)] + ["R%d" % qc]))
                            slotv = 8 + 2 * r4 + kslot
                            pvl.append((VH[:, slotv, hd * 64:(hd + 1) * 64], "V1_%d" % slotv,
                                        slice(r4 * 128, (r4 + 1) * 128), slice(r4, T, 4)))
                        pair_steps.append(dict(qk=qk, kp=128, c0=0, c1=512,
                                               mask=(maskb[:, 128:256] if prev else maskb[:, 0:128]), pv=pvl, p0=p0))
                    qc = 8 + hp
                    nk = 32 * (i + 1)
                    qk, pvl = [], []
                    for r16 in range(16):
                        qk.append((nk, slice(r16 * 32, (r16 + 1) * 32), KT[2][p0:p0 + 64, hp, r16:(i + 1) * T:16],
                                   R[p0:p0 + 64, qc, r16:T:16],
                                   ["KT2_%d_%d" % (ii, tb) for ii in range(i + 1) for tb in range(4)] + ["R%d" % qc]))
                        pvl.append((VH[0:nk, 16 + r16, hd * 64:(hd + 1) * 64], "V2_%d" % r16,
                                    slice(r16 * 32, (r16 + 1) * 32), slice(r16, T, 16)))
                    pair_steps.append(dict(qk=qk, kp=nk, c0=0, c1=512, mask=maskb[0:nk, 256 + 32 * i:256 + 32 * (i + 1)],
                                           pv=pvl, p0=p0))
                for j_, st_ in enumerate(pair_steps):
                    st_.update(bo=bo, bd=bd, hp=hp, first=(j_ == 0), last=(j_ == len(pair_steps) - 1))
                steps.extend(pair_steps)

            def att_front(st_, n):
                bS = n % 4
                pt, pk = Pt[n % 4], "Pt%d" % (n % 4)
                kp = st_["kp"]
                for (kparts, cols, lhsT, rhs, rd) in st_["qk"]:
                    MM(ps[bS][0:kparts, cols], lhsT, rhs, True, True, rd, ["ps%d" % bS])
                c0, c1, maskap = st_["c0"], st_["c1"], st_["mask"]
                mb = maskap.shape[-1]
                ACT(pt[0:kp, c0:c1], ps[bS][0:kp, c0:c1], AF.Exp, ["ps%d" % bS], [pk], scale=0.125)
                TT(pt[0:kp, c0:c1].rearrange("p (a b) -> p a b", b=mb), pt[0:kp, c0:c1].rearrange("p (a b) -> p a b", b=mb),
                   maskap.unsqueeze(1).to_broadcast([kp, (c1 - c0) // mb, mb]), OP.mult, [pk, "maskb"], [pk])

            def att_back(st_, n):
                pt, pk = Pt[n % 4], "Pt%d" % (n % 4)
                kp, bo, bd, p0 = st_["kp"], st_["bo"], st_["bd"], st_["p0"]
                if st_["first"]:
                    MSET(ps[bo][:], 0.0, ["ps%d" % bo])
                    MSET(ps[bd][:], 0.0, ["ps%d" % bd])
                for (vap, vkey, pc, oc) in st_["pv"]:
                    MM(ps[bo][p0:p0 + 64, oc], vap, pt[0:kp, pc], False, False, [vkey, pk, "ps%d" % bo], ["ps%d" % bo],
                       skip_group_check=True, tile_position=(0, p0))
                    MM(ps[bd][p0:p0 + 64, oc], onesb[0:kp, 0:64], pt[0:kp, pc], False, False,
                       ["onesb", pk, "ps%d" % bd], ["ps%d" % bd], skip_group_check=True, tile_position=(0, p0))
                if st_["last"]:
                    RCP(gt[2][:], ps[bd][:], ["ps%d" % bd], ["gt2"])
                    TT(attT(st_["hp"]), ps[bo][:], gt[2][:], OP.mult, ["ps%d" % bo, "gt2"], ["R%d" % (12 + st_["hp"])])

            LOOK = 2
            for n in range(len(steps) + LOOK):
                if n < len(steps):
                    att_front(steps[n], n)
                if n - LOOK >= 0:
                    att_back(steps[n - LOOK], n - LOOK)

            bank_state["set"] = list(range(8))
            if STAGE == 5:
                store_y(lambda tb: yp[s, tok0 + tb * 128: tok0 + (tb + 1) * 128, :], T, 128)
                return
            merge_and_out(T)
            ffn(T, 2, f2gu, f2dn)
            store_y(lambda tb: yp[s, tok0 + tb * 128: tok0 + (tb + 1) * 128, :], T, 128)

        def merge_and_out(Tn):
            for m in range(8):
                wt, wk = take(sl_m[m])
                Wg = wt[:, 0:2048].rearrange("p (s kc c) -> p s kc c", s=2, kc=8)
                Wb = wt[:, 2048:3072].rearrange("p (s kc c) -> p s kc c", s=2, kc=4)
                bgp, bbp, bga, bba = nb(), nb(), nb(), nb()
                for kc in range(8):
                    MM(ps[bgp][:, 0:Tn], Wg[:, 0, kc, :], h[:, kc, 0:Tn], kc == 0, kc == 7, [wk, "h%d" % kc], ["ps%d" % bgp])
                for kc in range(4):
                    MM(ps[bbp][:, 0:Tn], Wb[:, 0, kc, :], R[:, 16 + kc, 0:Tn], kc == 0, kc == 3, [wk, "R%d" % (16 + kc)], ["ps%d" % bbp])
                for kc in range(8):
                    MM(ps[bga][:, 0:Tn], Wg[:, 1, kc, :], h[:, kc, 0:Tn], kc == 0, kc == 7, [wk, "h%d" % kc], ["ps%d" % bga])
                for kc in range(4):
                    MM(ps[bba][:, 0:Tn], Wb[:, 1, kc, :], R[:, 12 + kc, 0:Tn], kc == 0, kc == 3, [wk, "R%d" % (12 + kc)], ["ps%d" % bba])
                ACT(gt[0][:, 0:Tn], ps[bgp][:, 0:Tn], AF.Sigmoid, ["ps%d" % bgp], ["gt0"])
                ACT(gt[1][:, 0:Tn], ps[bga][:, 0:Tn], AF.Sigmoid, ["ps%d" % bga], ["gt1"])
                TT(gt[0][:, 0:Tn], gt[0][:, 0:Tn], ps[bbp][:, 0:Tn], OP.mult, ["gt0", "ps%d" % bbp], ["gt0"])
                TT(gt[1][:, 0:Tn], gt[1][:, 0:Tn], ps[bba][:, 0:Tn], OP.mult, ["gt1", "ps%d" % bba], ["gt1"])
                TT(R[:, m, 0:Tn], gt[0][:, 0:Tn], gt[1][:, 0:Tn], OP.add, ["gt0", "gt1"], ["R%d" % m])
            for o in range(2):
                wt, wk = take(sl_o[o])
                Wv = wt[:, 0:4096].rearrange("p (kc c) -> p kc c", c=512)
                for mm in range(4):
                    m = 4 * o + mm
                    b_ = nb()
                    for kc in range(8):
                        MM(ps[b_][:, 0:Tn], Wv[:, kc, mm * 128:(mm + 1) * 128], R[:, kc, 0:Tn], kc == 0, kc == 7,
                           [wk, "R%d" % kc], ["ps%d" % b_])
                    TT(x[:, m, 0:Tn], ps[b_][:, 0:Tn], x[:, m, 0:Tn], OP.add, ["ps%d" % b_, "x%d" % m], ["x%d" % m])

        def sample_tile():
            Tn = NTOKS
            S.barrier()
            bank_state["set"] = list(range(6))
            Hf = HIST[:].bitcast(F32)
            qn = Hf[:, 0:1536]
            kn = Hf[:, 1536:3072]
            vv = Hf[:, 3072:4608]
            gq = Hf[:, 4608:6144]
            tA = Hf[:, 6144:7680]
            Oacc = Hf[:, 7680:8192]
            VHb = VH[:].rearrange("p a b -> p (a b)")
            Vf = VHb[:, 0:7168].bitcast(F32)
            CKb = [Vf[:, 1024 * j: 1024 * (j + 1)] for j in range(3)]
            Dacc = Vf[:, 3072:3080]
            pself = Vf[:, 3088:3112]
            p8 = Vf[:, 3120:3128]
            sred = Vf[:, 3136:3160]
            zf = Vf[:, 3168:3295]
            sh = Vf[:, 3296:3488]
            vld = Vf[:, 3488:3491]
            m0 = Vf[:, 3492:3496]
            qnb = VHb[:, 7168:8704]
            tB = VHb[:, 8704:9216]
            p8b = VHb[:, 9216:9224]
            zb = VHb[:, 9232:9359]
            identb = VHb[:, 9360:9488]
            ue = ub[:].rearrange("p a b -> p (a b)")[:, 0:1280].rearrange("p (g b t) -> p g b t", g=4, b=16)
            l1 = la[:, 0:320].rearrange("p (b t) -> p b t", t=20)
            l2 = lb[:, 0:320].rearrange("p (b t) -> p b t", t=20)
            ust = gt[2]
            stS = sq[0]
            DMA(gq, bass.AP(Wd["q_norm"].tensor, 0, [[0, 128], [1, 1536]]), [], ["gq"], "s0")
            DMA(zf, c_z, [], ["zf"], "s1")
            DMA(sh[0:64, :], c_shift, [], ["sh"], "s2")
            DMA(vld[0:64, :], c_valid, [], ["vld"], "s3")
            DMA(m0, c_m0, [], ["m0"], "s4")
            CP(zb, zf, ["zf"], ["zb"])
            CP(identb, ident[:], ["ident"], ["identb"])
            load_x(lambda tb: xs[:, :], Tn, Tn)
            ffn(Tn, 0, f1gu, f1dn)
            rmsnorm_to_h(Tn, 1)
            wt, wk = take(sl_u)
            Wv = wt[:, 0:4096].rearrange("p (kc c) -> p kc c", c=512)
            nblk = (NSB + 7) // 8
            for blk in range(nblk):
                nb_ = min(8, NSB - 8 * blk)
                nr = nb_ * 15
                DMA(stS[0:nr, 0:512], spool[8 * blk:8 * blk + nb_, :, :].rearrange("b r c -> (b r) c"), [], ["sq0"], "s5")
                b_ = nb()
                for g in range(4):
                    TR(ps[b_][:, g * 128:g * 128 + nr], stS[0:nr, g * 128:(g + 1) * 128], ident[0:nr, 0:nr], ["sq0", "ident"], ["ps%d" % b_])
                for g in range(4):
                    CP(ue[:, g, 8 * blk:8 * blk + nb_, 1:16], ps[b_][:, g * 128:g * 128 + nr].rearrange("p (b r) -> p b r", r=15),
                       ["ps%d" % b_], ["ub%d" % g], eng=("act" if g % 2 else "dve"))
            for g in range(4):
                b_ = nb()
                for kc in range(8):
                    MM(ps[b_][:, 0:Tn], Wv[:, kc, g * 128:(g + 1) * 128], h[:, kc, 0:Tn], kc == 0, kc == 7, [wk, "h%d" % kc], ["ps%d" % b_])
                CP(ue[:, g, 0:NSB, 16:20], ps[b_][:, 0:Tn].rearrange("p (b t) -> p b t", t=4), ["ps%d" % b_], ["ub%d" % g], eng="act")
            for g in range(4):
                ug = ue[:, g, 0:NSB, :]
                uk = "ub%d" % g
                A, Bf = l1[:, 0:NSB, :], l2[:, 0:NSB, :]
                TT(A[:, :, 2:20], ug[:, :, 2:20], ug[:, :, 1:19], OP.add, [uk], ["la"])
                cur, ck = A, "la"
                if g >= 1:
                    TT(Bf[:, :, 4:20], A[:, :, 4:20], A[:, :, 2:18], OP.add, ["la"], ["lb"])
                    cur, ck = Bf, "lb"
                if g >= 2:
                    TT(A[:, :, 8:20], Bf[:, :, 8:20], Bf[:, :, 4:16], OP.add, ["lb"], ["la"])
                    cur, ck = A, "la"
                if g >= 3:
                    TT(Bf[:, :, 16:20], A[:, :, 16:20], A[:, :, 8:12], OP.add, ["la"], ["lb"])
                    cur, ck = Bf, "lb"
                STT(pl[:, g, 0:Tn].rearrange("p (b t) -> p b t", t=4), cur[:, :, 16:20], 1.0 / POOL_W[g], ug[:, :, 16:20],
                    OP.mult, OP.subtract, [ck, uk], ["pl%d" % g])
            b_ = nb()
            for g in range(4):
                CP(tA[:, g * 64:g * 64 + Tn].rearrange("p (b t) -> p b t", t=4), ue[:, g, 0:NSB, 16:20], ["ub%d" % g], ["tA"])
            for g in range(4):
                TR(ps[b_][0:Tn, g * 128:(g + 1) * 128], tA[:, g * 64:g * 64 + Tn], ident[:], ["tA", "ident"], ["ps%d" % b_])
            CP(ust[0:Tn, :], ps[b_][0:Tn, :], ["ps%d" % b_], ["gt2"], eng="act")
            DMA(pools[:, 11:15, :], ust[0:Tn, :], ["gt2"], [], "s6")
            for g in range(4):
                b_ = nb()
                MM(ps[b_][:, 0:Tn], pwb[:, g, :], pl[:, g, 0:Tn], True, True, ["pwb", "pl%d" % g], ["ps%d" % b_])
                TS(R[:, 16 + g, 0:Tn], ps[b_][:, 0:Tn], psc[:, g:g + 1], None, OP.mult, None, ["ps%d" % b_, "psc"], ["R%d" % (16 + g)])
            for g in range(3):
                Wn = (128, 512, 2048)[g]
                wt, wk = take(sl_q[g])
                b_ = proj_tok(wt, wk, lambda kc: h[:, kc, 0:Tn], Tn)
                stg, sk = head_norm(b_, Tn, gq[0:Tn, g * 512:(g + 1) * 512], "gq")
                CP(qn[0:Tn, g * 512:(g + 1) * 512], stg[0:Tn, :], [sk], ["qn"])
                wt, wk = take(sl_k[g])
                b_ = proj_tok(wt, wk, lambda kc: h[:, kc, 0:Tn], Tn)
                stg, sk = head_norm(b_, Tn, gk[0:Tn, g * 512:(g + 1) * 512], "gk")
                CP(kn[0:Tn, g * 512:(g + 1) * 512], stg[0:Tn, :], [sk], ["kn"])
                DMA(kvs[g][:, Wn - 4:Wn, 0:512], stg[0:Tn, :], [sk], [], "ko" + sk)
                wt, wk = take(sl_v[g])
                b_ = proj_tok(wt, wk, lambda kc: h[:, kc, 0:Tn], Tn)
                CP(vv[0:Tn, g * 512:(g + 1) * 512], ps[b_][0:Tn, :], ["ps%d" % b_], ["vv"], eng="act")
                DMA(kvs[g][:, Wn - 4:Wn, 512:1024], vv[0:Tn, g * 512:(g + 1) * 512], ["vv"], [], "s7")
            CP(qnb[0:Tn, :], qn[0:Tn, :], ["qn"], ["qnb"])
            TT(tA[0:Tn, :], qn[0:Tn, :], kn[0:Tn, :], OP.mult, ["qn", "kn"], ["tA"])
            RED(sred[0:Tn, :], tA[0:Tn, :].rearrange("p (a b) -> p a b", b=64), ["tA"], ["sred"])
            ACT(pself[0:Tn, :], sred[0:Tn, :], AF.Exp, ["sred"], ["pself"], scale=0.125)
            TT(Dacc[0:Tn, :], pself[0:Tn, 0:8], pself[0:Tn, 8:16], OP.add, ["pself"], ["Dacc"])
            TT(Dacc[0:Tn, :], Dacc[0:Tn, :], pself[0:Tn, 16:24], OP.add, ["pself", "Dacc"], ["Dacc"])
            TT(tA[0:Tn, :].rearrange("p (a b) -> p a b", b=64), vv[0:Tn, :].rearrange("p (a b) -> p a b", b=64),
               pself[0:Tn, :].unsqueeze(2).to_broadcast([Tn, 24, 64]), OP.mult, ["vv", "pself", "tA"], ["tA"])
            TT(Oacc[0:Tn, :], tA[0:Tn, 0:512], tA[0:Tn, 512:1024], OP.add, ["tA"], ["Oacc"])
            TT(Oacc[0:Tn, :], Oacc[0:Tn, :], tA[0:Tn, 1024:1536], OP.add, ["tA", "Oacc"], ["Oacc"])
            for d in range(1, 4):
                bk, bv = nb(), nb()
                MM(ps[bk][0:Tn, :], sh[0:Tn, (d - 1) * 64:(d - 1) * 64 + Tn], kn[0:Tn, 0:512], True, True, ["sh", "kn"], ["ps%d" % bk])
                MM(ps[bv][0:Tn, :], sh[0:Tn, (d - 1) * 64:(d - 1) * 64 + Tn], vv[0:Tn, 0:512], True, True, ["sh", "vv"], ["ps%d" % bv])
                TT(tA[0:Tn, 0:512], qn[0:Tn, 0:512], ps[bk][0:Tn, :], OP.mult, ["qn", "ps%d" % bk], ["tA"])
                RED(sred[0:Tn, 0:8], tA[0:Tn, 0:512].rearrange("p (a b) -> p a b", b=64), ["tA"], ["sred"])
                ACT(p8[0:Tn, :], sred[0:Tn, 0:8], AF.Exp, ["sred"], ["p8"], scale=0.125)
                TS(p8[0:Tn, :], p8[0:Tn, :], vld[0:Tn, d - 1:d], None, OP.mult, None, ["p8", "vld"], ["p8"])
                TT(Dacc[0:Tn, :], Dacc[0:Tn, :], p8[0:Tn, :], OP.add, ["Dacc", "p8"], ["Dacc"])
                TT(tA[0:Tn, 0:512].rearrange("p (a b) -> p a b", b=64), ps[bv][0:Tn, :].rearrange("p (a b) -> p a b", b=64),
                   p8[0:Tn, :].unsqueeze(2).to_broadcast([Tn, 8, 64]), OP.mult, ["ps%d" % bv, "p8", "tA"], ["tA"])
                TT(Oacc[0:Tn, :], Oacc[0:Tn, :], tA[0:Tn, 0:512], OP.add, ["tA", "Oacc"], ["Oacc"])
            BO, BD = 6, 7
            MSET(ps[BO][:], 0.0, ["ps6"])
            MSET(ps[BD][:], 0.0, ["ps7"])
            VF2 = VHb[:, 9600:12800].bitcast(F32)
            tA3 = [VF2[:, 512 * j:512 * (j + 1)] for j in range(3)]
            sred3 = [VF2[:, 1536 + 8 * j:1544 + 8 * j] for j in range(3)]
            p83 = [VF2[:, 1568 + 8 * j:1576 + 8 * j] for j in range(3)]
            tB3 = [VHb[:, 12800 + 512 * j:12800 + 512 * (j + 1)] for j in range(3)]
            p8b3 = [VHb[:, 14336 + 8 * j:14344 + 8 * j] for j in range(3)]
            passes = []
            ckr = {"i": 0}
            cur_ck = None
            for b in range(NSB):
                for g in range(3):
                    dil = (1, 4, 16)[g]
                    for t in range(4):
                        load = None
                        if not (g == 0 and t > 0):
                            j = ckr["i"] % 6
                            ckr["i"] += 1
                            if j < 3:
                                cur_ck = (CKb[j], ["CK%d" % j], "ck%d" % j)
                            else:
                                cur_ck = (tokst(j - 3), tokst_keys(j - 3), "ck%d" % j)
                            load = cache[g][b, t:t + 127 * dil + 1:dil, :] if g > 0 else cache[g][b, 0:128, :]
                        passes.append((4 * b + t, g, t, cur_ck, load))

            def s_front(n):
                tok, g, t, (ck_ap, ck_key, ck_sem), load = passes[n]
                if load is not None:
                    DMA(ck_ap, load, [], ck_key, ck_sem)
                bq = n % 6
                MM(ps[bq][:], identb[0:Tn, tok:tok + 1].to_broadcast([Tn, 128]), qnb[0:Tn, g * 512:(g + 1) * 512], True, True,
                   ["identb", "qnb"], ["ps%d" % bq])

            def s_back(n):
                tok, g, t, (ck_ap, ck_key, ck_sem), load = passes[n]
                bq = n % 6
                j = n % 3
                TT(tA3[j], ck_ap[:, 0:512], ps[bq][:], OP.mult, ck_key + ["ps%d" % bq], ["tA3_%d" % j])
                RED(sred3[j], tA3[j].rearrange("p (a b) -> p a b", b=64), ["tA3_%d" % j], ["sred3_%d" % j])
                ACT(p83[j], sred3[j], AF.Exp, ["sred3_%d" % j], ["p83_%d" % j], scale=0.125)
                if g == 0:
                    TS(p83[j], p83[j], m0[:, t:t + 1], None, OP.mult, None, ["p83_%d" % j, "m0"], ["p83_%d" % j])
                CP(p8b3[j], p83[j], ["p83_%d" % j], ["p8b3_%d" % j], eng="act")
                TT(tB3[j].rearrange("p (a b) -> p a b", b=64), ck_ap[:, 512:1024].rearrange("p (a b) -> p a b", b=64),
                   p83[j].unsqueeze(2).to_broadcast([128, 8, 64]), OP.mult, ck_key + ["p83_%d" % j], ["tB3_%d" % j], eng="pool")
                MM(ps[BO][0:Tn, :], zb[:, 63 - tok:63 - tok + Tn], tB3[j], False, False, ["zb", "tB3_%d" % j, "ps6"], ["ps6"],
                   skip_group_check=True)
                MM(ps[BD][0:Tn, 0:8], zb[:, 63 - tok:63 - tok + Tn], p8b3[j], False, False, ["zb", "p8b3_%d" % j, "ps7"], ["ps7"],
                   skip_group_check=True)

            SL = 4
            for n in range(len(passes) + SL):
                if n < len(passes):
                    s_front(n)
                if n - SL >= 0:
                    s_back(n - SL)
            TT(Oacc[0:Tn, :], Oacc[0:Tn, :], ps[BO][0:Tn, :], OP.add, ["Oacc", "ps6"], ["Oacc"])
            TT(Dacc[0:Tn, :], Dacc[0:Tn, :], ps[BD][0:Tn, 0:8], OP.add, ["Dacc", "ps7"], ["Dacc"])
            RCP(Dacc[0:Tn, :], Dacc[0:Tn, :], ["Dacc"], ["Dacc"])
            TT(Oacc[0:Tn, :].rearrange("p (a b) -> p a b", b=64), Oacc[0:Tn, :].rearrange("p (a b) -> p a b", b=64),
               Dacc[0:Tn, :].unsqueeze(2).to_broadcast([Tn, 8, 64]), OP.mult, ["Oacc", "Dacc"], ["Oacc"])
            b_ = nb()
            for cc in range(4):
                TR(ps[b_][:, cc * 128:cc * 128 + Tn], Oacc[0:Tn, cc * 128:(cc + 1) * 128], ident[0:Tn, 0:Tn], ["Oacc", "ident"], ["ps%d" % b_])
            CP(R[:, 12:16, 0:Tn], ps[b_][:].rearrange("p (a b) -> p a b", a=4)[:, :, 0:Tn], ["ps%d" % b_],
               ["R%d" % c for c in range(12, 16)])
            bank_state["set"] = list(range(8))
            merge_and_out(Tn)
            ffn(Tn, 2, f2gu, f2dn)
            store_y(lambda tb: ys[:, :], Tn, Tn)

        n_pt = NSEQ * NT
        per_tile = -(-len(bulk) // max(1, n_pt - 3))
        for s in range(NSEQ):
            for i in range(NT):
                prompt_tile(s, i)
                if s * NT + i >= 2 or n_pt < 4:
                    issue_bulk(per_tile if (s * NT + i) < n_pt - 1 else len(bulk))
        if do_sample:
            issue_bulk(len(bulk))
            sample_tile()
        S.emit(nc)
    return nc


_PROG = {}


def kernel(x_prompt, x_sample, cache_kv_w128, cache_kv_w512, cache_kv_w2048, state_pool,
           ffn1_norm, ffn1_w_gu, ffn1_w_down, mix_norm, w_in, q_norm, k_norm, pool_w,
           pool_scale, w_branch_pool, w_branch_att, w_out, ffn2_norm, ffn2_w_gu, ffn2_w_down):
    f = lambda a: np.ascontiguousarray(np.asarray(a, dtype=np.float32))
    B, SEQ, D = x_prompt.shape
    DB = x_sample.shape[0]
    NSEQ, NSB = B // NCORES, DB // NCORES
    if "nc" not in _PROG:
        _PROG["nc"] = build_program(NSEQ=NSEQ, NSB=NSB, SEQ=SEQ)
    nc = _PROG["nc"]
    shared = {
        "ffn1_norm": f(ffn1_norm[0]), "ffn1_w_gu": f(ffn1_w_gu[0]), "ffn1_w_down": f(ffn1_w_down[0]),
        "mix_norm": f(mix_norm[0]), "w_in": f(w_in[0]), "q_norm": f(q_norm[0]).reshape(1536),
        "k_norm": f(k_norm[0]).reshape(1536), "pool_w": f(pool_w[0]), "pool_scale": f(pool_scale[0]),
        "w_branch_pool": f(w_branch_pool[0]), "w_branch_att": f(w_branch_att[0]), "w_out": f(w_out[0]),
        "ffn2_norm": f(ffn2_norm[0]), "ffn2_w_gu": f(ffn2_w_gu[0]), "ffn2_w_down": f(ffn2_w_down[0]),
    }
    shared.update(_const_tables())
    in_maps = []
    for c in range(NCORES):
        m = dict(shared)
        m["xp"] = f(x_prompt[c * NSEQ:(c + 1) * NSEQ])
        m["xs"] = f(x_sample[c * NSB:(c + 1) * NSB]).reshape(NSB * 4, D)
        m["c128"] = f(cache_kv_w128[0, c * NSB:(c + 1) * NSB]).reshape(NSB, 128, 1024)
        m["c512"] = f(cache_kv_w512[0, c * NSB:(c + 1) * NSB]).reshape(NSB, 512, 1024)
        m["c2048"] = f(cache_kv_w2048[0, c * NSB:(c + 1) * NSB]).reshape(NSB, 2048, 1024)
        m["spool"] = f(state_pool[0, c * NSB:(c + 1) * NSB])
        in_maps.append(m)
    res = run_bass_kernel_spmd(nc, in_maps, core_ids=list(range(NCORES)))
    r = res.results
    cat = lambda k: np.concatenate([np.asarray(r[c][k]) for c in range(NCORES)], axis=0)
    y_prompt = cat("yp")
    y_sample = cat("ys").reshape(DB, 4, D)
    kv128p = cat("kv128p").reshape(1, B, 128, 2, 8, 64)
    kv512p = cat("kv512p").reshape(1, B, 512, 2, 8, 64)
    kv2048p = cat("kv2048p").reshape(1, B, SEQ, 2, 8, 64)
    poolp = cat("poolp").reshape(1, B, 15, 512)
    kv128s = cat("kv128s").reshape(1, DB, 128, 2, 8, 64)
    kv512s = cat("kv512s").reshape(1, DB, 512, 2, 8, 64)
    kv2048s = cat("kv2048s").reshape(1, DB, 2048, 2, 8, 64)
    pools_ = cat("pools").reshape(1, DB, 15, 512)
    return (y_prompt, y_sample, kv128p, kv512p, kv2048p, poolp, kv128s, kv512s, kv2048s, pools_)
```
